# Optimizing a Trainium2 kernel written in Bass

```python
import jax, jax.numpy as jnp
from jax import lax
import numpy as np

D_MODEL = 1024
BATCH = 8
SEQ = 2048
DEPTH = 4
DEC_BATCH = 128
DEC_SEQ = 4
PAST_LEN = 16384
PAGE_SIZE = 128

D_A = D_MODEL
H_A = 8
G_A = D_A // H_A
CHUNK_A = 128
D_B = D_MODEL
H_B = 8
DK = D_B // H_B
DV = D_B // H_B
CONV_W = 4
CHUNK_B = 64
ALPHA_DN = (2 * DEPTH) ** 0.25
BETA_DN = (8 * DEPTH) ** -0.25
LN_EPS = 1e-5
NORM_EPS = 1e-6
SPLITS = [D_A, 2 * D_A, 3 * D_A, 3 * D_A + 3 * D_B, 3 * D_A + 4 * D_B,
          3 * D_A + 4 * D_B + H_B, 3 * D_A + 4 * D_B + 2 * H_B,
          3 * D_A + 4 * D_B + 2 * H_B + D_MODEL]
P_IN = 3 * D_A + 4 * D_B + 2 * H_B + 2 * D_MODEL

kernel_name = "hybrid_gmlp_gdn_deepnorm_adaln_step"


def _layernorm(x, g=None, b=None):
    xf = x.astype(jnp.float32)
    mu = jnp.mean(xf, axis=-1, keepdims=True)
    var = jnp.mean(jnp.square(xf - mu), axis=-1, keepdims=True)
    y = (xf - mu) * lax.rsqrt(var + LN_EPS)
    if g is not None:
        y = y * g.astype(jnp.float32) + b.astype(jnp.float32)
    return y


def _rmsnorm(x, g):
    xf = x.astype(jnp.float32)
    return xf * lax.rsqrt(jnp.mean(jnp.square(xf), -1, keepdims=True) + NORM_EPS) * g.astype(jnp.float32)


def _l2norm(x):
    xf = x.astype(jnp.float32)
    return xf * lax.rsqrt(jnp.sum(jnp.square(xf), -1, keepdims=True) + NORM_EPS)


def _chunk_gmlp(u, v, z, w_s, b_s, lnv_g, lnv_b):
    B, L, _ = v.shape
    vn = _layernorm(v, lnv_g, lnv_b)
    if L <= CHUNK_A:
        lc, n_chunk = L, 1
    else:
        lc, n_chunk = CHUNK_A, -(-L // CHUNK_A)
    lp = lc * n_chunk
    vp = jnp.pad(vn, ((0, 0), (0, lp - L), (0, 0))).reshape(B, n_chunk, lc, H_A, G_A)
    ws = jnp.tril(w_s[:, :lc, :lc].astype(jnp.float32))
    s = jnp.einsum('hts,bnshc->bnthc', ws, vp) + b_s[:, :lc].astype(jnp.float32).T[None, None, :, :, None]
    s = s.reshape(B, lp, D_A)[:, :L]
    y = u.astype(jnp.float32) * s * jax.nn.silu(z.astype(jnp.float32))
    start = ((L - 1) // CHUNK_A) * CHUNK_A
    return y, vn[:, start:]


def _short_conv(x, buf, w):
    L = x.shape[1]
    xp = jnp.concatenate([buf.astype(x.dtype), x], axis=1)
    y = w[0] * xp[:, 0:L]
    for j in range(1, CONV_W):
        y = y + w[j] * xp[:, j:j + L]
    return jax.nn.silu(y), xp[:, -(CONV_W - 1):]


def _gated_delta(q, k, v, g, beta, s0):
    f32 = jnp.float32
    q, k, v, g, beta = (t.astype(f32) for t in (q, k, v, g, beta))
    B, H, L, dk = q.shape
    dv = v.shape[-1]
    C = min(CHUNK_B, L)
    pad = (-L) % C
    if pad:
        p4 = ((0, 0), (0, 0), (0, pad), (0, 0))
        p3 = ((0, 0), (0, 0), (0, pad))
        q, k, v = jnp.pad(q, p4), jnp.pad(k, p4), jnp.pad(v, p4)
        g, beta = jnp.pad(g, p3), jnp.pad(beta, p3)
    N = (L + pad) // C
    q = q * (dk ** -0.5)
    q = q.reshape(B, H, N, C, dk)
    k = k.reshape(B, H, N, C, dk)
    v = v.reshape(B, H, N, C, dv)
    beta = beta.reshape(B, H, N, C)
    gc = jnp.cumsum(g.reshape(B, H, N, C), axis=-1)
    kb = k * beta[..., None]
    vb = v * beta[..., None]
    tril_incl = jnp.tril(jnp.ones((C, C), dtype=bool))
    strict = jnp.tril(jnp.ones((C, C), dtype=bool), -1)
    decay = jnp.exp(jnp.where(tril_incl, gc[..., :, None] - gc[..., None, :], -jnp.inf))
    a_mat = jnp.where(strict, jnp.einsum('bhnid,bhnjd->bhnij', kb, k) * decay, 0.0)
    lhs = a_mat + jnp.eye(C, dtype=f32)
    u = lax.linalg.triangular_solve(lhs, vb, left_side=True, lower=True, unit_diagonal=True)
    w = lax.linalg.triangular_solve(lhs, kb * jnp.exp(gc)[..., None], left_side=True, lower=True, unit_diagonal=True)
    qk = jnp.where(tril_incl, jnp.einsum('bhnid,bhnjd->bhnij', q, k) * decay, 0.0)

    def step(S, xs):
        q_n, k_n, u_n, w_n, qk_n, gc_n = xs
        v_new = u_n - jnp.einsum('bhck,bhkv->bhcv', w_n, S)
        o = (jnp.einsum('bhck,bhkv->bhcv', q_n * jnp.exp(gc_n)[..., None], S)
             + jnp.einsum('bhcs,bhsv->bhcv', qk_n, v_new))
        g_last = gc_n[..., -1:]
        S = (S * jnp.exp(g_last)[..., None]
             + jnp.einsum('bhck,bhcv->bhkv', k_n * jnp.exp(g_last - gc_n)[..., None], v_new))
        return S, o

    xs = tuple(jnp.moveaxis(t, 2, 0) for t in (q, k, u, w, qk, gc))
    s_fin, o = lax.scan(step, s0.astype(f32), xs)
    o = jnp.moveaxis(o, 0, 2).reshape(B, H, N * C, dv)[:, :, :L]
    return o, s_fin


def _layer(x, c, conv_buf, s0, w_ada, b_ada, w_in, w_s, b_s, lnv_g, lnv_b, conv_w,
           a_log, dt_bias, onorm_g, w_pa, w_pb, w_o, ln_g, ln_b):
    B, L, _ = x.shape
    mod = jax.nn.silu(c) @ w_ada + b_ada
    shift, scale, gate = jnp.split(mod, 3, axis=-1)
    h = _layernorm(x) * (1.0 + scale[:, None].astype(jnp.float32)) + shift[:, None].astype(jnp.float32)
    h = h.astype(x.dtype)
    p = h @ w_in
    u_a, v_a, z_a, qkv, z_b, b_raw, a_raw, ga, gb = jnp.split(p, SPLITS, axis=-1)
    y_a, v_rows = _chunk_gmlp(jax.nn.gelu(u_a), jax.nn.gelu(v_a), z_a, w_s, b_s, lnv_g, lnv_b)
    qkv_c, conv_new = _short_conv(qkv, conv_buf, conv_w)
    q, k, v = jnp.split(qkv_c, 3, axis=-1)
    q = _l2norm(q.reshape(B, L, H_B, DK))
    k = _l2norm(k.reshape(B, L, H_B, DK))
    v = v.reshape(B, L, H_B, DV)
    g = -jnp.exp(a_log.astype(jnp.float32)) * jax.nn.softplus(a_raw.astype(jnp.float32) + dt_bias.astype(jnp.float32))
    beta = jax.nn.sigmoid(b_raw.astype(jnp.float32))
    o, s_new = _gated_delta(q.transpose(0, 2, 1, 3), k.transpose(0, 2, 1, 3), v.transpose(0, 2, 1, 3),
                            g.transpose(0, 2, 1), beta.transpose(0, 2, 1), s0)
    o = _rmsnorm(o.transpose(0, 2, 1, 3), onorm_g).reshape(B, L, D_B)
    y_b = o * jax.nn.silu(z_b.astype(jnp.float32))
    m = (jax.nn.sigmoid(ga.astype(jnp.float32)) * (y_a.astype(x.dtype) @ w_pa)
         + jax.nn.sigmoid(gb.astype(jnp.float32)) * (y_b.astype(x.dtype) @ w_pb))
    out = m.astype(x.dtype) @ w_o
    x_new = _layernorm(ALPHA_DN * x.astype(jnp.float32) + gate[:, None].astype(jnp.float32) * out.astype(jnp.float32),
                       ln_g, ln_b).astype(x.dtype)
    return x_new, conv_new, s_new, v_rows


def setup_inputs(seed: int = 0) -> dict:
    key = jax.random.key(seed)
    ks = jax.random.split(key, 24)
    f32 = jnp.float32

    def nrm(k, shape, s):
        return jax.random.normal(k, shape, f32) * s

    return {
        "x_prompt": nrm(ks[0], (BATCH, SEQ, D_MODEL), 1.0),
        "x_sample": nrm(ks[1], (DEC_BATCH, DEC_SEQ, D_MODEL), 1.0),
        "state_conv": nrm(ks[2], (DEPTH, DEC_BATCH, CONV_W - 1, 3 * D_B), 1.0),
        "state_ssm": nrm(ks[3], (DEPTH, DEC_BATCH, H_B, DK, DV), DK ** -0.5),
        "c_prompt": nrm(ks[4], (BATCH, D_MODEL), 1.0),
        "c_sample": nrm(ks[5], (DEC_BATCH, D_MODEL), 1.0),
        "w_ada": nrm(ks[6], (DEPTH, D_MODEL, 3 * D_MODEL), 0.5 * D_MODEL ** -0.5),
        "b_ada": nrm(ks[7], (DEPTH, 3 * D_MODEL), 0.02),
        "w_in": nrm(ks[8], (DEPTH, D_MODEL, P_IN), D_MODEL ** -0.5),
        "w_s": nrm(ks[9], (DEPTH, H_A, CHUNK_A, CHUNK_A), CHUNK_A ** -0.5),
        "b_s": nrm(ks[10], (DEPTH, H_A, CHUNK_A), 0.02),
        "lnv_g": 1.0 + nrm(ks[11], (DEPTH, D_A), 0.02),
        "lnv_b": nrm(ks[12], (DEPTH, D_A), 0.02),
        "conv_w": nrm(ks[13], (DEPTH, CONV_W, 3 * D_B), CONV_W ** -0.5),
        "a_log": jnp.log(jax.random.uniform(ks[14], (DEPTH, H_B), f32, 1.0, 16.0)),
        "dt_bias": nrm(ks[15], (DEPTH, H_B), 0.1),
        "onorm_g": 1.0 + nrm(ks[16], (DEPTH, DV), 0.02),
        "w_pa": nrm(ks[17], (DEPTH, D_A, D_MODEL), BETA_DN * D_A ** -0.5),
        "w_pb": nrm(ks[18], (DEPTH, D_B, D_MODEL), BETA_DN * D_B ** -0.5),
        "w_o": nrm(ks[19], (DEPTH, D_MODEL, D_MODEL), BETA_DN * D_MODEL ** -0.5),
        "ln_g": 1.0 + nrm(ks[20], (DEPTH, D_MODEL), 0.02),
        "ln_b": nrm(ks[21], (DEPTH, D_MODEL), 0.02),
    }


def reference(x_prompt, x_sample, state_conv, state_ssm, c_prompt, c_sample, w_ada, b_ada, w_in,
              w_s, b_s, lnv_g, lnv_b, conv_w, a_log, dt_bias, onorm_g, w_pa, w_pb, w_o, ln_g, ln_b):
    xp, xs = x_prompt, x_sample
    bp = x_prompt.shape[0]
    conv_p, ssm_p, vrow_p, conv_s, ssm_s, vrow_s = [], [], [], [], [], []
    for l in range(DEPTH):
        wl = (w_ada[l], b_ada[l], w_in[l], w_s[l], b_s[l], lnv_g[l], lnv_b[l], conv_w[l],
              a_log[l], dt_bias[l], onorm_g[l], w_pa[l], w_pb[l], w_o[l], ln_g[l], ln_b[l])
        buf0 = jnp.zeros((bp, CONV_W - 1, 3 * D_B), x_prompt.dtype)
        s_zero = jnp.zeros((bp, H_B, DK, DV), jnp.float32)
        xp, cp, sp, vp = _layer(xp, c_prompt, buf0, s_zero, *wl)
        xs, cs, ss, vs = _layer(xs, c_sample, state_conv[l], state_ssm[l], *wl)
        conv_p.append(cp.astype(state_conv.dtype))
        ssm_p.append(sp.astype(state_ssm.dtype))
        vrow_p.append(vp.astype(x_prompt.dtype))
        conv_s.append(cs.astype(state_conv.dtype))
        ssm_s.append(ss.astype(state_ssm.dtype))
        vrow_s.append(vs.astype(x_sample.dtype))
    return (xp, xs, jnp.stack(conv_p), jnp.stack(ssm_p), jnp.stack(vrow_p),
            jnp.stack(conv_s), jnp.stack(ssm_s), jnp.stack(vrow_s))
```

```python
import numpy as np
import concourse.bass as bass
import concourse.mybir as mybir
from concourse.bass_utils import run_bass_kernel_spmd

F32, BF16 = mybir.dt.float32, mybir.dt.bfloat16
AF = mybir.ActivationFunctionType
ALU = mybir.AluOpType

D = 1024
DEPTH = 4
NCORE = 8
SEQ = 2048
NPT = 16
NSEQ = 16
P_IN = 9232
ALPHA = (2 * DEPTH) ** 0.25
LN_EPS = 1e-5
NORM_EPS = 1e-6
GROUPS = [[0, 1, 2, 3, 16], [4, 5, 6, 7], [8, 9, 10, 11], [12, 13, 14, 15]]
MAXTOK = 576


class Buf:
    __slots__ = ("name", "w", "r")

    def __init__(self, name):
        self.name = name
        self.w = None
        self.r = {}


class DmaSem:
    def __init__(self, sem, name):
        self.sem = sem
        self.count = 0
        self.name = name


class Sched:
    import os as _os
    EPOCH = int(_os.environ.get('KEPOCH', '8000'))

    def __init__(self, nc):
        self.nc = nc
        self.eng = {"pe": nc.tensor, "act": nc.scalar, "dve": nc.vector, "pool": nc.gpsimd, "sp": nc.sync}
        self.sem = {k: nc.alloc_semaphore("sem_" + k) for k in self.eng}
        self.epoch = {k: 0 for k in self.eng}
        self.cnt = {k: 0 for k in self.eng}
        self.total = {k: 0 for k in self.eng}
        self.known = {k: {} for k in self.eng}
        self.dsems = []
        self.final_waits = []
        self.nsem = len(self.eng)

    def dsem(self, name):
        d = DmaSem(self.nc.alloc_semaphore("ds_" + name), name)
        d.gen = 0
        self.dsems.append(d)
        self.nsem += 1
        return d

    def _wait(self, eng, ev):
        if ev is None:
            return
        if ev[0] == "e":
            _, src, ep, idx, sem = ev
            if src == eng and eng in ("pe", "sp"):
                return
            key = ("e", src)
        else:
            _, dname, ep, idx, sem = ev
            key = ("d", dname)
        kep, kval = self.known[eng].get(key, (-1, 0))
        if kep > ep or (kep == ep and kval >= idx):
            return
        self.known[eng][key] = (ep, idx)
        self.eng[eng].wait_ge(sem, idx)

    def _deps(self, eng, R, W):
        for b in R:
            self._wait(eng, b.w)
        for b in W:
            self._wait(eng, b.w)
            for ev in b.r.values():
                if ev[0] == "e" and ev[1] == eng and eng == "pe":
                    continue
                self._wait(eng, ev)

    def ops(self, eng, fns, R=(), W=()):
        self._deps(eng, R, W)
        if self.cnt[eng] >= self.EPOCH:
            self.sem[eng] = self.nc.alloc_semaphore(f"sem_{eng}_{self.epoch[eng] + 1}")
            self.epoch[eng] += 1
            self.cnt[eng] = 0
            self.nsem += 1
        ins = None
        for fn in fns:
            ins = fn()
        self.cnt[eng] += 1
        self.total[eng] += 1
        ins.then_inc(self.sem[eng], 1)
        ev = ("e", eng, self.epoch[eng], self.cnt[eng], self.sem[eng])
        for b in W:
            b.w = ev
            b.r = {}
        for b in R:
            b.r[eng] = ev
        return ins

    def op(self, eng, fn, R=(), W=()):
        return self.ops(eng, [fn], R, W)

    def dma(self, q, out, in_, ds, R=(), W=()):
        self._deps(q, R, W)
        if ds.count + 16 > self.EPOCH:
            self.final_waits.append((ds.sem, ds.count))
            ds.gen += 1
            ds.sem = self.nc.alloc_semaphore(f"ds_{ds.name}_{ds.gen}")
            ds.count = 0
            self.nsem += 1
        ins = self.eng[q].dma_start(out=out, in_=in_)
        ds.count += 16
        ins.then_inc(ds.sem, 16)
        ev = ("d", ds.name, ds.gen, ds.count, ds.sem)
        for b in W:
            b.w = ev
            b.r = {}
        for b in R:
            b.r[("d", ds.name)] = ev


class _Stop(Exception):
    pass


def build_nc(depth=DEPTH):
    import os
    KSTOP = float(os.environ.get("KSTOP", "99"))
    KGRP = os.environ.get("KGRP")
    groups = GROUPS if KGRP is None else [GROUPS[int(g_)] for g_ in KGRP.split(',')]

    def chk(n):
        if KSTOP <= n:
            raise _Stop()
    nc = bass.Bass("TRN2", target_bir_lowering=False)
    S = Sched(nc)
    dt_in = lambda name, shape: nc.dram_tensor(name, list(shape), F32, kind="ExternalInput").ap()
    dt_out = lambda name, shape: nc.dram_tensor(name, list(shape), F32, kind="ExternalOutput").ap()

    xin = dt_in("xin", [NPT * 128 + 64, D])
    cT = dt_in("cT", [128, 8, 17])
    sconv = dt_in("sconv", [depth, 128, 24, NSEQ, 3])
    sssm = dt_in("sssm", [depth, NSEQ, 8, 128, 128])
    w_ada = dt_in("w_ada", [depth, D, 3 * D])
    b_adaT = dt_in("b_adaT", [128, depth, 16])
    b_gate = dt_in("b_gate", [depth, D])
    w_in = dt_in("w_in", [depth, D, P_IN])
    w_sT = dt_in("w_sT", [depth, 128, 8, 128])
    w_sTs = dt_in("w_sTs", [depth, 64, 8, 64])
    bsr = dt_in("bsr", [1, depth * 8 * 128])
    bsrs = dt_in("bsrs", [1, depth * 8 * 64])
    lnv_g = dt_in("lnv_g", [depth, D])
    lnv_b = dt_in("lnv_b", [depth, D])
    cw = dt_in("cw", [128, depth, 24, 4])
    alog = dt_in("alog", [1, depth * 8])
    dtb = dt_in("dtb", [1, depth * 8])
    ogc = dt_in("ogc", [128, depth])
    w_pa = dt_in("w_pa", [depth, D, D])
    w_pb = dt_in("w_pb", [depth, D, D])
    w_o = dt_in("w_o", [depth, D, D])
    ln_g = dt_in("ln_g", [depth, D])
    ln_b = dt_in("ln_b", [depth, D])
    consts = dt_in("consts", [128, 12, 128])
    selc = dt_in("selc", [16, 8, 128])
    seqsel = dt_in("seqsel", [64, 16])

    y_p = dt_out("y_p", [NPT * 128, D])
    y_s = dt_out("y_s", [64, D])
    o_convp = dt_out("o_convp", [depth, 3, 3 * D])
    o_ssmp = dt_out("o_ssmp", [depth, 8, 128, 128])
    o_cvp = dt_out("o_cvp", [depth, 128, D])
    o_convs = dt_out("o_convs", [depth, NSEQ, 3, 3 * D])
    o_ssms = dt_out("o_ssms", [depth, NSEQ, 8, 128, 128])
    o_cvs = dt_out("o_cvs", [depth, 64, D])

    sb = lambda name, shape, dt=F32: nc.alloc_sbuf_tensor(name, list(shape), dt)
    x_sb = sb("x_sb", [128, 5, D])
    hT = sb("hT", [128, 8, MAXTOK], BF16)
    bufP = sb("bufP", [128, 5 * D], BF16)
    bufQ = sb("bufQ", [128, 8, MAXTOK], BF16)
    NSLOT = 2
    wslot = [sb(f"wslot{i}", [128, 8, 256]) for i in range(NSLOT)]
    rowA = sb("rowA", [128, D])
    rowB = sb("rowB", [128, D])
    gate_p = sb("gate_p", [128, D])
    gate_s = sb("gate_s", [128, D])
    scr = [sb(f"scr{i}", [128, D]) for i in range(2)]
    xn_bf = sb("xn_bf", [128, D], BF16)
    cst = sb("cst", [128, 12, 128])
    ident_bf = sb("ident_bf", [128, 128], BF16)
    ones_bf = sb("ones_bf", [128, 128], BF16)
    ones_f = sb("ones_f", [128, 128])
    sel_bf = sb("sel_bf", [16, 8, 128], BF16)
    seqsel_sb = sb("seqsel_sb", [64, 16])
    eps_ln = sb("eps_ln", [128, 1])
    eps_nm = sb("eps_nm", [128, 1])
    one_c = sb("one_c", [128, 1])
    scT = sb("scT", [128, 8, 17])
    scT_rep = sb("scT_rep", [128, 8, 128])
    scT_s4 = sb("scT_s4", [128, 8, 64])
    modT_all = sb("modT_all", [128, depth, 16, 17])
    opsT_all = sb("opsT_all", [128, depth, 8, 17])
    badaT = sb("badaT", [128, depth, 16])
    cw_sb = sb("cw_sb", [128, depth, 24, 4])
    ogc_sb = sb("ogc_sb", [128, depth])
    alog_sb = sb("alog_sb", [128, depth * 8])
    dtb_sb = sb("dtb_sb", [128, depth * 8])
    nA_sb = sb("nA_sb", [128, depth * 8])
    wsT_bf = sb("wsT_bf", [128, 8, 128], BF16)
    wsTs_bf = sb("wsTs_bf", [64, 8, 64], BF16)
    bsr_bf = sb("bsr_bf", [1, 8 * 128], BF16)
    bsrs_bf = sb("bsrs_bf", [1, 8 * 64], BF16)
    ones_row = sb("ones_row", [1, 128], BF16)
    scv_bf = sb("scv_bf", [128, 24, NSEQ, 3], BF16)
    halo_all = sb("halo_all", [128, depth, 24, 3], BF16)
    mv = sb("mv", [128, 2])
    st6 = sb("st6", [128, 2, 6])
    rstd = sb("rstd", [128, 1])
    nbias = sb("nbias", [128, 1])
    wba = sb("wba", [128, 8, 16], BF16)
    betaT = sb("betaT", [16, MAXTOK], BF16)
    beta_tok = sb("beta_tok", [128, 5, 8])
    apre = sb("apre", [128, 5, 8])
    g_tok = sb("g_tok", [128, 5, 8])
    gam_tok = sb("gam_tok", [128, 5, 8])
    nbG = sb("nbG", [128, 5, 8])
    egl = sb("egl", [128, 5, 8])
    GC = sb("GC", [128, 5, 8])
    G2 = sb("G2", [64, 16, 8])
    GCs = sb("GCs", [128, 16, 8])
    xpre = sb("xpre", [128, 3 + 512], BF16)
    xps = sb("xps", [128, NSEQ, 7], BF16)
    diag = sb("diag", [128, 12, 128], BF16)
    sqb = sb("sqb", [128, MAXTOK], BF16)
    rn = sb("rn", [128, MAXTOK])
    rn2 = sb("rn2", [128, MAXTOK])
    kT = sb("kT", [128, MAXTOK], BF16)
    kbT = sb("kbT", [128, MAXTOK], BF16)
    qT = sb("qT", [128, MAXTOK], BF16)
    vT = sb("vT", [128, MAXTOK], BF16)
    zbs = sb("zbs", [128, MAXTOK], BF16)
    Vb = sb("Vb", [128, 5, 128], BF16)
    Kd = sb("Kd", [128, 5, 128], BF16)
    S32_all = sb("S32_all", [128, depth, 8, 128])
    S_bf = sb("S_bf", [128, 8, 128], BF16)
    S0f = sb("S0f", [128, 8, 128])
    S0b = sb("S0b", [128, NSEQ, 128], BF16)
    Kdm = sb("Kdm", [64, 128], BF16)
    Gm = sb("Gm", [128, 128])
    eET = sb("eET", [128, 128])
    eE = sb("eE", [128, 128])
    Grow = sb("Grow", [128, 128])
    DTsn = sb("DTsn", [128, 128])
    DTi = sb("DTi", [128, 128])
    Dsn = sb("Dsn", [128, 128])
    qgT = sb("qgT", [128, 128], BF16)
    Mb = [sb(f"Mb{i}", [128, 128], BF16) for i in range(2)]
    Nb = [sb(f"Nb{i}", [128, 128], BF16) for i in range(2)]
    Xb = [sb(f"Xb{i}", [128, 128], BF16) for i in range(2)]
    PTb = sb("PTb", [128, 128], BF16)
    Rb = sb("Rb", [128, 128], BF16)
    Vnb = sb("Vnb", [128, 128], BF16)
    KSTb = sb("KSTb", [128, 64], BF16)
    sqo = sb("sqo", [128, 128], BF16)
    rro = sb("rro", [128, 128])
    t1o = sb("t1o", [128, 128])
    ev32 = [sb(f"ev32_{i}", [128, 512]) for i in range(2)]

    psum = [nc.alloc_psum_tensor(f"ps{i}", [128, 512], F32) for i in range(8)]

    T = {}

    def tok(name):
        if name not in T:
            T[name] = Buf(name)
        return T[name]

    tX = [tok(f"x{i}") for i in range(5)]
    tH = [tok(f"h{i}") for i in range(5)]
    tPt = [tok(f"Pt{i}") for i in range(5)]
    tPj = [tok(f"Pj{i}") for i in range(8)]
    tQ = [tok(f"Q{i}") for i in range(8)]
    tPS = [tok(f"ps{i}") for i in range(8)]
    tW = [tok(f"w{i}") for i in range(NSLOT)]
    dsW = [S.dsem(f"w{i}") for i in range(NSLOT)]
    dsWh = [S.dsem(f"wh{i}") for i in range(NSLOT)]
    state = {"bank": 0, "slot": 0}

    def bank():
        i = state["bank"]
        state["bank"] = (i + 1) % 8
        return psum[i], tPS[i]

    def slot(hw=False):
        i = state["slot"]
        state["slot"] = (i + 1) % NSLOT
        return wslot[i], tW[i], (dsWh[i] if hw else dsW[i])

    ds_c = S.dsem("const")
    ds_c2 = S.dsem("const_sw")
    ds_x = [S.dsem(f"x{i}") for i in range(5)]
    ds_row = [S.dsem("rowA"), S.dsem("rowB")]
    ds_ba = S.dsem("wba")
    _ds_named = {}

    def ds_for(name):
        if name not in _ds_named:
            _ds_named[name] = S.dsem(name)
        return _ds_named[name]

    V, A, PE, PO = nc.vector, nc.scalar, nc.tensor, nc.gpsimd

    def act(out, in_, func, R, W, **kw):
        S.op("act", lambda: A.activation(out=out, in_=in_, func=func, **kw), R, W)

    def tt(out, in0, in1, op, R, W, eng="dve"):
        e = V if eng == "dve" else PO
        S.op(eng, lambda: e.tensor_tensor(out=out, in0=in0, in1=in1, op=op), R, W)

    def ts(out, in0, s1, s2, op0, op1, R, W):
        if op1 is None:
            S.op("dve", lambda: V.tensor_scalar(out=out, in0=in0, scalar1=s1, scalar2=None, op0=op0), R, W)
        else:
            S.op("dve", lambda: V.tensor_scalar(out=out, in0=in0, scalar1=s1, scalar2=s2, op0=op0, op1=op1), R, W)

    def stt(out, in0, scalar, in1, op0, op1, R, W):
        S.op("dve", lambda: V.scalar_tensor_tensor(out=out, in0=in0, scalar=scalar, in1=in1, op0=op0, op1=op1), R, W)

    def cp(out, in_, R, W, eng="dve"):
        if eng == "act":
            S.op("act", lambda: A.copy(out=out, in_=in_), R, W)
        else:
            S.op("dve", lambda: V.tensor_copy(out=out, in_=in_), R, W)

    def mm(out, pairs, R, W, first=True, last=True):
        n = len(pairs)
        fns = []
        for i, (l, r) in enumerate(pairs):
            fns.append(lambda l=l, r=r, i=i: PE.matmul(out, l, r, start=(first and i == 0), stop=(last and i == n - 1),
                                                      skip_group_check=True))
        S.ops("pe", fns, R, W)

    def transpose(out, in_, idn, R, W):
        S.op("pe", lambda: PE.transpose(out, in_, idn), R, W)

    tC = tok("const")
    S.dma("sp", cst[:], consts, ds_c, W=[tC])
    S.dma("pool", ident_bf[:], consts[:, 0, :], ds_c2, W=[tok("c2")])
    S.dma("pool", sel_bf[:], selc, ds_c2, W=[tok("c2")])
    S.dma("sp", seqsel_sb[:], seqsel, ds_c, W=[tC])
    S.dma("sp", scT[:], cT, ds_c, W=[tC])
    S.dma("sp", badaT[:], b_adaT, ds_c, W=[tC])
    S.dma("sp", cw_sb[:], cw, ds_c, W=[tC])
    S.dma("sp", ogc_sb[:], ogc, ds_c, W=[tC])
    S.dma("sp", alog_sb[:], alog.partition_broadcast(128), ds_c, W=[tC])
    S.dma("sp", dtb_sb[:], dtb.partition_broadcast(128), ds_c, W=[tC])
    for e_ in ("pe", "act", "dve", "pool", "sp"):
        S.eng[e_].wait_ge(ds_c.sem, ds_c.count)
        S.eng[e_].wait_ge(ds_c2.sem, ds_c2.count)
    tC.w = None
    MASKS = {
        False: dict(Lincl=cst[:, 1, :], Ustr=cst[:, 2, :], mSTn=cst[:, 3, :], mIT=cst[:, 4, :], mSNn=cst[:, 5, :]),
        True: dict(Lincl=cst[:, 6, :], Ustr=cst[:, 7, :], mSTn=cst[:, 8, :], mIT=cst[:, 9, :], mSNn=cst[:, 10, :]),
    }
    tMisc = tok("misc")
    S.op("dve", lambda: V.memset(ones_bf[:], 1.0), W=[tMisc])
    S.op("dve", lambda: V.memset(ones_f[:], 1.0), W=[tMisc])
    S.op("dve", lambda: V.memset(ones_row[:], 1.0), W=[tMisc])
    S.op("dve", lambda: V.memset(eps_ln[:], LN_EPS), W=[tMisc])
    S.op("dve", lambda: V.memset(eps_nm[:], NORM_EPS), W=[tMisc])
    S.op("dve", lambda: V.memset(one_c[:], 1.0), W=[tMisc])
    S.op("dve", lambda: V.memset(halo_all[:], 0.0), W=[tok("halo")])
    S.op("dve", lambda: V.memset(S32_all[:], 0.0), W=[tok("S32")])
    tSc = tok("scT")
    act(scT[:], scT[:], AF.Silu, [tC], [tSc])
    cp(scT_rep[:], scT[:, :, 0:1].to_broadcast([128, 8, 128]), [tSc], [tok("scTrep")])
    cp(scT_s4[:].rearrange("p k (s t) -> p k s t", t=4), scT[:, :, 1:17].unsqueeze(3).to_broadcast([128, 8, 16, 4]),
       [tSc], [tok("scTs4")])
    act(nA_sb[:], alog_sb[:], AF.Exp, [tC], [tok("nA")])
    ts(nA_sb[:], nA_sb[:], -1.0, None, ALU.mult, None, [tok("nA")], [tok("nA")])

    def rsqrt_small(out, in_, eps_t, scale, R, W, n=128):
        act(out, in_, AF.Ln, R, W, bias=eps_t[:n, :], scale=scale)
        act(out, out, AF.Exp, W, W, scale=-0.5)

    def layer_norm_stats(src, pt, Rt):
        tS = tok("st6")
        S.op("dve", lambda: V.bn_stats(st6[:pt, 0, :], src[:, 0:512]), Rt, [tS])
        S.op("dve", lambda: V.bn_stats(st6[:pt, 1, :], src[:, 512:1024]), Rt + [tS], [tS])
        S.op("dve", lambda: V.bn_aggr(mv[:pt, :], st6[:pt, :, :]), [tS], [tok("mv")])
        rsqrt_small(rstd[:pt, :], mv[:pt, 1:2], eps_ln, 1.0, [tok("mv")], [tok("rstd")], n=pt)
        stt(nbias[:pt, :], mv[:pt, 0:1], -1.0, rstd[:pt, :], ALU.mult, ALU.mult, [tok("mv"), tok("rstd")], [tok("nbias")])

    def load_w(dst3, src_l, c0, ncols, tW_, dsW_, q="pool"):
        S.dma(q, dst3, src_l.rearrange("(k p) c -> p k c", p=128)[:, :, c0:c0 + ncols], dsW_, W=[tW_])

    out_ds_i = [0]

    def out_dma(dst, src, R):
        S.dma("sp", dst, src, ds_for("o_" + R[0].name), R=R)

    for l in range(depth):
        for g in range(8):
            ws, tw, dw = slot(hw=True)
            S.dma("sp", ws[:], w_ada[l].rearrange("(k p) c -> p k c", p=128)[:, :, g * 256:(g + 1) * 256], dw, W=[tw])
            for c2 in range(2):
                ct = g * 2 + c2
                ps, tp = bank()
                mm(ps[:, 0:17], [(ws[:, k, c2 * 128:(c2 + 1) * 128], scT[:, k, :]) for k in range(8)], [tw, tSc], [tp])
                ts(modT_all[:, l, ct, :], ps[:, 0:17], badaT[:, l, ct:ct + 1], None, ALU.add, None, [tp, tC], [tok("modT")])
    ts(opsT_all[:], modT_all[:, :, 8:16, :], 1.0, None, ALU.add, None, [tok("modT")], [tok("opsT")])

    def main_loop():
      chk(0)
      for gi, tiles in enumerate(groups):
          has_s = 16 in tiles
          nt = len(tiles)
          blocks = [(0, 512, list(range(4)))] + ([(512, 64, [4])] if has_s else [])
          tinfo = []
          for li, t in enumerate(tiles):
              tinfo.append((li, t, li * 128, 128 if t < 16 else 64, t == 16))
          for (li, t, c0, pt, smp) in tinfo:
              S.dma("sp", x_sb[:pt, li, :], xin[t * 128:t * 128 + pt, :], ds_x[li], W=[tX[li]])

          for l in range(depth):
              last_layer = (l == depth - 1)
              modT = modT_all[:, l]
              opsT = opsT_all[:, l]
              halo = halo_all[:, l]
              S32 = S32_all[:, l]
              wsT_f = scr[0][:, :].rearrange("p (h t) -> p h t", h=8)
              S.dma("sp", wsT_f, w_sT[l], ds_for("l_wsT"), W=[tok("scr0")])
              tt(wsT_bf[:], wsT_f, MASKS[False]["mIT"].unsqueeze(1).to_broadcast([128, 8, 128]), ALU.mult,
                 [tok("scr0"), tC], [tok("wsT")])
              S.dma("pool", bsr_bf[:], bsr[:, l * 1024:(l + 1) * 1024], ds_for("l_bsr"), W=[tok("bsr")])
              if has_s:
                  wsTs_f = scr[1][:64, 0:512].rearrange("p (h t) -> p h t", h=8)
                  S.dma("sp", wsTs_f, w_sTs[l], ds_for("l_wsTs"), W=[tok("scr1")])
                  tt(wsTs_bf[:], wsTs_f, MASKS[True]["mIT"][:64, :64].unsqueeze(1).to_broadcast([64, 8, 64]), ALU.mult,
                     [tok("scr1"), tC], [tok("wsTs")])
                  S.dma("pool", bsrs_bf[:], bsrs[:, l * 512:(l + 1) * 512], ds_for("l_bsrs"), W=[tok("bsrs")])
                  S.dma("pool", scv_bf[:], sconv[l], ds_for("l_scv"), W=[tok("scv")])
              cp(S_bf[:], S32, [tok("S32")], [tok("Sbf")], eng="act")
              S.dma("sp", rowA[:], b_gate[l:l + 1, :].partition_broadcast(128), ds_row[0], W=[tok("rowA")])
              for g in range(4):
                  ws, tw, dw = slot(hw=True)
                  S.dma("sp", ws[:], w_ada[l].rearrange("(k p) c -> p k c", p=128)[:, :, 2048 + g * 256:2048 + (g + 1) * 256],
                        dw, W=[tw])
                  c0g = g * 256
                  ps, tp = bank()
                  mm(ps[:, 0:256], [(scT_rep[:, k, :], ws[:, k, :]) for k in range(8)], [tw, tok("scTrep")], [tp])
                  tt(gate_p[:, c0g:c0g + 256], ps[:, 0:256], rowA[:, c0g:c0g + 256], ALU.add, [tp, tok("rowA")], [tok("gate_p")])
                  if has_s:
                      ps, tp = bank()
                      mm(ps[:64, 0:256], [(scT_s4[:, k, :], ws[:, k, :]) for k in range(8)], [tw, tok("scTs4")], [tp])
                      tt(gate_s[:64, c0g:c0g + 256], ps[:64, 0:256], rowA[:64, c0g:c0g + 256], ALU.add, [tp, tok("rowA")],
                         [tok("gate_s")])

              chk(1)
              for (li, t, c0, pt, smp) in tinfo:
                  layer_norm_stats(x_sb[:pt, li, :], pt, [tX[li]])
                  act(xn_bf[:pt, :], x_sb[:pt, li, :], AF.Identity, [tX[li], tok("rstd"), tok("nbias")], [tok("xn")],
                      bias=nbias[:pt, :], scale=rstd[:pt, :])
                  ps, tp = bank()
                  psb = ps[:].bitcast(BF16)
                  for k in range(8):
                      transpose(psb[:, k * 128:k * 128 + pt], xn_bf[:pt, k * 128:(k + 1) * 128], ident_bf[:pt, :pt],
                                [tok("xn"), tC], [tp])
                  for k in range(8):
                      if not smp:
                          act(hT[:, k, c0:c0 + pt], psb[:, k * 128:k * 128 + pt], AF.Identity,
                              [tp, tok("opsT"), tok("modT")], [tH[li]], bias=modT[:, k, 0:1], scale=opsT[:, k, 0:1])
                      else:
                          tt(ev32[0][:, 0:64].rearrange("p (s t) -> p s t", t=4),
                             psb[:, k * 128:k * 128 + 64].rearrange("p (s t) -> p s t", t=4),
                             opsT[:, k, 1:17].unsqueeze(2).to_broadcast([128, 16, 4]), ALU.mult,
                             [tp, tok("opsT")], [tok("ev0")])
                          tt(hT[:, k, c0:c0 + 64].rearrange("p (s t) -> p s t", t=4),
                             ev32[0][:, 0:64].rearrange("p (s t) -> p s t", t=4),
                             modT[:, k, 1:17].unsqueeze(2).to_broadcast([128, 16, 4]), ALU.add,
                             [tok("ev0"), tok("modT")], [tH[li]])

              chk(2)
              S.dma("sp", rowA[:], lnv_g[l:l + 1, :].partition_broadcast(128), ds_row[0], W=[tok("rowA")])
              S.dma("sp", rowB[:], lnv_b[l:l + 1, :].partition_broadcast(128), ds_row[1], W=[tok("rowB")])
              wv = []
              for hf in range(2):
                  ws, tw, dw = slot()
                  wsb = ws[:].bitcast(BF16)
                  load_w(wsb, w_in[l], 1024 + hf * 512, 512, tw, dw)
                  wv.append((wsb, tw))
              for (li, t, c0, pt, smp) in tinfo:
                  for hf in range(2):
                      ps, tp = bank()
                      mm(ps[:pt, :], [(hT[:, k, c0:c0 + pt], wv[hf][0][:, k, :]) for k in range(8)],
                         [tH[li], wv[hf][1]], [tp])
                      act(scr[0][:pt, hf * 512:(hf + 1) * 512], ps[:pt, :], AF.Gelu_apprx_tanh, [tp], [tok("scr0")])
                  layer_norm_stats(scr[0][:pt, :], pt, [tok("scr0")])
                  act(scr[1][:pt, :], scr[0][:pt, :], AF.Identity, [tok("scr0"), tok("rstd"), tok("nbias")], [tok("scr1")],
                      bias=nbias[:pt, :], scale=rstd[:pt, :])
                  tt(scr[1][:pt, :], scr[1][:pt, :], rowA[:pt, :], ALU.mult, [tok("scr1"), tok("rowA")], [tok("scr1")])
                  if t >= 15:
                      tt(scr[1][:pt, :], scr[1][:pt, :], rowB[:pt, :], ALU.add, [tok("scr1"), tok("rowB")], [tok("scr1")])
                      out_dma(o_cvp[l] if t == 15 else o_cvs[l], scr[1][:pt, :], [tok("scr1")])
                      cp(bufP[:pt, li * D:(li + 1) * D], scr[1][:pt, :], [tok("scr1")], [tPt[li]] + tPj)
                  else:
                      tt(bufP[:pt, li * D:(li + 1) * D], scr[1][:pt, :], rowB[:pt, :], ALU.add, [tok("scr1"), tok("rowB")],
                         [tPt[li]] + tPj)

              chk(3)
              for j in range(8):
                  ws, tw, dw = slot()
                  wsb = ws[:].bitcast(BF16)
                  load_w(wsb[:, :, 0:128], w_in[l], j * 128, 128, tw, dw)
                  load_w(wsb[:, :, 128:256], w_in[l], 2048 + j * 128, 128, tw, dw)
                  for (b0, bn, lis) in blocks:
                      pu, tpu = bank()
                      mm(pu[:, 0:bn], [(wsb[:, k, 0:128], hT[:, k, b0:b0 + bn]) for k in range(8)],
                         [tw] + [tH[i] for i in lis], [tpu])
                      pz, tpz = bank()
                      mm(pz[:, 0:bn], [(wsb[:, k, 128:256], hT[:, k, b0:b0 + bn]) for k in range(8)],
                         [tw] + [tH[i] for i in lis], [tpz])
                      pS, tpS = bank()
                      for li in lis:
                          (_, t, c0, pt, smp) = tinfo[li]
                          if not smp:
                              wmat = wsT_bf[:, j, :]
                              brow = bsr_bf[0:1, j * 128:(j + 1) * 128]
                              tws = [tok("wsT"), tok("bsr")]
                          else:
                              wmat = wsTs_bf[:, j, :]
                              brow = bsrs_bf[0:1, j * 64:(j + 1) * 64]
                              tws = [tok("wsTs"), tok("bsrs")]
                          mm(pS[:, c0 - b0:c0 - b0 + pt],
                             [(bufP[:pt, li * D + j * 128:li * D + (j + 1) * 128], wmat),
                              (ones_row[0:1, :], brow)], [tPt[li], tC, tMisc] + tws, [tpS])
                      act(ev32[0][:, 0:bn], pu[:, 0:bn], AF.Gelu_apprx_tanh, [tpu], [tok("ev0")])
                      act(ev32[1][:, 0:bn], pz[:, 0:bn], AF.Silu, [tpz], [tok("ev1")])
                      tt(ev32[0][:, 0:bn], ev32[0][:, 0:bn], pS[:, 0:bn], ALU.mult, [tok("ev0"), tpS], [tok("ev0")])
                      tt(bufQ[:, j, b0:b0 + bn], ev32[0][:, 0:bn], ev32[1][:, 0:bn], ALU.mult, [tok("ev0"), tok("ev1")],
                         [tQ[j]])

              chk(4)
              mTv = bufP[:, 0:8 * MAXTOK].rearrange("p (j n) -> p j n", j=8)
              for j in range(8):
                  ws, tw, dw = slot()
                  wsb = ws[:].bitcast(BF16)
                  load_w(wsb[:, :, 0:128], w_pa[l], j * 128, 128, tw, dw)
                  load_w(wsb[:, :, 128:256], w_in[l], 7184 + j * 128, 128, tw, dw)
                  for (b0, bn, lis) in blocks:
                      pa, tpa = bank()
                      mm(pa[:, 0:bn], [(wsb[:, k, 0:128], bufQ[:, k, b0:b0 + bn]) for k in range(8)], [tw] + tQ, [tpa])
                      pg, tpg = bank()
                      mm(pg[:, 0:bn], [(wsb[:, k, 128:256], hT[:, k, b0:b0 + bn]) for k in range(8)],
                         [tw] + [tH[i] for i in lis], [tpg])
                      act(ev32[0][:, 0:bn], pg[:, 0:bn], AF.Sigmoid, [tpg], [tok("ev0")])
                      tt(mTv[:, j, b0:b0 + bn], ev32[0][:, 0:bn], pa[:, 0:bn], ALU.mult, [tok("ev0"), tpa], [tPj[j]] + tPt)

              chk(5)
              S.dma("pool", wba[:], w_in[l].rearrange("(k p) c -> p k c", p=128)[:, :, 7168:7184], ds_ba, W=[tok("wba")])
              for (b0, bn, lis) in blocks:
                  ps, tp = bank()
                  mm(ps[:16, 0:bn], [(wba[:, k, :], hT[:, k, b0:b0 + bn]) for k in range(8)],
                     [tok("wba")] + [tH[i] for i in lis], [tp])
                  act(betaT[:, b0:b0 + bn], ps[:16, 0:bn], AF.Sigmoid, [tp], [tok("betaT")])
              psba, tpba = bank()
              for (li, t, c0, pt, smp) in tinfo:
                  mm(psba[:pt, li * 16:(li + 1) * 16], [(hT[:, k, c0:c0 + pt], wba[:, k, :]) for k in range(8)],
                     [tok("wba"), tH[li]], [tpba])
              for (li, t, c0, pt, smp) in tinfo:
                  act(beta_tok[:pt, li, :], psba[:pt, li * 16:li * 16 + 8], AF.Sigmoid, [tpba], [tok("beta_tok")])
                  tt(apre[:pt, li, :], psba[:pt, li * 16 + 8:li * 16 + 16], dtb_sb[:pt, l * 8:(l + 1) * 8], ALU.add,
                     [tpba, tC], [tok("apre")])
              for (li, t, c0, pt, smp) in tinfo:
                  act(apre[:pt, li, :], apre[:pt, li, :], AF.Exp, [tok("apre")], [tok("apre")])
                  act(apre[:pt, li, :], apre[:pt, li, :], AF.Ln, [tok("apre")], [tok("apre")], bias=one_c[:pt, :], scale=1.0)
                  tt(g_tok[:pt, li, :], apre[:pt, li, :], nA_sb[:pt, l * 8:(l + 1) * 8], ALU.mult, [tok("apre"), tok("nA")],
                     [tok("g_tok")])
                  mk = MASKS[smp]
                  ps, tp = bank()
                  mm(ps[:pt, 0:8], [(mk["Lincl"][:pt, :pt], g_tok[:pt, li, :])], [tok("g_tok"), tC], [tp])
                  mm(ps[:pt, 8:16], [(mk["Ustr"][:pt, :pt], g_tok[:pt, li, :])], [tok("g_tok"), tC], [tp])
                  if not smp:
                      mm(ps[:, 16:24], [(ones_f[:pt, :], g_tok[:pt, li, :])], [tok("g_tok"), tMisc], [tp])
                      act(GC[:, li, :], ps[:, 16:24], AF.Exp, [tp], [tok("GC")])
                  act(gam_tok[:pt, li, :], ps[:pt, 0:8], AF.Exp, [tp], [tok("gam")])
                  act(egl[:pt, li, :], ps[:pt, 8:16], AF.Exp, [tp], [tok("egl")])
                  stt(nbG[:pt, li, :], gam_tok[:pt, li, :], -1.0, beta_tok[:pt, li, :], ALU.mult, ALU.mult,
                      [tok("gam"), tok("beta_tok")], [tok("nbG")])
                  if smp:
                      tt(G2[:], g_tok[:64, li, :].unsqueeze(1).to_broadcast([64, 16, 8]),
                         seqsel_sb[:].unsqueeze(2).to_broadcast([64, 16, 8]), ALU.mult, [tok("g_tok"), tC], [tok("G2")])
                      ps2, tp2 = bank()
                      mm(ps2[:, 0:128], [(ones_f[:64, :], G2[:].rearrange("p s h -> p (s h)"))], [tok("G2"), tMisc], [tp2])
                      act(GCs[:].rearrange("p s h -> p (s h)"), ps2[:, 0:128], AF.Exp, [tp2], [tok("GCs")])

              chk(5.1)
              csq = scr[0][:, 0:MAXTOK]
              csk = scr[1][:, 0:MAXTOK]
              for h in range(8):
                  ws, tw, dw = slot()
                  wsb = ws[:].bitcast(BF16)
                  for ci, cbase in enumerate((3072, 4096, 5120, 6144)):
                      load_w(wsb[:, :, ci * 128:(ci + 1) * 128], w_in[l], cbase + h * 128, 128, tw, dw)
                  if has_s:
                      S.dma("pool", S0b[:], sssm[l, :, h].rearrange("s d v -> d s v"), ds_for("S0b"), W=[tok("S0b")])
                  for ci in range(3):
                      for jt in range(4):
                          ts(diag[:, ci * 4 + jt, :], ident_bf[:], cw_sb[:, l, ci * 8 + h, jt:jt + 1], None, ALU.mult, None,
                             [tC], [tok("diag")])
                  ssps = {}
                  for ci in range(3):
                      ctg = ci * 8 + h
                      dst = (csq, csk, vT)[ci]
                      tdst = (tok("scr0"), tok("scr1"), tok("vT"))[ci]
                      for (b0, bn, lis) in blocks:
                          smpb = (bn == 64)
                          pp, tpp = bank()
                          mm(pp[:, 0:bn], [(wsb[:, k, ci * 128:(ci + 1) * 128], hT[:, k, b0:b0 + bn]) for k in range(8)],
                             [tw] + [tH[i] for i in lis], [tpp])
                          pc, tpc = bank()
                          if not smpb:
                              cp(xpre[:, 0:3], halo[:, ctg, :], [tok("halo")], [tok("xpre")])
                              cp(xpre[:, 3:3 + bn], pp[:, 0:bn], [tpp], [tok("xpre")], eng="act")
                              cp(halo[:, ctg, :], xpre[:, bn:bn + 3], [tok("xpre")], [tok("halo")])
                              mm(pc[:, 0:bn], [(diag[:, ci * 4 + jt, :], xpre[:, jt:jt + bn]) for jt in range(4)],
                                 [tok("diag"), tok("xpre")], [tpc])
                          else:
                              cp(xps[:, :, 0:3], scv_bf[:, ctg, :, :], [tok("scv")], [tok("xps")])
                              cp(xps[:, :, 3:7], pp[:, 0:64].rearrange("p (s t) -> p s t", t=4), [tpp], [tok("xps")],
                                 eng="act")
                              mm(pc[:, 0:64].rearrange("p (s t) -> p s t", t=4),
                                 [(diag[:, ci * 4 + jt, :], xps[:, :, jt:jt + 4]) for jt in range(4)],
                                 [tok("diag"), tok("xps")], [tpc])
                          act(dst[:, b0:b0 + bn], pc[:, 0:bn], AF.Silu, [tpc], [tdst])
                          if ci < 2:
                              act(sqb[:, b0:b0 + bn], dst[:, b0:b0 + bn], AF.Square, [tdst], [tok("sqb")])
                              pss, tpss = bank()
                              mm(pss[:, 0:bn], [(ones_bf[:], sqb[:, b0:b0 + bn])], [tok("sqb"), tMisc], [tpss])
                              rsq_t = tok(f"rn{ci}")
                              rnb = (rn, rn2)[ci]
                              ssps[(ci, b0)] = (pss, tpss)
                              cp(rnb[:, b0:b0 + bn], pss[:, 0:bn], [tpss], [rsq_t])
                  chk(5.3)
                  for (b0, bn, lis) in blocks:
                      pz, tpz = bank()
                      mm(pz[:, 0:bn], [(wsb[:, k, 384:512], hT[:, k, b0:b0 + bn]) for k in range(8)],
                         [tw] + [tH[i] for i in lis], [tpz])
                      act(zbs[:, b0:b0 + bn], pz[:, 0:bn], AF.Silu, [tpz], [tok("zbs")])
                  for ci in range(2):
                      rnb = (rn, rn2)[ci]
                      rsq_t = tok(f"rn{ci}")
                      ntk = 512 + (64 if has_s else 0)
                      rsqrt_small(rnb[:, 0:ntk], rnb[:, 0:ntk], eps_nm, 1.0, [rsq_t], [rsq_t])
                      for (b0, bn, lis) in blocks:
                          if ci == 0:
                              stt(qT[:, b0:b0 + bn], csq[:, b0:b0 + bn], 128.0 ** -0.5, rnb[:, b0:b0 + bn], ALU.mult, ALU.mult,
                                  [tok("scr0"), rsq_t], [tok("qT")])
                          else:
                              tt(kT[:, b0:b0 + bn], csk[:, b0:b0 + bn], rnb[:, b0:b0 + bn], ALU.mult, [tok("scr1"), rsq_t],
                                 [tok("kT")])
                              pb_, tpb_ = bank()
                              mm(pb_[:, 0:bn], [(sel_bf[:, h, :], betaT[:, b0:b0 + bn])], [tok("betaT"), tC], [tpb_])
                              tt(kbT[:, b0:b0 + bn], kT[:, b0:b0 + bn], pb_[:, 0:bn], ALU.mult, [tok("kT"), tpb_],
                                 [tok("kbT")])
                  for (li, t, c0, pt, smp) in tinfo:
                      ps, tp = bank()
                      psb = ps[:].bitcast(BF16)
                      transpose(psb[:pt, 0:128], vT[:, c0:c0 + pt], ident_bf[:], [tok("vT"), tC], [tp])
                      transpose(psb[:pt, 128:256], kT[:, c0:c0 + pt], ident_bf[:], [tok("kT"), tC], [tp])
                      ts(Vb[:pt, li, :], psb[:pt, 0:128], beta_tok[:pt, li, h:h + 1], None, ALU.mult, None,
                         [tp, tok("beta_tok")], [tok("Vb")])
                      ts(Kd[:pt, li, :], psb[:pt, 128:256], egl[:pt, li, h:h + 1], None, ALU.mult, None,
                         [tp, tok("egl")], [tok("Kd")])

                  chk(5.4)
                  for (li, t, c0, pt, smp) in tinfo:
                      mk = MASKS[smp]
                      ts(Gm[:pt, :pt], mk["Lincl"][:pt, :pt], g_tok[:pt, li, h:h + 1], None, ALU.mult, None,
                         [tC, tok("g_tok")], [tok("Gm")])
                      pe_, tpe = bank()
                      mm(pe_[:pt, 0:pt], [(mk["Ustr"][:pt, :pt], Gm[:pt, :pt])], [tC, tok("Gm")], [tpe])
                      mm(pe_[:pt, 128:128 + pt], [(Gm[:pt, :pt], mk["Ustr"][:pt, :pt])], [tC, tok("Gm")], [tpe])
                      mm(pe_[:, 256:256 + pt], [(ones_f[:pt, :], Gm[:pt, :pt])], [tMisc, tok("Gm")], [tpe])
                      act(eET[:pt, :pt], pe_[:pt, 0:pt], AF.Exp, [tpe], [tok("eET")])
                      act(eE[:pt, :pt], pe_[:pt, 128:128 + pt], AF.Exp, [tpe], [tok("eE")])
                      act(Grow[:, :pt], pe_[:, 256:256 + pt], AF.Exp, [tpe], [tok("Grow")])
                      tt(DTsn[:pt, :pt], eET[:pt, :pt], mk["mSTn"][:pt, :pt], ALU.mult, [tok("eET"), tC], [tok("DTsn")])
                      tt(DTi[:pt, :pt], eET[:pt, :pt], mk["mIT"][:pt, :pt], ALU.mult, [tok("eET"), tC], [tok("DTi")])
                      tt(Dsn[:pt, :pt], eE[:pt, :pt], mk["mSNn"][:pt, :pt], ALU.mult, [tok("eE"), tC], [tok("Dsn")])
                      tt(qgT[:, :pt], qT[:, c0:c0 + pt], Grow[:, :pt], ALU.mult, [tok("qT"), tok("Grow")], [tok("qgT")])
                      pa_, tpa_ = bank()
                      mm(pa_[:pt, 0:pt], [(kT[:, c0:c0 + pt], kbT[:, c0:c0 + pt])], [tok("kT"), tok("kbT")], [tpa_])
                      mm(pa_[:pt, 128:128 + pt], [(kbT[:, c0:c0 + pt], kT[:, c0:c0 + pt])], [tok("kT"), tok("kbT")], [tpa_])
                      mm(pa_[:pt, 256:256 + pt], [(kT[:, c0:c0 + pt], qT[:, c0:c0 + pt])], [tok("kT"), tok("qT")], [tpa_])
                      tM = [tok("M0"), tok("M1")]
                      tN = [tok("N0"), tok("N1")]
                      tXx = [tok("X0"), tok("X1")]
                      tt(Mb[0][:pt, :pt], pa_[:pt, 0:pt], DTsn[:pt, :pt], ALU.mult, [tpa_, tok("DTsn")], [tM[0]])
                      tt(Nb[0][:pt, :pt], pa_[:pt, 128:128 + pt], Dsn[:pt, :pt], ALU.mult, [tpa_, tok("Dsn")], [tN[0]])
                      tt(PTb[:pt, :pt], pa_[:pt, 256:256 + pt], DTi[:pt, :pt], ALU.mult, [tpa_, tok("DTi")], [tok("PT")])
                      tt(Xb[0][:pt, :pt], Mb[0][:pt, :pt], ident_bf[:pt, :pt], ALU.add, [tM[0], tC], [tXx[0]])
                      chk(5.5)
                      cur = 0
                      nlev = 1 if smp else 6
                      for lv in range(nlev):
                          lastlv = (lv == nlev - 1)
                          nx = 1 - cur
                          pn, tpn = bank()
                          mm(pn[:pt, 0:pt], [(Mb[cur][:pt, :pt], Nb[cur][:pt, :pt])], [tM[cur], tN[cur]], [tpn])
                          if not lastlv:
                              mm(pn[:pt, 128:128 + pt], [(Nb[cur][:pt, :pt], Mb[cur][:pt, :pt])], [tM[cur], tN[cur]], [tpn])
                          cp(Nb[nx][:pt, :pt], pn[:pt, 0:pt], [tpn], [tN[nx]], eng="act")
                          if not lastlv:
                              cp(Mb[nx][:pt, :pt], pn[:pt, 128:128 + pt], [tpn], [tM[nx]], eng="act")
                          px, tpx = bank()
                          mm(px[:pt, 0:pt], [(ident_bf[:pt, :pt], Xb[cur][:pt, :pt]), (Nb[nx][:pt, :pt], Xb[cur][:pt, :pt])],
                             [tC, tXx[cur], tN[nx]], [tpx])
                          cp(Xb[nx][:pt, :pt], px[:pt, 0:pt], [tpx], [tXx[nx]], eng="act")
                          cur = nx
                      Xf, tXf = Xb[cur], tXx[cur]
                      chk(5.6)
                      pk, tpk = bank()
                      if not smp:
                          mm(pk[:pt, 0:128], [(kT[:, c0:c0 + pt], S_bf[:, h, :])], [tok("kT"), tok("Sbf")], [tpk])
                          stt(Rb[:pt, :], pk[:pt, 0:128], nbG[:pt, li, h:h + 1], Vb[:pt, li, :], ALU.mult, ALU.add,
                              [tpk, tok("nbG"), tok("Vb")], [tok("R")])
                      else:
                          for s in range(NSEQ):
                              mm(pk[:, 4 * s:4 * s + 4], [(S0b[:, s, :], kT[:, c0 + 4 * s:c0 + 4 * s + 4])],
                                 [tok("kT"), tok("S0b")], [tpk])
                          cp(KSTb[:, :], pk[:, 0:64], [tpk], [tok("KST")], eng="act")
                          pk2, tpk2 = bank()
                          pk2b = pk2[:].bitcast(BF16)
                          transpose(pk2b[:64, 0:128], KSTb[:, :], ident_bf[:], [tok("KST"), tC], [tpk2])
                          stt(Rb[:pt, :], pk2b[:64, 0:128], nbG[:pt, li, h:h + 1], Vb[:pt, li, :], ALU.mult, ALU.add,
                              [tpk2, tok("nbG"), tok("Vb")], [tok("R")])
                      pv, tpv = bank()
                      mm(pv[:pt, 0:128], [(Xf[:pt, :pt], Rb[:pt, :])], [tXf, tok("R")], [tpv])
                      cp(Vnb[:pt, :], pv[:pt, 0:128], [tpv], [tok("Vn")], eng="act")
                      po, tpo = bank()
                      if not smp:
                          mm(po[:, 0:pt], [(S_bf[:, h, :], qgT[:, :pt]), (Vnb[:pt, :], PTb[:pt, :pt])],
                             [tok("Sbf"), tok("qgT"), tok("Vn"), tok("PT")], [tpo])
                          pd, tpd = bank()
                          mm(pd[:, 0:128], [(Kd[:pt, li, :], Vnb[:pt, :])], [tok("Kd"), tok("Vn")], [tpd])
                          stt(S32[:, h, :], S32[:, h, :], GC[:, li, h:h + 1], pd[:, 0:128], ALU.mult, ALU.add,
                              [tok("S32"), tok("GC"), tpd], [tok("S32")])
                          cp(S_bf[:, h, :], S32[:, h, :], [tok("S32")], [tok("Sbf")], eng="act")
                          if t == 15:
                              out_dma(o_ssmp[l, h], S32[:, h, :], [tok("S32")])
                      else:
                          mm(po[:, 0:64], [(Vnb[:64, :], PTb[:64, :64])], [tok("Vn"), tok("PT")], [tpo], first=True, last=False)
                          for s in range(NSEQ):
                              mm(po[:, 4 * s:4 * s + 4], [(S0b[:, s, :], qgT[:, 4 * s:4 * s + 4])], [tok("S0b"), tok("qgT")],
                                 [tpo], first=False, last=(s == NSEQ - 1))
                          for sh in range(2):
                              S.dma("sp", S0f[:], sssm[l, 8 * sh:8 * sh + 8, h].rearrange("s d v -> d s v"), ds_for("S0f"),
                                    W=[tok("S0f")])
                              for s4 in range(2):
                                  pd, tpd = bank()
                                  for sq_ in range(4):
                                      s = 8 * sh + 4 * s4 + sq_
                                      ts(Kdm[:, :], Kd[:64, li, :], seqsel_sb[:, s:s + 1], None, ALU.mult, None,
                                         [tok("Kd"), tC], [tok("Kdm")])
                                      mm(pd[:, sq_ * 128:(sq_ + 1) * 128], [(Kdm[:, :], Vnb[:64, :])], [tok("Kdm"), tok("Vn")],
                                         [tpd])
                                  for sq_ in range(4):
                                      s = 8 * sh + 4 * s4 + sq_
                                      sl = 4 * s4 + sq_
                                      stt(S0f[:, sl, :], S0f[:, sl, :], GCs[:, s, h:h + 1], pd[:, sq_ * 128:(sq_ + 1) * 128],
                                          ALU.mult, ALU.add, [tok("S0f"), tok("GCs"), tpd], [tok("S0f")])
                              out_dma(o_ssms[l, 8 * sh:8 * sh + 8, h].rearrange("s d v -> d s v"), S0f[:], [tok("S0f")])
                      act(sqo[:, :pt], po[:, 0:pt], AF.Square, [tpo], [tok("sqo")])
                      pr, tpr = bank()
                      mm(pr[:, 0:pt], [(ones_bf[:], sqo[:, :pt])], [tok("sqo"), tMisc], [tpr])
                      rsqrt_small(rro[:, :pt], pr[:, 0:pt], eps_nm, 1.0 / 128.0, [tpr], [tok("rro")])
                      tt(t1o[:, :pt], po[:, 0:pt], rro[:, :pt], ALU.mult, [tpo, tok("rro")], [tok("t1o")])
                      stt(bufQ[:, h, c0:c0 + pt], t1o[:, :pt], ogc_sb[:, l:l + 1], zbs[:, c0:c0 + pt], ALU.mult, ALU.mult,
                          [tok("t1o"), tC, tok("zbs")], [tQ[h]])

              chk(6)
              if has_s or (15 in tiles):
                  for grp6 in range(6):
                      ws, tw, dw = slot()
                      wsb = ws[:].bitcast(BF16)
                      load_w(wsb, w_in[l], 3072 + grp6 * 512, 512, tw, dw)
                      if has_s:
                          ps, tp = bank()
                          mm(ps[:64, :], [(hT[:, k, 512:576], wsb[:, k, :]) for k in range(8)], [tw, tH[4]], [tp])
                          cp(ev32[0][:64, :], ps[:64, :], [tp], [tok("ev0")], eng="act")
                          for tq in range(1, 4):
                              out_dma(o_convs[l, :, tq - 1, grp6 * 512:(grp6 + 1) * 512], ev32[0][tq:64:4, :], [tok("ev0")])
                      if 15 in tiles:
                          l15 = tiles.index(15)
                          ps2, tp2 = bank()
                          mm(ps2[:, :], [(hT[:, k, l15 * 128:(l15 + 1) * 128], wsb[:, k, :]) for k in range(8)], [tw, tH[l15]], [tp2])
                          cp(ev32[1][64:128, :], ps2[64:128, :], [tp2], [tok("ev1")], eng="act")
                          out_dma(o_convp[l, :, grp6 * 512:(grp6 + 1) * 512], ev32[1][125:128, :], [tok("ev1")])

              chk(7)
              for j in range(8):
                  ws, tw, dw = slot()
                  wsb = ws[:].bitcast(BF16)
                  load_w(wsb[:, :, 0:128], w_pb[l], j * 128, 128, tw, dw)
                  load_w(wsb[:, :, 128:256], w_in[l], 8208 + j * 128, 128, tw, dw)
                  for (b0, bn, lis) in blocks:
                      pa, tpa = bank()
                      mm(pa[:, 0:bn], [(wsb[:, k, 0:128], bufQ[:, k, b0:b0 + bn]) for k in range(8)], [tw] + tQ, [tpa])
                      pg, tpg = bank()
                      mm(pg[:, 0:bn], [(wsb[:, k, 128:256], hT[:, k, b0:b0 + bn]) for k in range(8)],
                         [tw] + [tH[i] for i in lis], [tpg])
                      act(ev32[0][:, 0:bn], pg[:, 0:bn], AF.Sigmoid, [tpg], [tok("ev0")])
                      tt(ev32[0][:, 0:bn], ev32[0][:, 0:bn], pa[:, 0:bn], ALU.mult, [tok("ev0"), tpa], [tok("ev0")])
                      tt(mTv[:, j, b0:b0 + bn], ev32[0][:, 0:bn], mTv[:, j, b0:b0 + bn], ALU.add, [tok("ev0"), tPj[j]],
                         [tPj[j]])

              chk(8)
              S.dma("sp", rowA[:], ln_g[l:l + 1, :].partition_broadcast(128), ds_row[0], W=[tok("rowA")])
              S.dma("sp", rowB[:], ln_b[l:l + 1, :].partition_broadcast(128), ds_row[1], W=[tok("rowB")])
              wo = []
              for hf in range(2):
                  ws, tw, dw = slot()
                  wsb = ws[:].bitcast(BF16)
                  load_w(wsb, w_o[l], hf * 512, 512, tw, dw)
                  wo.append((wsb, tw))
              for (li, t, c0, pt, smp) in tinfo:
                  gt = gate_s if smp else gate_p
                  tg = tok("gate_s") if smp else tok("gate_p")
                  for hf in range(2):
                      ps, tp = bank()
                      mm(ps[:pt, :], [(mTv[:, k, c0:c0 + pt], wo[hf][0][:, k, :]) for k in range(8)], tPj + [wo[hf][1]], [tp])
                      tt(scr[0][:pt, hf * 512:(hf + 1) * 512], ps[:pt, :], gt[:pt, hf * 512:(hf + 1) * 512], ALU.mult,
                         [tp, tg], [tok("scr0")])
                  stt(scr[0][:pt, :], x_sb[:pt, li, :], float(ALPHA), scr[0][:pt, :], ALU.mult, ALU.add, [tX[li], tok("scr0")],
                      [tok("scr0")])
                  layer_norm_stats(scr[0][:pt, :], pt, [tok("scr0")])
                  act(scr[1][:pt, :], scr[0][:pt, :], AF.Identity, [tok("scr0"), tok("rstd"), tok("nbias")], [tok("scr1")],
                      bias=nbias[:pt, :], scale=rstd[:pt, :])
                  tt(scr[1][:pt, :], scr[1][:pt, :], rowA[:pt, :], ALU.mult, [tok("scr1"), tok("rowA")], [tok("scr1")])
                  tt(x_sb[:pt, li, :], scr[1][:pt, :], rowB[:pt, :], ALU.add, [tok("scr1"), tok("rowB")], [tX[li]])
                  if last_layer:
                      if smp:
                          out_dma(y_s, x_sb[:64, li, :], [tX[li]])
                      else:
                          out_dma(y_p[t * 128:(t + 1) * 128, :], x_sb[:, li, :], [tX[li]])

    try:
        main_loop()
    except _Stop:
        pass
    for i_ in range(int(os.environ.get("KDMA", "0"))):
        S.dma("pool", wba[:], w_in[0].rearrange("(k p) c -> p k c", p=128)[:, :, 16 * (i_ % 500):16 * (i_ % 500) + 16], ds_ba,
              W=[tok("wba")])
    for i_ in range(int(os.environ.get("KDMAH", "0"))):
        S.dma("sp", rowA[:], xin[i_:i_ + 1, :].partition_broadcast(128), ds_row[0], W=[tok("rowA")])
    for i_ in range(int(os.environ.get("KACT", "0"))):
        nc.scalar.activation(out=ev32[0][0:1, 0:1], in_=one_c[0:1, 0:1], func=AF.Gelu_apprx_tanh)
        nc.scalar.activation(out=ev32[0][0:1, 0:1], in_=one_c[0:1, 0:1], func=AF.Silu)
    if os.environ.get("KINC"):
        semx = nc.alloc_semaphore("dummy_inc")
        for i_ in range(int(os.environ["KINC"])):
            nc.vector.memset(ev32[1][0:1, 0:1], 0.0).then_inc(semx, 1)
    if os.environ.get("KDUMMY"):
        for e_ in ("pe", "act", "dve"):
            for b in tPS:
                S._wait(e_, b.w)
                for ev in b.r.values():
                    S._wait(e_, ev)
        for _ in range(20000):
            nc.tensor.matmul(psum[0][0:1, 0:1], ones_row[0:1, 0:1], ones_row[0:1, 0:1], start=True, stop=True)
        for _ in range(12000):
            nc.scalar.copy(out=ev32[0][0:1, 0:1], in_=one_c[0:1, 0:1])
        for _ in range(12000):
            nc.vector.memset(ev32[1][0:1, 0:1], 0.0)
    for (sm, c) in S.final_waits:
        nc.sync.wait_ge(sm, c)
    for d in S.dsems:
        if d.count:
            nc.sync.wait_ge(d.sem, d.count)
    print("sbuf bytes remaining:", nc.sbuf_bytes_remaining, "instr counts:", S.total, "nsem:", S.nsem)
    return nc


def _consts():
    c = np.zeros((128, 12, 128), np.float32)
    i = np.arange(128)
    a, b = i[:, None], i[None, :]
    c[:, 0] = (a == b)
    c[:, 1] = (a <= b)
    c[:, 2] = (a > b)
    c[:, 3] = -1.0 * (b > a)
    c[:, 4] = (b >= a)
    c[:, 5] = -1.0 * (b < a)
    same = ((a // 4) == (b // 4)) & (a < 64) & (b < 64)
    c[:, 6] = same & (a <= b)
    c[:, 7] = same & (a > b)
    c[:, 8] = -1.0 * (same & (b > a))
    c[:, 9] = same & (b >= a)
    c[:, 10] = -1.0 * (same & (b < a))
    sel = np.zeros((16, 8, 128), np.float32)
    for h in range(8):
        sel[h, h, :] = 1.0
    seqsel = (np.arange(64)[:, None] // 4 == np.arange(16)[None, :]).astype(np.float32)
    return c, sel, seqsel


def kernel(x_prompt, x_sample, state_conv, state_ssm, c_prompt, c_sample, w_ada, b_ada, w_in,
           w_s, b_s, lnv_g, lnv_b, conv_w, a_log, dt_bias, onorm_g, w_pa, w_pb, w_o, ln_g, ln_b):
    f = lambda a: np.ascontiguousarray(np.asarray(a, dtype=np.float32))
    (x_prompt, x_sample, state_conv, state_ssm, c_prompt, c_sample, w_ada, b_ada, w_in, w_s, b_s, lnv_g, lnv_b,
     conv_w, a_log, dt_bias, onorm_g, w_pa, w_pb, w_o, ln_g, ln_b) = [f(a) for a in (
        x_prompt, x_sample, state_conv, state_ssm, c_prompt, c_sample, w_ada, b_ada, w_in, w_s, b_s, lnv_g, lnv_b,
        conv_w, a_log, dt_bias, onorm_g, w_pa, w_pb, w_o, ln_g, ln_b)]
    depth = w_in.shape[0]
    nc = build_nc(depth)
    consts, sel, seqsel = _consts()
    b_adaT = f(b_ada[:, :2048].reshape(depth, 16, 128).transpose(2, 0, 1))
    b_gate = f(b_ada[:, 2048:])
    w_sT = f(w_s.transpose(0, 3, 1, 2))
    w_sTs = np.zeros((depth, 64, 8, 64), np.float32)
    for q in range(16):
        w_sTs[:, 4 * q:4 * q + 4, :, 4 * q:4 * q + 4] = w_s[:, :, :4, :4].transpose(0, 3, 1, 2)
    bsr = f(b_s.reshape(1, -1))
    bsrs = f(np.tile(b_s[:, :, :4], (1, 1, 16)).reshape(1, -1))
    cw = f(conv_w.reshape(depth, 4, 24, 128).transpose(3, 0, 2, 1))
    alog = f(a_log.reshape(1, -1))
    dtb = f(dt_bias.reshape(1, -1))
    ogc = f(onorm_g.T)
    shared = dict(w_ada=w_ada, b_adaT=b_adaT, b_gate=b_gate, w_in=w_in, w_sT=w_sT, w_sTs=w_sTs, bsr=bsr, bsrs=bsrs,
                  lnv_g=lnv_g, lnv_b=lnv_b, cw=cw, alog=alog, dtb=dtb, ogc=ogc, w_pa=w_pa, w_pb=w_pb, w_o=w_o,
                  ln_g=ln_g, ln_b=ln_b, consts=consts, selc=sel, seqsel=seqsel)
    in_maps = []
    for i in range(NCORE):
        ss = slice(NSEQ * i, NSEQ * (i + 1))
        xin = f(np.concatenate([x_prompt[i], x_sample[ss].reshape(64, D)], axis=0))
        cc = np.concatenate([c_prompt[i:i + 1], c_sample[ss]], axis=0)
        cT = f(cc.reshape(17, 8, 128).transpose(2, 1, 0))
        sconv = f(state_conv[:, ss].reshape(depth, NSEQ, 3, 24, 128).transpose(0, 4, 3, 1, 2))
        sssm = f(state_ssm[:, ss])
        m = dict(shared)
        m.update(xin=xin, cT=cT, sconv=sconv, sssm=sssm)
        in_maps.append(m)
    import os
    if os.environ.get('KTRACE'):
        res = run_bass_kernel_spmd(nc, in_maps, core_ids=list(range(NCORE)), trace=True)
        print('EXEC_TIME_NS', res.exec_time_ns)
    else:
        res = run_bass_kernel_spmd(nc, in_maps, core_ids=list(range(NCORE)))
    R = res.results
    y_prompt = np.stack([R[i]["y_p"] for i in range(NCORE)], 0)
    y_sample = np.concatenate([R[i]["y_s"].reshape(NSEQ, 4, D) for i in range(NCORE)], 0)
    conv_p = np.stack([R[i]["o_convp"] for i in range(NCORE)], 1)
    ssm_p = np.stack([R[i]["o_ssmp"] for i in range(NCORE)], 1)
    cv_p = np.stack([R[i]["o_cvp"] for i in range(NCORE)], 1)
    conv_s = np.concatenate([R[i]["o_convs"] for i in range(NCORE)], 1)
    ssm_s = np.concatenate([R[i]["o_ssms"] for i in range(NCORE)], 1)
    cv_s = np.concatenate([R[i]["o_cvs"].reshape(depth, NSEQ, 4, D) for i in range(NCORE)], 1)
    return tuple(np.ascontiguousarray(a, dtype=np.float32) for a in
                 (y_prompt, y_sample, conv_p, ssm_p, cv_p, conv_s, ssm_s, cv_s))
```

```python
import numpy as np
import concourse.bass as bass
import concourse.mybir as mybir
from concourse.bass_utils import run_bass_kernel_spmd

F32, BF16 = mybir.dt.float32, mybir.dt.bfloat16
AF = mybir.ActivationFunctionType
ALU = mybir.AluOpType

D = 1024
DEPTH = 4
NCORE = 8
SEQ = 2048
NPT = 16
NSEQ = 16
P_IN = 9232
ALPHA = (2 * DEPTH) ** 0.25
LN_EPS = 1e-5
NORM_EPS = 1e-6
GROUPS = [[0, 1, 2, 3, 16], [4, 5, 6, 7], [8, 9, 10, 11], [12, 13, 14, 15]]
MAXTOK = 576


class Buf:
    __slots__ = ("name", "w", "r")

    def __init__(self, name):
        self.name = name
        self.w = None
        self.r = {}


class DmaSem:
    def __init__(self, sem, name):
        self.sem = sem
        self.count = 0
        self.name = name


class Sched:
    import os as _os
    EPOCH = int(_os.environ.get('KEPOCH', '8000'))

    def __init__(self, nc):
        self.nc = nc
        self.eng = {"pe": nc.tensor, "act": nc.scalar, "dve": nc.vector, "pool": nc.gpsimd, "sp": nc.sync}
        self.sem = {k: nc.alloc_semaphore("sem_" + k) for k in self.eng}
        self.epoch = {k: 0 for k in self.eng}
        self.cnt = {k: 0 for k in self.eng}
        self.total = {k: 0 for k in self.eng}
        self.known = {k: {} for k in self.eng}
        self.dsems = []
        self.final_waits = []
        self.nsem = len(self.eng)

    def dsem(self, name):
        d = DmaSem(self.nc.alloc_semaphore("ds_" + name), name)
        d.gen = 0
        self.dsems.append(d)
        self.nsem += 1
        return d

    def _wait(self, eng, ev):
        if ev is None:
            return
        if ev[0] == "e":
            _, src, ep, idx, sem = ev
            if src == eng and eng in ("pe", "sp"):
                return
            key = ("e", src)
        else:
            _, dname, ep, idx, sem = ev
            key = ("d", dname)
        kep, kval = self.known[eng].get(key, (-1, 0))
        if kep > ep or (kep == ep and kval >= idx):
            return
        self.known[eng][key] = (ep, idx)
        self.eng[eng].wait_ge(sem, idx)

    def _deps(self, eng, R, W):
        for b in R:
            self._wait(eng, b.w)
        for b in W:
            self._wait(eng, b.w)
            for ev in b.r.values():
                if ev[0] == "e" and ev[1] == eng and eng == "pe":
                    continue
                self._wait(eng, ev)

    def ops(self, eng, fns, R=(), W=()):
        self._deps(eng, R, W)
        if self.cnt[eng] >= self.EPOCH:
            self.sem[eng] = self.nc.alloc_semaphore(f"sem_{eng}_{self.epoch[eng] + 1}")
            self.epoch[eng] += 1
            self.cnt[eng] = 0
            self.nsem += 1
        ins = None
        for fn in fns:
            ins = fn()
        self.cnt[eng] += 1
        self.total[eng] += 1
        ins.then_inc(self.sem[eng], 1)
        ev = ("e", eng, self.epoch[eng], self.cnt[eng], self.sem[eng])
        for b in W:
            b.w = ev
            b.r = {}
        for b in R:
            b.r[eng] = ev
        return ins

    def op(self, eng, fn, R=(), W=()):
        return self.ops(eng, [fn], R, W)

    def dma(self, q, out, in_, ds, R=(), W=()):
        self._deps(q, R, W)
        if ds.count + 16 > self.EPOCH:
            self.final_waits.append((ds.sem, ds.count))
            ds.gen += 1
            ds.sem = self.nc.alloc_semaphore(f"ds_{ds.name}_{ds.gen}")
            ds.count = 0
            self.nsem += 1
        ins = self.eng[q].dma_start(out=out, in_=in_)
        ds.count += 16
        ins.then_inc(ds.sem, 16)
        ev = ("d", ds.name, ds.gen, ds.count, ds.sem)
        for b in W:
            b.w = ev
            b.r = {}
        for b in R:
            b.r[("d", ds.name)] = ev


class _Stop(Exception):
    pass


def build_nc(depth=DEPTH):
    import os
    KSTOP = float(os.environ.get("KSTOP", "99"))
    KGRP = os.environ.get("KGRP")
    groups = GROUPS if KGRP is None else [GROUPS[int(g_)] for g_ in KGRP.split(',')]

    def chk(n):
        if KSTOP <= n:
            raise _Stop()
    nc = bass.Bass("TRN2", target_bir_lowering=False)
    S = Sched(nc)
    dt_in = lambda name, shape: nc.dram_tensor(name, list(shape), F32, kind="ExternalInput").ap()
    dt_out = lambda name, shape: nc.dram_tensor(name, list(shape), F32, kind="ExternalOutput").ap()

    xin = dt_in("xin", [NPT * 128 + 64, D])
    cT = dt_in("cT", [128, 8, 17])
    sconv = dt_in("sconv", [depth, 128, 24, NSEQ, 3])
    sssm = dt_in("sssm", [depth, NSEQ, 8, 128, 128])
    w_ada = dt_in("w_ada", [depth, D, 3 * D])
    b_adaT = dt_in("b_adaT", [128, depth, 16])
    b_gate = dt_in("b_gate", [depth, D])
    w_in = dt_in("w_in", [depth, D, P_IN])
    w_sT = dt_in("w_sT", [depth, 128, 8, 128])
    w_sTs = dt_in("w_sTs", [depth, 64, 8, 64])
    bsr = dt_in("bsr", [1, depth * 8 * 128])
    bsrs = dt_in("bsrs", [1, depth * 8 * 64])
    lnv_g = dt_in("lnv_g", [depth, D])
    lnv_b = dt_in("lnv_b", [depth, D])
    cw = dt_in("cw", [128, depth, 24, 4])
    alog = dt_in("alog", [1, depth * 8])
    dtb = dt_in("dtb", [1, depth * 8])
    ogc = dt_in("ogc", [128, depth])
    w_pa = dt_in("w_pa", [depth, D, D])
    w_pb = dt_in("w_pb", [depth, D, D])
    w_o = dt_in("w_o", [depth, D, D])
    ln_g = dt_in("ln_g", [depth, D])
    ln_b = dt_in("ln_b", [depth, D])
    consts = dt_in("consts", [128, 12, 128])
    selc = dt_in("selc", [16, 8, 128])
    seqsel = dt_in("seqsel", [64, 16])

    y_p = dt_out("y_p", [NPT * 128, D])
    y_s = dt_out("y_s", [64, D])
    o_convp = dt_out("o_convp", [depth, 3, 3 * D])
    o_ssmp = dt_out("o_ssmp", [depth, 8, 128, 128])
    o_cvp = dt_out("o_cvp", [depth, 128, D])
    o_convs = dt_out("o_convs", [depth, NSEQ, 3, 3 * D])
    o_ssms = dt_out("o_ssms", [depth, NSEQ, 8, 128, 128])
    o_cvs = dt_out("o_cvs", [depth, 64, D])

    sb = lambda name, shape, dt=F32: nc.alloc_sbuf_tensor(name, list(shape), dt)
    x_sb = sb("x_sb", [128, 5, D])
    hT = sb("hT", [128, 8, MAXTOK], BF16)
    bufP = sb("bufP", [128, 5 * D], BF16)
    bufQ = sb("bufQ", [128, 8, MAXTOK], BF16)
    NSLOT = 2
    wslot = [sb(f"wslot{i}", [128, 8, 256]) for i in range(NSLOT)]
    rowA = sb("rowA", [128, D])
    rowB = sb("rowB", [128, D])
    gate_p = sb("gate_p", [128, D])
    gate_s = sb("gate_s", [128, D])
    scr = [sb(f"scr{i}", [128, D]) for i in range(2)]
    xn_bf = sb("xn_bf", [128, D], BF16)
    cst = sb("cst", [128, 12, 128])
    ident_bf = sb("ident_bf", [128, 128], BF16)
    ones_bf = sb("ones_bf", [128, 128], BF16)
    ones_f = sb("ones_f", [128, 128])
    sel_bf = sb("sel_bf", [16, 8, 128], BF16)
    seqsel_sb = sb("seqsel_sb", [64, 16])
    eps_ln = sb("eps_ln", [128, 1])
    eps_nm = sb("eps_nm", [128, 1])
    one_c = sb("one_c", [128, 1])
    scT = sb("scT", [128, 8, 17])
    scT_rep = sb("scT_rep", [128, 8, 128])
    scT_s4 = sb("scT_s4", [128, 8, 64])
    modT_all = sb("modT_all", [128, depth, 16, 17])
    opsT_all = sb("opsT_all", [128, depth, 8, 17])
    badaT = sb("badaT", [128, depth, 16])
    cw_sb = sb("cw_sb", [128, depth, 24, 4])
    ogc_sb = sb("ogc_sb", [128, depth])
    alog_sb = sb("alog_sb", [128, depth * 8])
    dtb_sb = sb("dtb_sb", [128, depth * 8])
    nA_sb = sb("nA_sb", [128, depth * 8])
    wsT_bf = sb("wsT_bf", [128, 8, 128], BF16)
    wsTs_bf = sb("wsTs_bf", [64, 8, 64], BF16)
    bsr_bf = sb("bsr_bf", [1, 8 * 128], BF16)
    bsrs_bf = sb("bsrs_bf", [1, 8 * 64], BF16)
    ones_row = sb("ones_row", [1, 128], BF16)
    scv_bf = sb("scv_bf", [128, 24, NSEQ, 3], BF16)
    halo_all = sb("halo_all", [128, depth, 24, 3], BF16)
    mv = sb("mv", [128, 2])
    st6 = sb("st6", [128, 2, 6])
    rstd = sb("rstd", [128, 1])
    nbias = sb("nbias", [128, 1])
    wba = sb("wba", [128, 8, 16], BF16)
    betaT = sb("betaT", [16, MAXTOK], BF16)
    beta_tok = sb("beta_tok", [128, 5, 8])
    apre = sb("apre", [128, 5, 8])
    g_tok = sb("g_tok", [128, 5, 8])
    gam_tok = sb("gam_tok", [128, 5, 8])
    nbG = sb("nbG", [128, 5, 8])
    egl = sb("egl", [128, 5, 8])
    GC = sb("GC", [128, 5, 8])
    G2 = sb("G2", [64, 16, 8])
    GCs = sb("GCs", [128, 16, 8])
    xpre = sb("xpre", [128, 3 + 512], BF16)
    xps = sb("xps", [128, NSEQ, 7], BF16)
    diag = sb("diag", [128, 12, 128], BF16)
    sqb = sb("sqb", [128, MAXTOK], BF16)
    rn = sb("rn", [128, MAXTOK])
    rn2 = sb("rn2", [128, MAXTOK])
    kT = sb("kT", [128, MAXTOK], BF16)
    kbT = sb("kbT", [128, MAXTOK], BF16)
    qT = sb("qT", [128, MAXTOK], BF16)
    vT = sb("vT", [128, MAXTOK], BF16)
    zbs = sb("zbs", [128, MAXTOK], BF16)
    Vb = sb("Vb", [128, 5, 128], BF16)
    Kd = sb("Kd", [128, 5, 128], BF16)
    S32_all = sb("S32_all", [128, depth, 8, 128])
    S_bf = sb("S_bf", [128, 8, 128], BF16)
    S0f = sb("S0f", [128, 8, 128])
    S0b = sb("S0b", [128, NSEQ, 128], BF16)
    Kdm = sb("Kdm", [64, 128], BF16)
    NSETS = int(os.environ.get("KSETS", "3"))
    tsets = []
    for si in range(NSETS):
        tsets.append((
            sb(f"Gm{si}", [128, 128]), sb(f"eET{si}", [128, 128]), sb(f"eE{si}", [128, 128]), sb(f"Grow{si}", [128, 128]),
            sb(f"DTsn{si}", [128, 128]), sb(f"DTi{si}", [128, 128]), sb(f"Dsn{si}", [128, 128]),
            sb(f"qgT{si}", [128, 128], BF16),
            [sb(f"Mb{si}_{i}", [128, 128], BF16) for i in range(2)],
            [sb(f"Nb{si}_{i}", [128, 128], BF16) for i in range(2)],
            [sb(f"Xb{si}_{i}", [128, 128], BF16) for i in range(2)],
            sb(f"PTb{si}", [128, 128], BF16), sb(f"Rb{si}", [128, 128], BF16), sb(f"Vnb{si}", [128, 128], BF16),
            sb(f"KSTb{si}", [128, 64], BF16), sb(f"sqo{si}", [128, 128], BF16), sb(f"rro{si}", [128, 128]),
            sb(f"t1o{si}", [128, 128])))
    ev32 = [sb(f"ev32_{i}", [128, 512]) for i in range(2)]

    psum = [nc.alloc_psum_tensor(f"ps{i}", [128, 512], F32) for i in range(8)]

    T = {}

    def tok(name):
        if name not in T:
            T[name] = Buf(name)
        return T[name]

    tX = [tok(f"x{i}") for i in range(5)]
    tH = [tok(f"h{i}") for i in range(5)]
    tPt = [tok(f"Pt{i}") for i in range(5)]
    tPj = [tok(f"Pj{i}") for i in range(8)]
    tQ = [tok(f"Q{i}") for i in range(8)]
    tPS = [tok(f"ps{i}") for i in range(8)]
    tW = [tok(f"w{i}") for i in range(NSLOT)]
    dsW = [S.dsem(f"w{i}") for i in range(NSLOT)]
    dsWh = [S.dsem(f"wh{i}") for i in range(NSLOT)]
    state = {"bank": 0, "slot": 0}

    def bank():
        i = state["bank"]
        state["bank"] = (i + 1) % 8
        return psum[i], tPS[i]

    def slot(hw=False):
        i = state["slot"]
        state["slot"] = (i + 1) % NSLOT
        return wslot[i], tW[i], (dsWh[i] if hw else dsW[i])

    ds_c = S.dsem("const")
    ds_c2 = S.dsem("const_sw")
    ds_x = [S.dsem(f"x{i}") for i in range(5)]
    ds_row = [S.dsem("rowA"), S.dsem("rowB")]
    ds_ba = S.dsem("wba")
    _ds_named = {}

    def ds_for(name):
        if name not in _ds_named:
            _ds_named[name] = S.dsem(name)
        return _ds_named[name]

    V, A, PE, PO = nc.vector, nc.scalar, nc.tensor, nc.gpsimd

    def act(out, in_, func, R, W, **kw):
        S.op("act", lambda: A.activation(out=out, in_=in_, func=func, **kw), R, W)

    def tt(out, in0, in1, op, R, W, eng="dve"):
        e = V if eng == "dve" else PO
        S.op(eng, lambda: e.tensor_tensor(out=out, in0=in0, in1=in1, op=op), R, W)

    def ts(out, in0, s1, s2, op0, op1, R, W):
        if op1 is None:
            S.op("dve", lambda: V.tensor_scalar(out=out, in0=in0, scalar1=s1, scalar2=None, op0=op0), R, W)
        else:
            S.op("dve", lambda: V.tensor_scalar(out=out, in0=in0, scalar1=s1, scalar2=s2, op0=op0, op1=op1), R, W)

    def stt(out, in0, scalar, in1, op0, op1, R, W):
        S.op("dve", lambda: V.scalar_tensor_tensor(out=out, in0=in0, scalar=scalar, in1=in1, op0=op0, op1=op1), R, W)

    def cp(out, in_, R, W, eng="dve"):
        if eng == "act":
            S.op("act", lambda: A.copy(out=out, in_=in_), R, W)
        else:
            S.op("dve", lambda: V.tensor_copy(out=out, in_=in_), R, W)

    def mm(out, pairs, R, W, first=True, last=True):
        n = len(pairs)
        fns = []
        for i, (l, r) in enumerate(pairs):
            fns.append(lambda l=l, r=r, i=i: PE.matmul(out, l, r, start=(first and i == 0), stop=(last and i == n - 1),
                                                      skip_group_check=True))
        S.ops("pe", fns, R, W)

    def transpose(out, in_, idn, R, W):
        S.op("pe", lambda: PE.transpose(out, in_, idn), R, W)

    tC = tok("const")
    S.dma("sp", cst[:], consts, ds_c, W=[tC])
    S.dma("pool", ident_bf[:], consts[:, 0, :], ds_c2, W=[tok("c2")])
    S.dma("pool", sel_bf[:], selc, ds_c2, W=[tok("c2")])
    S.dma("sp", seqsel_sb[:], seqsel, ds_c, W=[tC])
    S.dma("sp", scT[:], cT, ds_c, W=[tC])
    S.dma("sp", badaT[:], b_adaT, ds_c, W=[tC])
    S.dma("sp", cw_sb[:], cw, ds_c, W=[tC])
    S.dma("sp", ogc_sb[:], ogc, ds_c, W=[tC])
    S.dma("sp", alog_sb[:], alog.partition_broadcast(128), ds_c, W=[tC])
    S.dma("sp", dtb_sb[:], dtb.partition_broadcast(128), ds_c, W=[tC])
    for e_ in ("pe", "act", "dve", "pool", "sp"):
        S.eng[e_].wait_ge(ds_c.sem, ds_c.count)
        S.eng[e_].wait_ge(ds_c2.sem, ds_c2.count)
    tC.w = None
    MASKS = {
        False: dict(Lincl=cst[:, 1, :], Ustr=cst[:, 2, :], mSTn=cst[:, 3, :], mIT=cst[:, 4, :], mSNn=cst[:, 5, :]),
        True: dict(Lincl=cst[:, 6, :], Ustr=cst[:, 7, :], mSTn=cst[:, 8, :], mIT=cst[:, 9, :], mSNn=cst[:, 10, :]),
    }
    tMisc = tok("misc")
    S.op("dve", lambda: V.memset(ones_bf[:], 1.0), W=[tMisc])
    S.op("dve", lambda: V.memset(ones_f[:], 1.0), W=[tMisc])
    S.op("dve", lambda: V.memset(ones_row[:], 1.0), W=[tMisc])
    S.op("dve", lambda: V.memset(eps_ln[:], LN_EPS), W=[tMisc])
    S.op("dve", lambda: V.memset(eps_nm[:], NORM_EPS), W=[tMisc])
    S.op("dve", lambda: V.memset(one_c[:], 1.0), W=[tMisc])
    S.op("dve", lambda: V.memset(halo_all[:], 0.0), W=[tok("halo")])
    S.op("dve", lambda: V.memset(S32_all[:], 0.0), W=[tok("S32")])
    tSc = tok("scT")
    act(scT[:], scT[:], AF.Silu, [tC], [tSc])
    cp(scT_rep[:], scT[:, :, 0:1].to_broadcast([128, 8, 128]), [tSc], [tok("scTrep")])
    cp(scT_s4[:].rearrange("p k (s t) -> p k s t", t=4), scT[:, :, 1:17].unsqueeze(3).to_broadcast([128, 8, 16, 4]),
       [tSc], [tok("scTs4")])
    act(nA_sb[:], alog_sb[:], AF.Exp, [tC], [tok("nA")])
    ts(nA_sb[:], nA_sb[:], -1.0, None, ALU.mult, None, [tok("nA")], [tok("nA")])

    def rsqrt_small(out, in_, eps_t, scale, R, W, n=128):
        act(out, in_, AF.Ln, R, W, bias=eps_t[:n, :], scale=scale)
        act(out, out, AF.Exp, W, W, scale=-0.5)

    def layer_norm_stats(src, pt, Rt):
        tS = tok("st6")
        S.op("dve", lambda: V.bn_stats(st6[:pt, 0, :], src[:, 0:512]), Rt, [tS])
        S.op("dve", lambda: V.bn_stats(st6[:pt, 1, :], src[:, 512:1024]), Rt + [tS], [tS])
        S.op("dve", lambda: V.bn_aggr(mv[:pt, :], st6[:pt, :, :]), [tS], [tok("mv")])
        rsqrt_small(rstd[:pt, :], mv[:pt, 1:2], eps_ln, 1.0, [tok("mv")], [tok("rstd")], n=pt)
        stt(nbias[:pt, :], mv[:pt, 0:1], -1.0, rstd[:pt, :], ALU.mult, ALU.mult, [tok("mv"), tok("rstd")], [tok("nbias")])

    def load_w(dst3, src_l, c0, ncols, tW_, dsW_, q="pool"):
        S.dma(q, dst3, src_l.rearrange("(k p) c -> p k c", p=128)[:, :, c0:c0 + ncols], dsW_, W=[tW_])

    out_ds_i = [0]

    def out_dma(dst, src, R):
        S.dma("sp", dst, src, ds_for("o_" + R[0].name), R=R)

    for l in range(depth):
        for g in range(8):
            ws, tw, dw = slot(hw=True)
            S.dma("sp", ws[:], w_ada[l].rearrange("(k p) c -> p k c", p=128)[:, :, g * 256:(g + 1) * 256], dw, W=[tw])
            for c2 in range(2):
                ct = g * 2 + c2
                ps, tp = bank()
                mm(ps[:, 0:17], [(ws[:, k, c2 * 128:(c2 + 1) * 128], scT[:, k, :]) for k in range(8)], [tw, tSc], [tp])
                ts(modT_all[:, l, ct, :], ps[:, 0:17], badaT[:, l, ct:ct + 1], None, ALU.add, None, [tp, tC], [tok("modT")])
    ts(opsT_all[:], modT_all[:, :, 8:16, :], 1.0, None, ALU.add, None, [tok("modT")], [tok("opsT")])

    def main_loop():
      chk(0)
      for gi, tiles in enumerate(groups):
          has_s = 16 in tiles
          nt = len(tiles)
          blocks = [(0, 512, list(range(4)))] + ([(512, 64, [4])] if has_s else [])
          tinfo = []
          for li, t in enumerate(tiles):
              tinfo.append((li, t, li * 128, 128 if t < 16 else 64, t == 16))
          for (li, t, c0, pt, smp) in tinfo:
              S.dma("sp", x_sb[:pt, li, :], xin[t * 128:t * 128 + pt, :], ds_x[li], W=[tX[li]])

          for l in range(depth):
              last_layer = (l == depth - 1)
              modT = modT_all[:, l]
              opsT = opsT_all[:, l]
              halo = halo_all[:, l]
              S32 = S32_all[:, l]
              wsT_f = scr[0][:, :].rearrange("p (h t) -> p h t", h=8)
              S.dma("sp", wsT_f, w_sT[l], ds_for("l_wsT"), W=[tok("scr0")])
              tt(wsT_bf[:], wsT_f, MASKS[False]["mIT"].unsqueeze(1).to_broadcast([128, 8, 128]), ALU.mult,
                 [tok("scr0"), tC], [tok("wsT")])
              S.dma("pool", bsr_bf[:], bsr[:, l * 1024:(l + 1) * 1024], ds_for("l_bsr"), W=[tok("bsr")])
              if has_s:
                  wsTs_f = scr[1][:64, 0:512].rearrange("p (h t) -> p h t", h=8)
                  S.dma("sp", wsTs_f, w_sTs[l], ds_for("l_wsTs"), W=[tok("scr1")])
                  tt(wsTs_bf[:], wsTs_f, MASKS[True]["mIT"][:64, :64].unsqueeze(1).to_broadcast([64, 8, 64]), ALU.mult,
                     [tok("scr1"), tC], [tok("wsTs")])
                  S.dma("pool", bsrs_bf[:], bsrs[:, l * 512:(l + 1) * 512], ds_for("l_bsrs"), W=[tok("bsrs")])
                  S.dma("pool", scv_bf[:], sconv[l], ds_for("l_scv"), W=[tok("scv")])
              cp(S_bf[:], S32, [tok("S32")], [tok("Sbf")], eng="act")
              S.dma("sp", rowA[:], b_gate[l:l + 1, :].partition_broadcast(128), ds_row[0], W=[tok("rowA")])
              for g in range(4):
                  ws, tw, dw = slot(hw=True)
                  S.dma("sp", ws[:], w_ada[l].rearrange("(k p) c -> p k c", p=128)[:, :, 2048 + g * 256:2048 + (g + 1) * 256],
                        dw, W=[tw])
                  c0g = g * 256
                  ps, tp = bank()
                  mm(ps[:, 0:256], [(scT_rep[:, k, :], ws[:, k, :]) for k in range(8)], [tw, tok("scTrep")], [tp])
                  tt(gate_p[:, c0g:c0g + 256], ps[:, 0:256], rowA[:, c0g:c0g + 256], ALU.add, [tp, tok("rowA")], [tok("gate_p")])
                  if has_s:
                      ps, tp = bank()
                      mm(ps[:64, 0:256], [(scT_s4[:, k, :], ws[:, k, :]) for k in range(8)], [tw, tok("scTs4")], [tp])
                      tt(gate_s[:64, c0g:c0g + 256], ps[:64, 0:256], rowA[:64, c0g:c0g + 256], ALU.add, [tp, tok("rowA")],
                         [tok("gate_s")])

              chk(1)
              for (li, t, c0, pt, smp) in tinfo:
                  layer_norm_stats(x_sb[:pt, li, :], pt, [tX[li]])
                  act(xn_bf[:pt, :], x_sb[:pt, li, :], AF.Identity, [tX[li], tok("rstd"), tok("nbias")], [tok("xn")],
                      bias=nbias[:pt, :], scale=rstd[:pt, :])
                  ps, tp = bank()
                  psb = ps[:].bitcast(BF16)
                  for k in range(8):
                      transpose(psb[:, k * 128:k * 128 + pt], xn_bf[:pt, k * 128:(k + 1) * 128], ident_bf[:pt, :pt],
                                [tok("xn"), tC], [tp])
                  for k in range(8):
                      if not smp:
                          act(hT[:, k, c0:c0 + pt], psb[:, k * 128:k * 128 + pt], AF.Identity,
                              [tp, tok("opsT"), tok("modT")], [tH[li]], bias=modT[:, k, 0:1], scale=opsT[:, k, 0:1])
                      else:
                          tt(ev32[0][:, 0:64].rearrange("p (s t) -> p s t", t=4),
                             psb[:, k * 128:k * 128 + 64].rearrange("p (s t) -> p s t", t=4),
                             opsT[:, k, 1:17].unsqueeze(2).to_broadcast([128, 16, 4]), ALU.mult,
                             [tp, tok("opsT")], [tok("ev0")])
                          tt(hT[:, k, c0:c0 + 64].rearrange("p (s t) -> p s t", t=4),
                             ev32[0][:, 0:64].rearrange("p (s t) -> p s t", t=4),
                             modT[:, k, 1:17].unsqueeze(2).to_broadcast([128, 16, 4]), ALU.add,
                             [tok("ev0"), tok("modT")], [tH[li]])

              chk(2)
              S.dma("sp", rowA[:], lnv_g[l:l + 1, :].partition_broadcast(128), ds_row[0], W=[tok("rowA")])
              S.dma("sp", rowB[:], lnv_b[l:l + 1, :].partition_broadcast(128), ds_row[1], W=[tok("rowB")])
              wv = []
              for hf in range(2):
                  ws, tw, dw = slot()
                  wsb = ws[:].bitcast(BF16)
                  load_w(wsb, w_in[l], 1024 + hf * 512, 512, tw, dw)
                  wv.append((wsb, tw))
              for (li, t, c0, pt, smp) in tinfo:
                  for hf in range(2):
                      ps, tp = bank()
                      mm(ps[:pt, :], [(hT[:, k, c0:c0 + pt], wv[hf][0][:, k, :]) for k in range(8)],
                         [tH[li], wv[hf][1]], [tp])
                      act(scr[0][:pt, hf * 512:(hf + 1) * 512], ps[:pt, :], AF.Gelu_apprx_tanh, [tp], [tok("scr0")])
                  layer_norm_stats(scr[0][:pt, :], pt, [tok("scr0")])
                  act(scr[1][:pt, :], scr[0][:pt, :], AF.Identity, [tok("scr0"), tok("rstd"), tok("nbias")], [tok("scr1")],
                      bias=nbias[:pt, :], scale=rstd[:pt, :])
                  tt(scr[1][:pt, :], scr[1][:pt, :], rowA[:pt, :], ALU.mult, [tok("scr1"), tok("rowA")], [tok("scr1")])
                  if t >= 15:
                      tt(scr[1][:pt, :], scr[1][:pt, :], rowB[:pt, :], ALU.add, [tok("scr1"), tok("rowB")], [tok("scr1")])
                      out_dma(o_cvp[l] if t == 15 else o_cvs[l], scr[1][:pt, :], [tok("scr1")])
                      cp(bufP[:pt, li * D:(li + 1) * D], scr[1][:pt, :], [tok("scr1")], [tPt[li]] + tPj)
                  else:
                      tt(bufP[:pt, li * D:(li + 1) * D], scr[1][:pt, :], rowB[:pt, :], ALU.add, [tok("scr1"), tok("rowB")],
                         [tPt[li]] + tPj)

              chk(3)
              for j in range(8):
                  ws, tw, dw = slot()
                  wsb = ws[:].bitcast(BF16)
                  load_w(wsb[:, :, 0:128], w_in[l], j * 128, 128, tw, dw)
                  load_w(wsb[:, :, 128:256], w_in[l], 2048 + j * 128, 128, tw, dw)
                  for (b0, bn, lis) in blocks:
                      pu, tpu = bank()
                      mm(pu[:, 0:bn], [(wsb[:, k, 0:128], hT[:, k, b0:b0 + bn]) for k in range(8)],
                         [tw] + [tH[i] for i in lis], [tpu])
                      pz, tpz = bank()
                      mm(pz[:, 0:bn], [(wsb[:, k, 128:256], hT[:, k, b0:b0 + bn]) for k in range(8)],
                         [tw] + [tH[i] for i in lis], [tpz])
                      pS, tpS = bank()
                      for li in lis:
                          (_, t, c0, pt, smp) = tinfo[li]
                          if not smp:
                              wmat = wsT_bf[:, j, :]
                              brow = bsr_bf[0:1, j * 128:(j + 1) * 128]
                              tws = [tok("wsT"), tok("bsr")]
                          else:
                              wmat = wsTs_bf[:, j, :]
                              brow = bsrs_bf[0:1, j * 64:(j + 1) * 64]
                              tws = [tok("wsTs"), tok("bsrs")]
                          mm(pS[:, c0 - b0:c0 - b0 + pt],
                             [(bufP[:pt, li * D + j * 128:li * D + (j + 1) * 128], wmat),
                              (ones_row[0:1, :], brow)], [tPt[li], tC, tMisc] + tws, [tpS])
                      act(ev32[0][:, 0:bn], pu[:, 0:bn], AF.Gelu_apprx_tanh, [tpu], [tok("ev0")])
                      act(ev32[1][:, 0:bn], pz[:, 0:bn], AF.Silu, [tpz], [tok("ev1")])
                      tt(ev32[0][:, 0:bn], ev32[0][:, 0:bn], pS[:, 0:bn], ALU.mult, [tok("ev0"), tpS], [tok("ev0")])
                      tt(bufQ[:, j, b0:b0 + bn], ev32[0][:, 0:bn], ev32[1][:, 0:bn], ALU.mult, [tok("ev0"), tok("ev1")],
                         [tQ[j]])

              chk(4)
              mTv = bufP[:, 0:8 * MAXTOK].rearrange("p (j n) -> p j n", j=8)
              for j in range(8):
                  ws, tw, dw = slot()
                  wsb = ws[:].bitcast(BF16)
                  load_w(wsb[:, :, 0:128], w_pa[l], j * 128, 128, tw, dw)
                  load_w(wsb[:, :, 128:256], w_in[l], 7184 + j * 128, 128, tw, dw)
                  for (b0, bn, lis) in blocks:
                      pa, tpa = bank()
                      mm(pa[:, 0:bn], [(wsb[:, k, 0:128], bufQ[:, k, b0:b0 + bn]) for k in range(8)], [tw] + tQ, [tpa])
                      pg, tpg = bank()
                      mm(pg[:, 0:bn], [(wsb[:, k, 128:256], hT[:, k, b0:b0 + bn]) for k in range(8)],
                         [tw] + [tH[i] for i in lis], [tpg])
                      act(ev32[0][:, 0:bn], pg[:, 0:bn], AF.Sigmoid, [tpg], [tok("ev0")])
                      tt(mTv[:, j, b0:b0 + bn], ev32[0][:, 0:bn], pa[:, 0:bn], ALU.mult, [tok("ev0"), tpa], [tPj[j]] + tPt)

              chk(5)
              S.dma("pool", wba[:], w_in[l].rearrange("(k p) c -> p k c", p=128)[:, :, 7168:7184], ds_ba, W=[tok("wba")])
              for (b0, bn, lis) in blocks:
                  ps, tp = bank()
                  mm(ps[:16, 0:bn], [(wba[:, k, :], hT[:, k, b0:b0 + bn]) for k in range(8)],
                     [tok("wba")] + [tH[i] for i in lis], [tp])
                  act(betaT[:, b0:b0 + bn], ps[:16, 0:bn], AF.Sigmoid, [tp], [tok("betaT")])
              psba, tpba = bank()
              for (li, t, c0, pt, smp) in tinfo:
                  mm(psba[:pt, li * 16:(li + 1) * 16], [(hT[:, k, c0:c0 + pt], wba[:, k, :]) for k in range(8)],
                     [tok("wba"), tH[li]], [tpba])
              for (li, t, c0, pt, smp) in tinfo:
                  act(beta_tok[:pt, li, :], psba[:pt, li * 16:li * 16 + 8], AF.Sigmoid, [tpba], [tok("beta_tok")])
                  tt(apre[:pt, li, :], psba[:pt, li * 16 + 8:li * 16 + 16], dtb_sb[:pt, l * 8:(l + 1) * 8], ALU.add,
                     [tpba, tC], [tok("apre")])
              for (li, t, c0, pt, smp) in tinfo:
                  act(apre[:pt, li, :], apre[:pt, li, :], AF.Exp, [tok("apre")], [tok("apre")])
                  act(apre[:pt, li, :], apre[:pt, li, :], AF.Ln, [tok("apre")], [tok("apre")], bias=one_c[:pt, :], scale=1.0)
                  tt(g_tok[:pt, li, :], apre[:pt, li, :], nA_sb[:pt, l * 8:(l + 1) * 8], ALU.mult, [tok("apre"), tok("nA")],
                     [tok("g_tok")])
                  mk = MASKS[smp]
                  ps, tp = bank()
                  mm(ps[:pt, 0:8], [(mk["Lincl"][:pt, :pt], g_tok[:pt, li, :])], [tok("g_tok"), tC], [tp])
                  mm(ps[:pt, 8:16], [(mk["Ustr"][:pt, :pt], g_tok[:pt, li, :])], [tok("g_tok"), tC], [tp])
                  if not smp:
                      mm(ps[:, 16:24], [(ones_f[:pt, :], g_tok[:pt, li, :])], [tok("g_tok"), tMisc], [tp])
                      act(GC[:, li, :], ps[:, 16:24], AF.Exp, [tp], [tok("GC")])
                  act(gam_tok[:pt, li, :], ps[:pt, 0:8], AF.Exp, [tp], [tok("gam")])
                  act(egl[:pt, li, :], ps[:pt, 8:16], AF.Exp, [tp], [tok("egl")])
                  stt(nbG[:pt, li, :], gam_tok[:pt, li, :], -1.0, beta_tok[:pt, li, :], ALU.mult, ALU.mult,
                      [tok("gam"), tok("beta_tok")], [tok("nbG")])
                  if smp:
                      tt(G2[:], g_tok[:64, li, :].unsqueeze(1).to_broadcast([64, 16, 8]),
                         seqsel_sb[:].unsqueeze(2).to_broadcast([64, 16, 8]), ALU.mult, [tok("g_tok"), tC], [tok("G2")])
                      ps2, tp2 = bank()
                      mm(ps2[:, 0:128], [(ones_f[:64, :], G2[:].rearrange("p s h -> p (s h)"))], [tok("G2"), tMisc], [tp2])
                      act(GCs[:].rearrange("p s h -> p (s h)"), ps2[:, 0:128], AF.Exp, [tp2], [tok("GCs")])

              chk(5.1)
              csq = scr[0][:, 0:MAXTOK]
              csk = scr[1][:, 0:MAXTOK]
              for h in range(8):
                  ws, tw, dw = slot()
                  wsb = ws[:].bitcast(BF16)
                  for ci, cbase in enumerate((3072, 4096, 5120, 6144)):
                      load_w(wsb[:, :, ci * 128:(ci + 1) * 128], w_in[l], cbase + h * 128, 128, tw, dw)
                  if has_s:
                      S.dma("pool", S0b[:], sssm[l, :, h].rearrange("s d v -> d s v"), ds_for("S0b"), W=[tok("S0b")])
                  for ci in range(3):
                      for jt in range(4):
                          ts(diag[:, ci * 4 + jt, :], ident_bf[:], cw_sb[:, l, ci * 8 + h, jt:jt + 1], None, ALU.mult, None,
                             [tC], [tok("diag")])
                  ssps = {}
                  for ci in range(3):
                      ctg = ci * 8 + h
                      dst = (csq, csk, vT)[ci]
                      tdst = (tok("scr0"), tok("scr1"), tok("vT"))[ci]
                      for (b0, bn, lis) in blocks:
                          smpb = (bn == 64)
                          pp, tpp = bank()
                          mm(pp[:, 0:bn], [(wsb[:, k, ci * 128:(ci + 1) * 128], hT[:, k, b0:b0 + bn]) for k in range(8)],
                             [tw] + [tH[i] for i in lis], [tpp])
                          pc, tpc = bank()
                          if not smpb:
                              cp(xpre[:, 0:3], halo[:, ctg, :], [tok("halo")], [tok("xpre")])
                              cp(xpre[:, 3:3 + bn], pp[:, 0:bn], [tpp], [tok("xpre")], eng="act")
                              cp(halo[:, ctg, :], xpre[:, bn:bn + 3], [tok("xpre")], [tok("halo")])
                              mm(pc[:, 0:bn], [(diag[:, ci * 4 + jt, :], xpre[:, jt:jt + bn]) for jt in range(4)],
                                 [tok("diag"), tok("xpre")], [tpc])
                          else:
                              cp(xps[:, :, 0:3], scv_bf[:, ctg, :, :], [tok("scv")], [tok("xps")])
                              cp(xps[:, :, 3:7], pp[:, 0:64].rearrange("p (s t) -> p s t", t=4), [tpp], [tok("xps")],
                                 eng="act")
                              mm(pc[:, 0:64].rearrange("p (s t) -> p s t", t=4),
                                 [(diag[:, ci * 4 + jt, :], xps[:, :, jt:jt + 4]) for jt in range(4)],
                                 [tok("diag"), tok("xps")], [tpc])
                          act(dst[:, b0:b0 + bn], pc[:, 0:bn], AF.Silu, [tpc], [tdst])
                          if ci < 2:
                              act(sqb[:, b0:b0 + bn], dst[:, b0:b0 + bn], AF.Square, [tdst], [tok("sqb")])
                              pss, tpss = bank()
                              mm(pss[:, 0:bn], [(ones_bf[:], sqb[:, b0:b0 + bn])], [tok("sqb"), tMisc], [tpss])
                              rsq_t = tok(f"rn{ci}")
                              rnb = (rn, rn2)[ci]
                              ssps[(ci, b0)] = (pss, tpss)
                              cp(rnb[:, b0:b0 + bn], pss[:, 0:bn], [tpss], [rsq_t])
                  chk(5.3)
                  for (b0, bn, lis) in blocks:
                      pz, tpz = bank()
                      mm(pz[:, 0:bn], [(wsb[:, k, 384:512], hT[:, k, b0:b0 + bn]) for k in range(8)],
                         [tw] + [tH[i] for i in lis], [tpz])
                      act(zbs[:, b0:b0 + bn], pz[:, 0:bn], AF.Silu, [tpz], [tok("zbs")])
                  for ci in range(2):
                      rnb = (rn, rn2)[ci]
                      rsq_t = tok(f"rn{ci}")
                      ntk = 512 + (64 if has_s else 0)
                      rsqrt_small(rnb[:, 0:ntk], rnb[:, 0:ntk], eps_nm, 1.0, [rsq_t], [rsq_t])
                      for (b0, bn, lis) in blocks:
                          if ci == 0:
                              stt(qT[:, b0:b0 + bn], csq[:, b0:b0 + bn], 128.0 ** -0.5, rnb[:, b0:b0 + bn], ALU.mult, ALU.mult,
                                  [tok("scr0"), rsq_t], [tok("qT")])
                          else:
                              tt(kT[:, b0:b0 + bn], csk[:, b0:b0 + bn], rnb[:, b0:b0 + bn], ALU.mult, [tok("scr1"), rsq_t],
                                 [tok("kT")])
                              pb_, tpb_ = bank()
                              mm(pb_[:, 0:bn], [(sel_bf[:, h, :], betaT[:, b0:b0 + bn])], [tok("betaT"), tC], [tpb_])
                              tt(kbT[:, b0:b0 + bn], kT[:, b0:b0 + bn], pb_[:, 0:bn], ALU.mult, [tok("kT"), tpb_],
                                 [tok("kbT")])
                  for (li, t, c0, pt, smp) in tinfo:
                      ps, tp = bank()
                      psb = ps[:].bitcast(BF16)
                      transpose(psb[:pt, 0:128], vT[:, c0:c0 + pt], ident_bf[:], [tok("vT"), tC], [tp])
                      transpose(psb[:pt, 128:256], kT[:, c0:c0 + pt], ident_bf[:], [tok("kT"), tC], [tp])
                      ts(Vb[:pt, li, :], psb[:pt, 0:128], beta_tok[:pt, li, h:h + 1], None, ALU.mult, None,
                         [tp, tok("beta_tok")], [tok("Vb")])
                      ts(Kd[:pt, li, :], psb[:pt, 128:256], egl[:pt, li, h:h + 1], None, ALU.mult, None,
                         [tp, tok("egl")], [tok("Kd")])

                  chk(5.4)
                  def tile_gen(li, t, c0, pt, smp, si, done):
                      (Gm, eET, eE, Grow, DTsn, DTi, Dsn, qgT, Mb, Nb, Xb, PTb, Rb, Vnb, KSTb, sqo, rro, t1o) = tsets[si]
                      mk = MASKS[smp]
                      yield
                      ts(Gm[:pt, :pt], mk["Lincl"][:pt, :pt], g_tok[:pt, li, h:h + 1], None, ALU.mult, None,
                         [tC, tok("g_tok")], [tok("Gm_%d" % si)])
                      pe_, tpe = bank()
                      yield
                      mm(pe_[:pt, 0:pt], [(mk["Ustr"][:pt, :pt], Gm[:pt, :pt])], [tC, tok("Gm_%d" % si)], [tpe])
                      yield
                      mm(pe_[:pt, 128:128 + pt], [(Gm[:pt, :pt], mk["Ustr"][:pt, :pt])], [tC, tok("Gm_%d" % si)], [tpe])
                      yield
                      mm(pe_[:, 256:256 + pt], [(ones_f[:pt, :], Gm[:pt, :pt])], [tMisc, tok("Gm_%d" % si)], [tpe])
                      yield
                      act(eET[:pt, :pt], pe_[:pt, 0:pt], AF.Exp, [tpe], [tok("eET_%d" % si)])
                      yield
                      act(eE[:pt, :pt], pe_[:pt, 128:128 + pt], AF.Exp, [tpe], [tok("eE_%d" % si)])
                      yield
                      act(Grow[:, :pt], pe_[:, 256:256 + pt], AF.Exp, [tpe], [tok("Grow_%d" % si)])
                      yield
                      tt(DTsn[:pt, :pt], eET[:pt, :pt], mk["mSTn"][:pt, :pt], ALU.mult, [tok("eET_%d" % si), tC], [tok("DTsn_%d" % si)])
                      yield
                      tt(DTi[:pt, :pt], eET[:pt, :pt], mk["mIT"][:pt, :pt], ALU.mult, [tok("eET_%d" % si), tC], [tok("DTi_%d" % si)])
                      yield
                      tt(Dsn[:pt, :pt], eE[:pt, :pt], mk["mSNn"][:pt, :pt], ALU.mult, [tok("eE_%d" % si), tC], [tok("Dsn_%d" % si)])
                      yield
                      tt(qgT[:, :pt], qT[:, c0:c0 + pt], Grow[:, :pt], ALU.mult, [tok("qT"), tok("Grow_%d" % si)], [tok("qgT_%d" % si)])
                      pa_, tpa_ = bank()
                      yield
                      mm(pa_[:pt, 0:pt], [(kT[:, c0:c0 + pt], kbT[:, c0:c0 + pt])], [tok("kT"), tok("kbT")], [tpa_])
                      yield
                      mm(pa_[:pt, 128:128 + pt], [(kbT[:, c0:c0 + pt], kT[:, c0:c0 + pt])], [tok("kT"), tok("kbT")], [tpa_])
                      yield
                      mm(pa_[:pt, 256:256 + pt], [(kT[:, c0:c0 + pt], qT[:, c0:c0 + pt])], [tok("kT"), tok("qT")], [tpa_])
                      tM = [tok("M0_%d" % si), tok("M1_%d" % si)]
                      tN = [tok("N0_%d" % si), tok("N1_%d" % si)]
                      tXx = [tok("X0_%d" % si), tok("X1_%d" % si)]
                      yield
                      tt(Mb[0][:pt, :pt], pa_[:pt, 0:pt], DTsn[:pt, :pt], ALU.mult, [tpa_, tok("DTsn_%d" % si)], [tM[0]])
                      yield
                      tt(Nb[0][:pt, :pt], pa_[:pt, 128:128 + pt], Dsn[:pt, :pt], ALU.mult, [tpa_, tok("Dsn_%d" % si)], [tN[0]])
                      yield
                      tt(PTb[:pt, :pt], pa_[:pt, 256:256 + pt], DTi[:pt, :pt], ALU.mult, [tpa_, tok("DTi_%d" % si)], [tok("PT_%d" % si)])
                      yield
                      tt(Xb[0][:pt, :pt], Mb[0][:pt, :pt], ident_bf[:pt, :pt], ALU.add, [tM[0], tC], [tXx[0]])
                      cur = 0
                      nlev = 1 if smp else 6
                      for lv in range(nlev):
                          lastlv = (lv == nlev - 1)
                          nx = 1 - cur
                          pn, tpn = bank()
                          yield
                          mm(pn[:pt, 0:pt], [(Mb[cur][:pt, :pt], Nb[cur][:pt, :pt])], [tM[cur], tN[cur]], [tpn])
                          if not lastlv:
                              yield
                              mm(pn[:pt, 128:128 + pt], [(Nb[cur][:pt, :pt], Mb[cur][:pt, :pt])], [tM[cur], tN[cur]], [tpn])
                          yield
                          cp(Nb[nx][:pt, :pt], pn[:pt, 0:pt], [tpn], [tN[nx]], eng="act")
                          if not lastlv:
                              yield
                              cp(Mb[nx][:pt, :pt], pn[:pt, 128:128 + pt], [tpn], [tM[nx]], eng="act")
                          px, tpx = bank()
                          yield
                          mm(px[:pt, 0:pt], [(ident_bf[:pt, :pt], Xb[cur][:pt, :pt]), (Nb[nx][:pt, :pt], Xb[cur][:pt, :pt])],
                             [tC, tXx[cur], tN[nx]], [tpx])
                          yield
                          cp(Xb[nx][:pt, :pt], px[:pt, 0:pt], [tpx], [tXx[nx]], eng="act")
                          cur = nx
                      Xf, tXf = Xb[cur], tXx[cur]
                      while li > 0 and not done.get(li - 1):
                          yield
                      pk, tpk = bank()
                      if not smp:
                          yield
                          mm(pk[:pt, 0:128], [(kT[:, c0:c0 + pt], S_bf[:, h, :])], [tok("kT"), tok("Sbf")], [tpk])
                          yield
                          stt(Rb[:pt, :], pk[:pt, 0:128], nbG[:pt, li, h:h + 1], Vb[:pt, li, :], ALU.mult, ALU.add,
                              [tpk, tok("nbG"), tok("Vb")], [tok("R_%d" % si)])
                      else:
                          for s in range(NSEQ):
                              yield
                              mm(pk[:, 4 * s:4 * s + 4], [(S0b[:, s, :], kT[:, c0 + 4 * s:c0 + 4 * s + 4])],
                                 [tok("kT"), tok("S0b")], [tpk])
                          yield
                          cp(KSTb[:, :], pk[:, 0:64], [tpk], [tok("KST_%d" % si)], eng="act")
                          pk2, tpk2 = bank()
                          pk2b = pk2[:].bitcast(BF16)
                          yield
                          transpose(pk2b[:64, 0:128], KSTb[:, :], ident_bf[:], [tok("KST_%d" % si), tC], [tpk2])
                          yield
                          stt(Rb[:pt, :], pk2b[:64, 0:128], nbG[:pt, li, h:h + 1], Vb[:pt, li, :], ALU.mult, ALU.add,
                              [tpk2, tok("nbG"), tok("Vb")], [tok("R_%d" % si)])
                      pv, tpv = bank()
                      yield
                      mm(pv[:pt, 0:128], [(Xf[:pt, :pt], Rb[:pt, :])], [tXf, tok("R_%d" % si)], [tpv])
                      yield
                      cp(Vnb[:pt, :], pv[:pt, 0:128], [tpv], [tok("Vn_%d" % si)], eng="act")
                      po, tpo = bank()
                      if not smp:
                          yield
                          mm(po[:, 0:pt], [(S_bf[:, h, :], qgT[:, :pt]), (Vnb[:pt, :], PTb[:pt, :pt])],
                             [tok("Sbf"), tok("qgT_%d" % si), tok("Vn_%d" % si), tok("PT_%d" % si)], [tpo])
                          pd, tpd = bank()
                          yield
                          mm(pd[:, 0:128], [(Kd[:pt, li, :], Vnb[:pt, :])], [tok("Kd"), tok("Vn_%d" % si)], [tpd])
                          yield
                          stt(S32[:, h, :], S32[:, h, :], GC[:, li, h:h + 1], pd[:, 0:128], ALU.mult, ALU.add,
                              [tok("S32"), tok("GC"), tpd], [tok("S32")])
                          yield
                          cp(S_bf[:, h, :], S32[:, h, :], [tok("S32")], [tok("Sbf")], eng="act")
                          if t == 15:
                              yield
                              out_dma(o_ssmp[l, h], S32[:, h, :], [tok("S32")])
                      else:
                          yield
                          mm(po[:, 0:64], [(Vnb[:64, :], PTb[:64, :64])], [tok("Vn_%d" % si), tok("PT_%d" % si)], [tpo], first=True, last=False)
                          for s in range(NSEQ):
                              yield
                              mm(po[:, 4 * s:4 * s + 4], [(S0b[:, s, :], qgT[:, 4 * s:4 * s + 4])], [tok("S0b"), tok("qgT_%d" % si)],
                                 [tpo], first=False, last=(s == NSEQ - 1))
                          for sh in range(2):
                              yield
                              S.dma("sp", S0f[:], sssm[l, 8 * sh:8 * sh + 8, h].rearrange("s d v -> d s v"), ds_for("S0f"),
                                    W=[tok("S0f")])
                              for s4 in range(2):
                                  pd, tpd = bank()
                                  for sq_ in range(4):
                                      s = 8 * sh + 4 * s4 + sq_
                                      yield
                                      ts(Kdm[:, :], Kd[:64, li, :], seqsel_sb[:, s:s + 1], None, ALU.mult, None,
                                         [tok("Kd"), tC], [tok("Kdm")])
                                      yield
                                      mm(pd[:, sq_ * 128:(sq_ + 1) * 128], [(Kdm[:, :], Vnb[:64, :])], [tok("Kdm"), tok("Vn_%d" % si)],
                                         [tpd])
                                  for sq_ in range(4):
                                      s = 8 * sh + 4 * s4 + sq_
                                      sl = 4 * s4 + sq_
                                      yield
                                      stt(S0f[:, sl, :], S0f[:, sl, :], GCs[:, s, h:h + 1], pd[:, sq_ * 128:(sq_ + 1) * 128],
                                          ALU.mult, ALU.add, [tok("S0f"), tok("GCs"), tpd], [tok("S0f")])
                              yield
                              out_dma(o_ssms[l, 8 * sh:8 * sh + 8, h].rearrange("s d v -> d s v"), S0f[:], [tok("S0f")])
                      yield
                      act(sqo[:, :pt], po[:, 0:pt], AF.Square, [tpo], [tok("sqo_%d" % si)])
                      pr, tpr = bank()
                      yield
                      mm(pr[:, 0:pt], [(ones_bf[:], sqo[:, :pt])], [tok("sqo_%d" % si), tMisc], [tpr])
                      yield
                      rsqrt_small(rro[:, :pt], pr[:, 0:pt], eps_nm, 1.0 / 128.0, [tpr], [tok("rro_%d" % si)])
                      yield
                      tt(t1o[:, :pt], po[:, 0:pt], rro[:, :pt], ALU.mult, [tpo, tok("rro_%d" % si)], [tok("t1o_%d" % si)])
                      yield
                      stt(bufQ[:, h, c0:c0 + pt], t1o[:, :pt], ogc_sb[:, l:l + 1], zbs[:, c0:c0 + pt], ALU.mult, ALU.mult,
                          [tok("t1o_%d" % si), tC, tok("zbs")], [tQ[h]])

                      done[li] = True
                  done = {}
                  pending = list(tinfo)
                  active = []
                  free_sets = list(range(NSETS))
                  while pending or active:
                      while pending and free_sets:
                          ti_ = pending.pop(0)
                          si_ = free_sets.pop(0)
                          active.append((tile_gen(*ti_, si_, done), si_))
                      for (g_, si_) in list(active):
                          try:
                              next(g_)
                          except StopIteration:
                              active.remove((g_, si_))
                              free_sets.append(si_)
              chk(6)
              if has_s or (15 in tiles):
                  for grp6 in range(6):
                      ws, tw, dw = slot()
                      wsb = ws[:].bitcast(BF16)
                      load_w(wsb, w_in[l], 3072 + grp6 * 512, 512, tw, dw)
                      if has_s:
                          ps, tp = bank()
                          mm(ps[:64, :], [(hT[:, k, 512:576], wsb[:, k, :]) for k in range(8)], [tw, tH[4]], [tp])
                          cp(ev32[0][:64, :], ps[:64, :], [tp], [tok("ev0")], eng="act")
                          for tq in range(1, 4):
                              out_dma(o_convs[l, :, tq - 1, grp6 * 512:(grp6 + 1) * 512], ev32[0][tq:64:4, :], [tok("ev0")])
                      if 15 in tiles:
                          l15 = tiles.index(15)
                          ps2, tp2 = bank()
                          mm(ps2[:, :], [(hT[:, k, l15 * 128:(l15 + 1) * 128], wsb[:, k, :]) for k in range(8)], [tw, tH[l15]], [tp2])
                          cp(ev32[1][64:128, :], ps2[64:128, :], [tp2], [tok("ev1")], eng="act")
                          out_dma(o_convp[l, :, grp6 * 512:(grp6 + 1) * 512], ev32[1][125:128, :], [tok("ev1")])

              chk(7)
              for j in range(8):
                  ws, tw, dw = slot()
                  wsb = ws[:].bitcast(BF16)
                  load_w(wsb[:, :, 0:128], w_pb[l], j * 128, 128, tw, dw)
                  load_w(wsb[:, :, 128:256], w_in[l], 8208 + j * 128, 128, tw, dw)
                  for (b0, bn, lis) in blocks:
                      pa, tpa = bank()
                      mm(pa[:, 0:bn], [(wsb[:, k, 0:128], bufQ[:, k, b0:b0 + bn]) for k in range(8)], [tw] + tQ, [tpa])
                      pg, tpg = bank()
                      mm(pg[:, 0:bn], [(wsb[:, k, 128:256], hT[:, k, b0:b0 + bn]) for k in range(8)],
                         [tw] + [tH[i] for i in lis], [tpg])
                      act(ev32[0][:, 0:bn], pg[:, 0:bn], AF.Sigmoid, [tpg], [tok("ev0")])
                      tt(ev32[0][:, 0:bn], ev32[0][:, 0:bn], pa[:, 0:bn], ALU.mult, [tok("ev0"), tpa], [tok("ev0")])
                      tt(mTv[:, j, b0:b0 + bn], ev32[0][:, 0:bn], mTv[:, j, b0:b0 + bn], ALU.add, [tok("ev0"), tPj[j]],
                         [tPj[j]])

              chk(8)
              S.dma("sp", rowA[:], ln_g[l:l + 1, :].partition_broadcast(128), ds_row[0], W=[tok("rowA")])
              S.dma("sp", rowB[:], ln_b[l:l + 1, :].partition_broadcast(128), ds_row[1], W=[tok("rowB")])
              wo = []
              for hf in range(2):
                  ws, tw, dw = slot()
                  wsb = ws[:].bitcast(BF16)
                  load_w(wsb, w_o[l], hf * 512, 512, tw, dw)
                  wo.append((wsb, tw))
              for (li, t, c0, pt, smp) in tinfo:
                  gt = gate_s if smp else gate_p
                  tg = tok("gate_s") if smp else tok("gate_p")
                  for hf in range(2):
                      ps, tp = bank()
                      mm(ps[:pt, :], [(mTv[:, k, c0:c0 + pt], wo[hf][0][:, k, :]) for k in range(8)], tPj + [wo[hf][1]], [tp])
                      tt(scr[0][:pt, hf * 512:(hf + 1) * 512], ps[:pt, :], gt[:pt, hf * 512:(hf + 1) * 512], ALU.mult,
                         [tp, tg], [tok("scr0")])
                  stt(scr[0][:pt, :], x_sb[:pt, li, :], float(ALPHA), scr[0][:pt, :], ALU.mult, ALU.add, [tX[li], tok("scr0")],
                      [tok("scr0")])
                  layer_norm_stats(scr[0][:pt, :], pt, [tok("scr0")])
                  act(scr[1][:pt, :], scr[0][:pt, :], AF.Identity, [tok("scr0"), tok("rstd"), tok("nbias")], [tok("scr1")],
                      bias=nbias[:pt, :], scale=rstd[:pt, :])
                  tt(scr[1][:pt, :], scr[1][:pt, :], rowA[:pt, :], ALU.mult, [tok("scr1"), tok("rowA")], [tok("scr1")])
                  tt(x_sb[:pt, li, :], scr[1][:pt, :], rowB[:pt, :], ALU.add, [tok("scr1"), tok("rowB")], [tX[li]])
                  if last_layer:
                      if smp:
                          out_dma(y_s, x_sb[:64, li, :], [tX[li]])
                      else:
                          out_dma(y_p[t * 128:(t + 1) * 128, :], x_sb[:, li, :], [tX[li]])

    try:
        main_loop()
    except _Stop:
        pass
    for i_ in range(int(os.environ.get("KDMA", "0"))):
        S.dma("pool", wba[:], w_in[0].rearrange("(k p) c -> p k c", p=128)[:, :, 16 * (i_ % 500):16 * (i_ % 500) + 16], ds_ba,
              W=[tok("wba")])
    for i_ in range(int(os.environ.get("KDMAH", "0"))):
        S.dma("sp", rowA[:], xin[i_:i_ + 1, :].partition_broadcast(128), ds_row[0], W=[tok("rowA")])
    for i_ in range(int(os.environ.get("KACT", "0"))):
        nc.scalar.activation(out=ev32[0][0:1, 0:1], in_=one_c[0:1, 0:1], func=AF.Gelu_apprx_tanh)
        nc.scalar.activation(out=ev32[0][0:1, 0:1], in_=one_c[0:1, 0:1], func=AF.Silu)
    if os.environ.get("KINC"):
        semx = nc.alloc_semaphore("dummy_inc")
        for i_ in range(int(os.environ["KINC"])):
            nc.vector.memset(ev32[1][0:1, 0:1], 0.0).then_inc(semx, 1)
    if os.environ.get("KDUMMY"):
        for e_ in ("pe", "act", "dve"):
            for b in tPS:
                S._wait(e_, b.w)
                for ev in b.r.values():
                    S._wait(e_, ev)
        for _ in range(20000):
            nc.tensor.matmul(psum[0][0:1, 0:1], ones_row[0:1, 0:1], ones_row[0:1, 0:1], start=True, stop=True)
        for _ in range(12000):
            nc.scalar.copy(out=ev32[0][0:1, 0:1], in_=one_c[0:1, 0:1])
        for _ in range(12000):
            nc.vector.memset(ev32[1][0:1, 0:1], 0.0)
    for (sm, c) in S.final_waits:
        nc.sync.wait_ge(sm, c)
    for d in S.dsems:
        if d.count:
            nc.sync.wait_ge(d.sem, d.count)
    print("sbuf bytes remaining:", nc.sbuf_bytes_remaining, "instr counts:", S.total, "nsem:", S.nsem)
    return nc


def _consts():
    c = np.zeros((128, 12, 128), np.float32)
    i = np.arange(128)
    a, b = i[:, None], i[None, :]
    c[:, 0] = (a == b)
    c[:, 1] = (a <= b)
    c[:, 2] = (a > b)
    c[:, 3] = -1.0 * (b > a)
    c[:, 4] = (b >= a)
    c[:, 5] = -1.0 * (b < a)
    same = ((a // 4) == (b // 4)) & (a < 64) & (b < 64)
    c[:, 6] = same & (a <= b)
    c[:, 7] = same & (a > b)
    c[:, 8] = -1.0 * (same & (b > a))
    c[:, 9] = same & (b >= a)
    c[:, 10] = -1.0 * (same & (b < a))
    sel = np.zeros((16, 8, 128), np.float32)
    for h in range(8):
        sel[h, h, :] = 1.0
    seqsel = (np.arange(64)[:, None] // 4 == np.arange(16)[None, :]).astype(np.float32)
    return c, sel, seqsel


def kernel(x_prompt, x_sample, state_conv, state_ssm, c_prompt, c_sample, w_ada, b_ada, w_in,
           w_s, b_s, lnv_g, lnv_b, conv_w, a_log, dt_bias, onorm_g, w_pa, w_pb, w_o, ln_g, ln_b):
    f = lambda a: np.ascontiguousarray(np.asarray(a, dtype=np.float32))
    (x_prompt, x_sample, state_conv, state_ssm, c_prompt, c_sample, w_ada, b_ada, w_in, w_s, b_s, lnv_g, lnv_b,
     conv_w, a_log, dt_bias, onorm_g, w_pa, w_pb, w_o, ln_g, ln_b) = [f(a) for a in (
        x_prompt, x_sample, state_conv, state_ssm, c_prompt, c_sample, w_ada, b_ada, w_in, w_s, b_s, lnv_g, lnv_b,
        conv_w, a_log, dt_bias, onorm_g, w_pa, w_pb, w_o, ln_g, ln_b)]
    depth = w_in.shape[0]
    nc = build_nc(depth)
    consts, sel, seqsel = _consts()
    b_adaT = f(b_ada[:, :2048].reshape(depth, 16, 128).transpose(2, 0, 1))
    b_gate = f(b_ada[:, 2048:])
    w_sT = f(w_s.transpose(0, 3, 1, 2))
    w_sTs = np.zeros((depth, 64, 8, 64), np.float32)
    for q in range(16):
        w_sTs[:, 4 * q:4 * q + 4, :, 4 * q:4 * q + 4] = w_s[:, :, :4, :4].transpose(0, 3, 1, 2)
    bsr = f(b_s.reshape(1, -1))
    bsrs = f(np.tile(b_s[:, :, :4], (1, 1, 16)).reshape(1, -1))
    cw = f(conv_w.reshape(depth, 4, 24, 128).transpose(3, 0, 2, 1))
    alog = f(a_log.reshape(1, -1))
    dtb = f(dt_bias.reshape(1, -1))
    ogc = f(onorm_g.T)
    shared = dict(w_ada=w_ada, b_adaT=b_adaT, b_gate=b_gate, w_in=w_in, w_sT=w_sT, w_sTs=w_sTs, bsr=bsr, bsrs=bsrs,
                  lnv_g=lnv_g, lnv_b=lnv_b, cw=cw, alog=alog, dtb=dtb, ogc=ogc, w_pa=w_pa, w_pb=w_pb, w_o=w_o,
                  ln_g=ln_g, ln_b=ln_b, consts=consts, selc=sel, seqsel=seqsel)
    in_maps = []
    for i in range(NCORE):
        ss = slice(NSEQ * i, NSEQ * (i + 1))
        xin = f(np.concatenate([x_prompt[i], x_sample[ss].reshape(64, D)], axis=0))
        cc = np.concatenate([c_prompt[i:i + 1], c_sample[ss]], axis=0)
        cT = f(cc.reshape(17, 8, 128).transpose(2, 1, 0))
        sconv = f(state_conv[:, ss].reshape(depth, NSEQ, 3, 24, 128).transpose(0, 4, 3, 1, 2))
        sssm = f(state_ssm[:, ss])
        m = dict(shared)
        m.update(xin=xin, cT=cT, sconv=sconv, sssm=sssm)
        in_maps.append(m)
    import os
    if os.environ.get('KTRACE'):
        res = run_bass_kernel_spmd(nc, in_maps, core_ids=list(range(NCORE)), trace=True)
        print('EXEC_TIME_NS', res.exec_time_ns)
    else:
        res = run_bass_kernel_spmd(nc, in_maps, core_ids=list(range(NCORE)))
    R = res.results
    y_prompt = np.stack([R[i]["y_p"] for i in range(NCORE)], 0)
    y_sample = np.concatenate([R[i]["y_s"].reshape(NSEQ, 4, D) for i in range(NCORE)], 0)
    conv_p = np.stack([R[i]["o_convp"] for i in range(NCORE)], 1)
    ssm_p = np.stack([R[i]["o_ssmp"] for i in range(NCORE)], 1)
    cv_p = np.stack([R[i]["o_cvp"] for i in range(NCORE)], 1)
    conv_s = np.concatenate([R[i]["o_convs"] for i in range(NCORE)], 1)
    ssm_s = np.concatenate([R[i]["o_ssms"] for i in range(NCORE)], 1)
    cv_s = np.concatenate([R[i]["o_cvs"].reshape(depth, NSEQ, 4, D) for i in range(NCORE)], 1)
    return tuple(np.ascontiguousarray(a, dtype=np.float32) for a in
                 (y_prompt, y_sample, conv_p, ssm_p, cv_p, conv_s, ssm_s, cv_s))
```

```python
import numpy as np
import concourse.bass as bass
import concourse.mybir as mybir
from concourse.bass_utils import run_bass_kernel_spmd

F32, BF16 = mybir.dt.float32, mybir.dt.bfloat16
AF = mybir.ActivationFunctionType
ALU = mybir.AluOpType

D = 1024
DEPTH = 4
NCORE = 8
SEQ = 2048
NPT = 16
NSEQ = 16
P_IN = 9232
ALPHA = (2 * DEPTH) ** 0.25
LN_EPS = 1e-5
NORM_EPS = 1e-6
GROUPS = [[0, 1, 2, 3, 16], [4, 5, 6, 7], [8, 9, 10, 11], [12, 13, 14, 15]]
MAXTOK = 576


class Buf:
    __slots__ = ("name", "w", "r")

    def __init__(self, name):
        self.name = name
        self.w = None
        self.r = {}


class DmaSem:
    def __init__(self, sem, name):
        self.sem = sem
        self.count = 0
        self.name = name


class Sched:
    import os as _os
    EPOCH = int(_os.environ.get('KEPOCH', '8000'))

    def __init__(self, nc):
        self.nc = nc
        self.eng = {"pe": nc.tensor, "act": nc.scalar, "dve": nc.vector, "pool": nc.gpsimd, "sp": nc.sync}
        self.sem = {k: nc.alloc_semaphore("sem_" + k) for k in self.eng}
        self.epoch = {k: 0 for k in self.eng}
        self.cnt = {k: 0 for k in self.eng}
        self.total = {k: 0 for k in self.eng}
        self.known = {k: {} for k in self.eng}
        self.dsems = []
        self.final_waits = []
        self.nsem = len(self.eng)

    def dsem(self, name):
        d = DmaSem(self.nc.alloc_semaphore("ds_" + name), name)
        d.gen = 0
        self.dsems.append(d)
        self.nsem += 1
        return d

    def _wait(self, eng, ev):
        if ev is None:
            return
        if ev[0] == "e":
            _, src, ep, idx, sem = ev
            if src == eng and eng in ("pe", "sp"):
                return
            key = ("e", src)
        else:
            _, dname, ep, idx, sem = ev
            key = ("d", dname)
        kep, kval = self.known[eng].get(key, (-1, 0))
        if kep > ep or (kep == ep and kval >= idx):
            return
        self.known[eng][key] = (ep, idx)
        self.eng[eng].wait_ge(sem, idx)

    def _deps(self, eng, R, W):
        for b in R:
            self._wait(eng, b.w)
        for b in W:
            self._wait(eng, b.w)
            for ev in b.r.values():
                if ev[0] == "e" and ev[1] == eng and eng == "pe":
                    continue
                self._wait(eng, ev)

    def ops(self, eng, fns, R=(), W=()):
        self._deps(eng, R, W)
        if self.cnt[eng] >= self.EPOCH:
            self.sem[eng] = self.nc.alloc_semaphore(f"sem_{eng}_{self.epoch[eng] + 1}")
            self.epoch[eng] += 1
            self.cnt[eng] = 0
            self.nsem += 1
        ins = None
        for fn in fns:
            ins = fn()
        self.cnt[eng] += 1
        self.total[eng] += 1
        ins.then_inc(self.sem[eng], 1)
        ev = ("e", eng, self.epoch[eng], self.cnt[eng], self.sem[eng])
        for b in W:
            b.w = ev
            b.r = {}
        for b in R:
            b.r[eng] = ev
        return ins

    def op(self, eng, fn, R=(), W=()):
        return self.ops(eng, [fn], R, W)

    def dma(self, q, out, in_, ds, R=(), W=()):
        self._deps(q, R, W)
        if ds.count + 16 > self.EPOCH:
            self.final_waits.append((ds.sem, ds.count))
            ds.gen += 1
            ds.sem = self.nc.alloc_semaphore(f"ds_{ds.name}_{ds.gen}")
            ds.count = 0
            self.nsem += 1
        ins = self.eng[q].dma_start(out=out, in_=in_)
        ds.count += 16
        ins.then_inc(ds.sem, 16)
        ev = ("d", ds.name, ds.gen, ds.count, ds.sem)
        for b in W:
            b.w = ev
            b.r = {}
        for b in R:
            b.r[("d", ds.name)] = ev


class _Stop(Exception):
    pass


def build_nc(depth=DEPTH):
    import os
    KSTOP = float(os.environ.get("KSTOP", "99"))
    KGRP = os.environ.get("KGRP")
    groups = GROUPS if KGRP is None else [GROUPS[int(g_)] for g_ in KGRP.split(',')]

    def chk(n):
        if KSTOP <= n:
            raise _Stop()
    nc = bass.Bass("TRN2", target_bir_lowering=False)
    S = Sched(nc)
    dt_in = lambda name, shape: nc.dram_tensor(name, list(shape), F32, kind="ExternalInput").ap()
    dt_out = lambda name, shape: nc.dram_tensor(name, list(shape), F32, kind="ExternalOutput").ap()

    xin = dt_in("xin", [NPT * 128 + 64, D])
    cT = dt_in("cT", [128, 8, 17])
    sconv = dt_in("sconv", [depth, 128, 24, NSEQ, 3])
    sssm = dt_in("sssm", [depth, NSEQ, 8, 128, 128])
    w_ada = dt_in("w_ada", [depth, D, 3 * D])
    b_adaT = dt_in("b_adaT", [128, depth, 16])
    b_gate = dt_in("b_gate", [depth, D])
    w_in = dt_in("w_in", [depth, D, P_IN])
    w_sT = dt_in("w_sT", [depth, 128, 8, 128])
    w_sTs = dt_in("w_sTs", [depth, 64, 8, 64])
    bsr = dt_in("bsr", [1, depth * 8 * 128])
    bsrs = dt_in("bsrs", [1, depth * 8 * 64])
    lnv_g = dt_in("lnv_g", [depth, D])
    lnv_b = dt_in("lnv_b", [depth, D])
    cw = dt_in("cw", [128, depth, 24, 4])
    alog = dt_in("alog", [1, depth * 8])
    dtb = dt_in("dtb", [1, depth * 8])
    ogc = dt_in("ogc", [128, depth])
    w_pa = dt_in("w_pa", [depth, D, D])
    w_pb = dt_in("w_pb", [depth, D, D])
    w_o = dt_in("w_o", [depth, D, D])
    ln_g = dt_in("ln_g", [depth, D])
    ln_b = dt_in("ln_b", [depth, D])
    consts = dt_in("consts", [128, 12, 128])
    selc = dt_in("selc", [16, 8, 128])
    seqsel = dt_in("seqsel", [64, 16])

    y_p = dt_out("y_p", [NPT * 128, D])
    y_s = dt_out("y_s", [64, D])
    o_convp = dt_out("o_convp", [depth, 3, 3 * D])
    o_ssmp = dt_out("o_ssmp", [depth, 8, 128, 128])
    o_cvp = dt_out("o_cvp", [depth, 128, D])
    o_convs = dt_out("o_convs", [depth, NSEQ, 3, 3 * D])
    o_ssms = dt_out("o_ssms", [depth, NSEQ, 8, 128, 128])
    o_cvs = dt_out("o_cvs", [depth, 64, D])

    sb = lambda name, shape, dt=F32: nc.alloc_sbuf_tensor(name, list(shape), dt)
    x_sb = sb("x_sb", [128, 5, D])
    hT = sb("hT", [128, 8, MAXTOK], BF16)
    bufP = sb("bufP", [128, 5 * D], BF16)
    bufQ = sb("bufQ", [128, 8, MAXTOK], BF16)
    NSLOT = 2
    wslot = [sb(f"wslot{i}", [128, 8, 256]) for i in range(NSLOT)]
    rowA = sb("rowA", [128, D])
    rowB = sb("rowB", [128, D])
    gate_p = sb("gate_p", [128, D])
    gate_s = sb("gate_s", [128, D])
    scr = [sb(f"scr{i}", [128, D]) for i in range(2)]
    xn_bf = sb("xn_bf", [128, D], BF16)
    cst = sb("cst", [128, 12, 128])
    ident_bf = sb("ident_bf", [128, 128], BF16)
    ones_bf = sb("ones_bf", [128, 128], BF16)
    ones_f = sb("ones_f", [128, 128])
    sel_bf = sb("sel_bf", [16, 8, 128], BF16)
    seqsel_sb = sb("seqsel_sb", [64, 16])
    eps_ln = sb("eps_ln", [128, 1])
    eps_nm = sb("eps_nm", [128, 1])
    one_c = sb("one_c", [128, 1])
    scT = sb("scT", [128, 8, 17])
    scT_rep = sb("scT_rep", [128, 8, 128])
    scT_s4 = sb("scT_s4", [128, 8, 64])
    modT_all = sb("modT_all", [128, depth, 16, 17])
    opsT_all = sb("opsT_all", [128, depth, 8, 17])
    badaT = sb("badaT", [128, depth, 16])
    cw_sb = sb("cw_sb", [128, depth, 24, 4])
    ogc_sb = sb("ogc_sb", [128, depth])
    alog_sb = sb("alog_sb", [128, depth * 8])
    dtb_sb = sb("dtb_sb", [128, depth * 8])
    nA_sb = sb("nA_sb", [128, depth * 8])
    wsT_bf = sb("wsT_bf", [128, 8, 128], BF16)
    wsTs_bf = sb("wsTs_bf", [64, 8, 64], BF16)
    bsr_bf = sb("bsr_bf", [1, 8 * 128], BF16)
    bsrs_bf = sb("bsrs_bf", [1, 8 * 64], BF16)
    ones_row = sb("ones_row", [1, 128], BF16)
    scv_bf = sb("scv_bf", [128, 24, NSEQ, 3], BF16)
    halo_all = sb("halo_all", [128, depth, 24, 3], BF16)
    mv = sb("mv", [128, 2])
    st6 = sb("st6", [128, 2, 6])
    rstd = sb("rstd", [128, 1])
    nbias = sb("nbias", [128, 1])
    wba = sb("wba", [128, 8, 16], BF16)
    betaT = sb("betaT", [16, MAXTOK], BF16)
    beta_tok = sb("beta_tok", [128, 5, 8])
    apre = sb("apre", [128, 5, 8])
    g_tok = sb("g_tok", [128, 5, 8])
    gam_tok = sb("gam_tok", [128, 5, 8])
    nbG = sb("nbG", [128, 5, 8])
    egl = sb("egl", [128, 5, 8])
    GC = sb("GC", [128, 5, 8])
    G2 = sb("G2", [64, 16, 8])
    GCs = sb("GCs", [128, 16, 8])
    xpre = sb("xpre", [128, 3 + 512], BF16)
    xps = sb("xps", [128, NSEQ, 7], BF16)
    diag = sb("diag", [128, 12, 128], BF16)
    sqb = sb("sqb", [128, MAXTOK], BF16)
    rn = sb("rn", [128, MAXTOK])
    rn2 = sb("rn2", [128, MAXTOK])
    kT = sb("kT", [128, MAXTOK], BF16)
    kbT = sb("kbT", [128, MAXTOK], BF16)
    qT = sb("qT", [128, MAXTOK], BF16)
    vT = sb("vT", [128, MAXTOK], BF16)
    zbs = sb("zbs", [128, MAXTOK], BF16)
    Vb = sb("Vb", [128, 5, 128], BF16)
    Kd = sb("Kd", [128, 5, 128], BF16)
    S32_all = sb("S32_all", [128, depth, 8, 128])
    S_bf = sb("S_bf", [128, 8, 128], BF16)
    S0f = sb("S0f", [128, 8, 128])
    S0b = sb("S0b", [128, NSEQ, 128], BF16)
    Kdm = sb("Kdm", [64, 128], BF16)
    NSETS = int(os.environ.get("KSETS", "4"))
    tsets = []
    for si in range(NSETS):
        tsets.append((
            sb(f"Gm{si}", [128, 128]), sb(f"eET{si}", [128, 128]), sb(f"eE{si}", [128, 128]), sb(f"Grow{si}", [128, 128]),
            sb(f"DTsn{si}", [128, 128]), sb(f"DTi{si}", [128, 128]), sb(f"Dsn{si}", [128, 128]),
            sb(f"qgT{si}", [128, 128], BF16),
            [sb(f"Mb{si}_{i}", [128, 128], BF16) for i in range(2)],
            [sb(f"Nb{si}_{i}", [128, 128], BF16) for i in range(2)],
            [sb(f"Xb{si}_{i}", [128, 128], BF16) for i in range(2)],
            sb(f"PTb{si}", [128, 128], BF16), sb(f"Rb{si}", [128, 128], BF16), sb(f"Vnb{si}", [128, 128], BF16),
            sb(f"KSTb{si}", [128, 64], BF16), sb(f"sqo{si}", [128, 128], BF16), sb(f"rro{si}", [128, 128]),
            sb(f"t1o{si}", [128, 128])))
    ev32 = [sb(f"ev32_{i}", [128, 512]) for i in range(2)]

    psum = [nc.alloc_psum_tensor(f"ps{i}", [128, 512], F32) for i in range(8)]

    T = {}

    def tok(name):
        if name not in T:
            T[name] = Buf(name)
        return T[name]

    tX = [tok(f"x{i}") for i in range(5)]
    tH = [tok(f"h{i}") for i in range(5)]
    tPt = [tok(f"Pt{i}") for i in range(5)]
    tPj = [tok(f"Pj{i}") for i in range(8)]
    tQ = [tok(f"Q{i}") for i in range(8)]
    tPS = [tok(f"ps{i}") for i in range(8)]
    tW = [tok(f"w{i}") for i in range(NSLOT)]
    dsW = [S.dsem(f"w{i}") for i in range(NSLOT)]
    dsWh = [S.dsem(f"wh{i}") for i in range(NSLOT)]
    state = {"bank": 0, "slot": 0}

    def bank():
        i = state["bank"]
        state["bank"] = (i + 1) % 8
        return psum[i], tPS[i]

    def slot(hw=False):
        i = state["slot"]
        state["slot"] = (i + 1) % NSLOT
        return wslot[i], tW[i], (dsWh[i] if hw else dsW[i])

    ds_c = S.dsem("const")
    ds_c2 = S.dsem("const_sw")
    ds_x = [S.dsem(f"x{i}") for i in range(5)]
    ds_row = [S.dsem("rowA"), S.dsem("rowB")]
    ds_ba = S.dsem("wba")
    _ds_named = {}

    def ds_for(name):
        if name not in _ds_named:
            _ds_named[name] = S.dsem(name)
        return _ds_named[name]

    V, A, PE, PO = nc.vector, nc.scalar, nc.tensor, nc.gpsimd

    def act(out, in_, func, R, W, **kw):
        S.op("act", lambda: A.activation(out=out, in_=in_, func=func, **kw), R, W)

    def tt(out, in0, in1, op, R, W, eng="dve"):
        e = V if eng == "dve" else PO
        S.op(eng, lambda: e.tensor_tensor(out=out, in0=in0, in1=in1, op=op), R, W)

    def ts(out, in0, s1, s2, op0, op1, R, W):
        if op1 is None:
            S.op("dve", lambda: V.tensor_scalar(out=out, in0=in0, scalar1=s1, scalar2=None, op0=op0), R, W)
        else:
            S.op("dve", lambda: V.tensor_scalar(out=out, in0=in0, scalar1=s1, scalar2=s2, op0=op0, op1=op1), R, W)

    def stt(out, in0, scalar, in1, op0, op1, R, W):
        S.op("dve", lambda: V.scalar_tensor_tensor(out=out, in0=in0, scalar=scalar, in1=in1, op0=op0, op1=op1), R, W)

    def cp(out, in_, R, W, eng="dve"):
        if eng == "act":
            S.op("act", lambda: A.copy(out=out, in_=in_), R, W)
        else:
            S.op("dve", lambda: V.tensor_copy(out=out, in_=in_), R, W)

    def mm(out, pairs, R, W, first=True, last=True):
        n = len(pairs)
        fns = []
        for i, (l, r) in enumerate(pairs):
            fns.append(lambda l=l, r=r, i=i: PE.matmul(out, l, r, start=(first and i == 0), stop=(last and i == n - 1),
                                                      skip_group_check=True))
        S.ops("pe", fns, R, W)

    def transpose(out, in_, idn, R, W):
        S.op("pe", lambda: PE.transpose(out, in_, idn), R, W)

    tC = tok("const")
    S.dma("sp", cst[:], consts, ds_c, W=[tC])
    S.dma("pool", ident_bf[:], consts[:, 0, :], ds_c2, W=[tok("c2")])
    S.dma("pool", sel_bf[:], selc, ds_c2, W=[tok("c2")])
    S.dma("sp", seqsel_sb[:], seqsel, ds_c, W=[tC])
    S.dma("sp", scT[:], cT, ds_c, W=[tC])
    S.dma("sp", badaT[:], b_adaT, ds_c, W=[tC])
    S.dma("sp", cw_sb[:], cw, ds_c, W=[tC])
    S.dma("sp", ogc_sb[:], ogc, ds_c, W=[tC])
    S.dma("sp", alog_sb[:], alog.partition_broadcast(128), ds_c, W=[tC])
    S.dma("sp", dtb_sb[:], dtb.partition_broadcast(128), ds_c, W=[tC])
    for e_ in ("pe", "act", "dve", "pool", "sp"):
        S.eng[e_].wait_ge(ds_c.sem, ds_c.count)
        S.eng[e_].wait_ge(ds_c2.sem, ds_c2.count)
    tC.w = None
    MASKS = {
        False: dict(Lincl=cst[:, 1, :], Ustr=cst[:, 2, :], mSTn=cst[:, 3, :], mIT=cst[:, 4, :], mSNn=cst[:, 5, :]),
        True: dict(Lincl=cst[:, 6, :], Ustr=cst[:, 7, :], mSTn=cst[:, 8, :], mIT=cst[:, 9, :], mSNn=cst[:, 10, :]),
    }
    tMisc = tok("misc")
    S.op("dve", lambda: V.memset(ones_bf[:], 1.0), W=[tMisc])
    S.op("dve", lambda: V.memset(ones_f[:], 1.0), W=[tMisc])
    S.op("dve", lambda: V.memset(ones_row[:], 1.0), W=[tMisc])
    S.op("dve", lambda: V.memset(eps_ln[:], LN_EPS), W=[tMisc])
    S.op("dve", lambda: V.memset(eps_nm[:], NORM_EPS), W=[tMisc])
    S.op("dve", lambda: V.memset(one_c[:], 1.0), W=[tMisc])
    S.op("dve", lambda: V.memset(halo_all[:], 0.0), W=[tok("halo")])
    S.op("dve", lambda: V.memset(S32_all[:], 0.0), W=[tok("S32")])
    tSc = tok("scT")
    act(scT[:], scT[:], AF.Silu, [tC], [tSc])
    cp(scT_rep[:], scT[:, :, 0:1].to_broadcast([128, 8, 128]), [tSc], [tok("scTrep")])
    cp(scT_s4[:].rearrange("p k (s t) -> p k s t", t=4), scT[:, :, 1:17].unsqueeze(3).to_broadcast([128, 8, 16, 4]),
       [tSc], [tok("scTs4")])
    act(nA_sb[:], alog_sb[:], AF.Exp, [tC], [tok("nA")])
    ts(nA_sb[:], nA_sb[:], -1.0, None, ALU.mult, None, [tok("nA")], [tok("nA")])

    def rsqrt_small(out, in_, eps_t, scale, R, W, n=128):
        act(out, in_, AF.Ln, R, W, bias=eps_t[:n, :], scale=scale)
        act(out, out, AF.Exp, W, W, scale=-0.5)

    def layer_norm_stats(src, pt, Rt):
        tS = tok("st6")
        S.op("dve", lambda: V.bn_stats(st6[:pt, 0, :], src[:, 0:512]), Rt, [tS])
        S.op("dve", lambda: V.bn_stats(st6[:pt, 1, :], src[:, 512:1024]), Rt + [tS], [tS])
        S.op("dve", lambda: V.bn_aggr(mv[:pt, :], st6[:pt, :, :]), [tS], [tok("mv")])
        rsqrt_small(rstd[:pt, :], mv[:pt, 1:2], eps_ln, 1.0, [tok("mv")], [tok("rstd")], n=pt)
        stt(nbias[:pt, :], mv[:pt, 0:1], -1.0, rstd[:pt, :], ALU.mult, ALU.mult, [tok("mv"), tok("rstd")], [tok("nbias")])

    def load_w(dst3, src_l, c0, ncols, tW_, dsW_, q="pool"):
        S.dma(q, dst3, src_l.rearrange("(k p) c -> p k c", p=128)[:, :, c0:c0 + ncols], dsW_, W=[tW_])

    out_ds_i = [0]

    def out_dma(dst, src, R):
        S.dma("sp", dst, src, ds_for("o_" + R[0].name), R=R)

    for l in range(depth):
        for g in range(8):
            ws, tw, dw = slot(hw=True)
            S.dma("sp", ws[:], w_ada[l].rearrange("(k p) c -> p k c", p=128)[:, :, g * 256:(g + 1) * 256], dw, W=[tw])
            for c2 in range(2):
                ct = g * 2 + c2
                ps, tp = bank()
                mm(ps[:, 0:17], [(ws[:, k, c2 * 128:(c2 + 1) * 128], scT[:, k, :]) for k in range(8)], [tw, tSc], [tp])
                ts(modT_all[:, l, ct, :], ps[:, 0:17], badaT[:, l, ct:ct + 1], None, ALU.add, None, [tp, tC], [tok("modT")])
    ts(opsT_all[:], modT_all[:, :, 8:16, :], 1.0, None, ALU.add, None, [tok("modT")], [tok("opsT")])

    def main_loop():
      chk(0)
      for gi, tiles in enumerate(groups):
          has_s = 16 in tiles
          nt = len(tiles)
          blocks = [(0, 512, list(range(4)))] + ([(512, 64, [4])] if has_s else [])
          tinfo = []
          for li, t in enumerate(tiles):
              tinfo.append((li, t, li * 128, 128 if t < 16 else 64, t == 16))
          for (li, t, c0, pt, smp) in tinfo:
              S.dma("sp", x_sb[:pt, li, :], xin[t * 128:t * 128 + pt, :], ds_x[li], W=[tX[li]])

          for l in range(depth):
              last_layer = (l == depth - 1)
              modT = modT_all[:, l]
              opsT = opsT_all[:, l]
              halo = halo_all[:, l]
              S32 = S32_all[:, l]
              wsT_f = scr[0][:, :].rearrange("p (h t) -> p h t", h=8)
              S.dma("sp", wsT_f, w_sT[l], ds_for("l_wsT"), W=[tok("scr0")])
              tt(wsT_bf[:], wsT_f, MASKS[False]["mIT"].unsqueeze(1).to_broadcast([128, 8, 128]), ALU.mult,
                 [tok("scr0"), tC], [tok("wsT")])
              S.dma("pool", bsr_bf[:], bsr[:, l * 1024:(l + 1) * 1024], ds_for("l_bsr"), W=[tok("bsr")])
              if has_s:
                  wsTs_f = scr[1][:64, 0:512].rearrange("p (h t) -> p h t", h=8)
                  S.dma("sp", wsTs_f, w_sTs[l], ds_for("l_wsTs"), W=[tok("scr1")])
                  tt(wsTs_bf[:], wsTs_f, MASKS[True]["mIT"][:64, :64].unsqueeze(1).to_broadcast([64, 8, 64]), ALU.mult,
                     [tok("scr1"), tC], [tok("wsTs")])
                  S.dma("pool", bsrs_bf[:], bsrs[:, l * 512:(l + 1) * 512], ds_for("l_bsrs"), W=[tok("bsrs")])
                  S.dma("pool", scv_bf[:], sconv[l], ds_for("l_scv"), W=[tok("scv")])
              cp(S_bf[:], S32, [tok("S32")], [tok("Sbf")], eng="act")
              S.dma("sp", rowA[:], b_gate[l:l + 1, :].partition_broadcast(128), ds_row[0], W=[tok("rowA")])
              for g in range(4):
                  ws, tw, dw = slot(hw=True)
                  S.dma("sp", ws[:], w_ada[l].rearrange("(k p) c -> p k c", p=128)[:, :, 2048 + g * 256:2048 + (g + 1) * 256],
                        dw, W=[tw])
                  c0g = g * 256
                  ps, tp = bank()
                  mm(ps[:, 0:256], [(scT_rep[:, k, :], ws[:, k, :]) for k in range(8)], [tw, tok("scTrep")], [tp])
                  tt(gate_p[:, c0g:c0g + 256], ps[:, 0:256], rowA[:, c0g:c0g + 256], ALU.add, [tp, tok("rowA")], [tok("gate_p")])
                  if has_s:
                      ps, tp = bank()
                      mm(ps[:64, 0:256], [(scT_s4[:, k, :], ws[:, k, :]) for k in range(8)], [tw, tok("scTs4")], [tp])
                      tt(gate_s[:64, c0g:c0g + 256], ps[:64, 0:256], rowA[:64, c0g:c0g + 256], ALU.add, [tp, tok("rowA")],
                         [tok("gate_s")])

              chk(1)
              for (li, t, c0, pt, smp) in tinfo:
                  layer_norm_stats(x_sb[:pt, li, :], pt, [tX[li]])
                  act(xn_bf[:pt, :], x_sb[:pt, li, :], AF.Identity, [tX[li], tok("rstd"), tok("nbias")], [tok("xn")],
                      bias=nbias[:pt, :], scale=rstd[:pt, :])
                  ps, tp = bank()
                  psb = ps[:].bitcast(BF16)
                  for k in range(8):
                      transpose(psb[:, k * 128:k * 128 + pt], xn_bf[:pt, k * 128:(k + 1) * 128], ident_bf[:pt, :pt],
                                [tok("xn"), tC], [tp])
                  for k in range(8):
                      if not smp:
                          act(hT[:, k, c0:c0 + pt], psb[:, k * 128:k * 128 + pt], AF.Identity,
                              [tp, tok("opsT"), tok("modT")], [tH[li]], bias=modT[:, k, 0:1], scale=opsT[:, k, 0:1])
                      else:
                          tt(ev32[0][:, 0:64].rearrange("p (s t) -> p s t", t=4),
                             psb[:, k * 128:k * 128 + 64].rearrange("p (s t) -> p s t", t=4),
                             opsT[:, k, 1:17].unsqueeze(2).to_broadcast([128, 16, 4]), ALU.mult,
                             [tp, tok("opsT")], [tok("ev0")])
                          tt(hT[:, k, c0:c0 + 64].rearrange("p (s t) -> p s t", t=4),
                             ev32[0][:, 0:64].rearrange("p (s t) -> p s t", t=4),
                             modT[:, k, 1:17].unsqueeze(2).to_broadcast([128, 16, 4]), ALU.add,
                             [tok("ev0"), tok("modT")], [tH[li]])

              chk(2)
              S.dma("sp", rowA[:], lnv_g[l:l + 1, :].partition_broadcast(128), ds_row[0], W=[tok("rowA")])
              S.dma("sp", rowB[:], lnv_b[l:l + 1, :].partition_broadcast(128), ds_row[1], W=[tok("rowB")])
              wv = []
              for hf in range(2):
                  ws, tw, dw = slot()
                  wsb = ws[:].bitcast(BF16)
                  load_w(wsb, w_in[l], 1024 + hf * 512, 512, tw, dw)
                  wv.append((wsb, tw))
              for (li, t, c0, pt, smp) in tinfo:
                  for hf in range(2):
                      ps, tp = bank()
                      mm(ps[:pt, :], [(hT[:, k, c0:c0 + pt], wv[hf][0][:, k, :]) for k in range(8)],
                         [tH[li], wv[hf][1]], [tp])
                      act(scr[0][:pt, hf * 512:(hf + 1) * 512], ps[:pt, :], AF.Gelu_apprx_tanh, [tp], [tok("scr0")])
                  layer_norm_stats(scr[0][:pt, :], pt, [tok("scr0")])
                  act(scr[1][:pt, :], scr[0][:pt, :], AF.Identity, [tok("scr0"), tok("rstd"), tok("nbias")], [tok("scr1")],
                      bias=nbias[:pt, :], scale=rstd[:pt, :])
                  tt(scr[1][:pt, :], scr[1][:pt, :], rowA[:pt, :], ALU.mult, [tok("scr1"), tok("rowA")], [tok("scr1")])
                  if t >= 15:
                      tt(scr[1][:pt, :], scr[1][:pt, :], rowB[:pt, :], ALU.add, [tok("scr1"), tok("rowB")], [tok("scr1")])
                      out_dma(o_cvp[l] if t == 15 else o_cvs[l], scr[1][:pt, :], [tok("scr1")])
                      cp(bufP[:pt, li * D:(li + 1) * D], scr[1][:pt, :], [tok("scr1")], [tPt[li]] + tPj)
                  else:
                      tt(bufP[:pt, li * D:(li + 1) * D], scr[1][:pt, :], rowB[:pt, :], ALU.add, [tok("scr1"), tok("rowB")],
                         [tPt[li]] + tPj)

              chk(3)
              for j in range(8):
                  ws, tw, dw = slot()
                  wsb = ws[:].bitcast(BF16)
                  load_w(wsb[:, :, 0:128], w_in[l], j * 128, 128, tw, dw)
                  load_w(wsb[:, :, 128:256], w_in[l], 2048 + j * 128, 128, tw, dw)
                  for (b0, bn, lis) in blocks:
                      pu, tpu = bank()
                      mm(pu[:, 0:bn], [(wsb[:, k, 0:128], hT[:, k, b0:b0 + bn]) for k in range(8)],
                         [tw] + [tH[i] for i in lis], [tpu])
                      pz, tpz = bank()
                      mm(pz[:, 0:bn], [(wsb[:, k, 128:256], hT[:, k, b0:b0 + bn]) for k in range(8)],
                         [tw] + [tH[i] for i in lis], [tpz])
                      pS, tpS = bank()
                      for li in lis:
                          (_, t, c0, pt, smp) = tinfo[li]
                          if not smp:
                              wmat = wsT_bf[:, j, :]
                              brow = bsr_bf[0:1, j * 128:(j + 1) * 128]
                              tws = [tok("wsT"), tok("bsr")]
                          else:
                              wmat = wsTs_bf[:, j, :]
                              brow = bsrs_bf[0:1, j * 64:(j + 1) * 64]
                              tws = [tok("wsTs"), tok("bsrs")]
                          mm(pS[:, c0 - b0:c0 - b0 + pt],
                             [(bufP[:pt, li * D + j * 128:li * D + (j + 1) * 128], wmat),
                              (ones_row[0:1, :], brow)], [tPt[li], tC, tMisc] + tws, [tpS])
                      act(ev32[0][:, 0:bn], pu[:, 0:bn], AF.Gelu_apprx_tanh, [tpu], [tok("ev0")])
                      act(ev32[1][:, 0:bn], pz[:, 0:bn], AF.Silu, [tpz], [tok("ev1")])
                      tt(ev32[0][:, 0:bn], ev32[0][:, 0:bn], pS[:, 0:bn], ALU.mult, [tok("ev0"), tpS], [tok("ev0")])
                      tt(bufQ[:, j, b0:b0 + bn], ev32[0][:, 0:bn], ev32[1][:, 0:bn], ALU.mult, [tok("ev0"), tok("ev1")],
                         [tQ[j]])

              chk(4)
              mTv = bufP[:, 0:8 * MAXTOK].rearrange("p (j n) -> p j n", j=8)
              for j in range(8):
                  ws, tw, dw = slot()
                  wsb = ws[:].bitcast(BF16)
                  load_w(wsb[:, :, 0:128], w_pa[l], j * 128, 128, tw, dw)
                  load_w(wsb[:, :, 128:256], w_in[l], 7184 + j * 128, 128, tw, dw)
                  for (b0, bn, lis) in blocks:
                      pa, tpa = bank()
                      mm(pa[:, 0:bn], [(wsb[:, k, 0:128], bufQ[:, k, b0:b0 + bn]) for k in range(8)], [tw] + tQ, [tpa])
                      pg, tpg = bank()
                      mm(pg[:, 0:bn], [(wsb[:, k, 128:256], hT[:, k, b0:b0 + bn]) for k in range(8)],
                         [tw] + [tH[i] for i in lis], [tpg])
                      act(ev32[0][:, 0:bn], pg[:, 0:bn], AF.Sigmoid, [tpg], [tok("ev0")])
                      tt(mTv[:, j, b0:b0 + bn], ev32[0][:, 0:bn], pa[:, 0:bn], ALU.mult, [tok("ev0"), tpa], [tPj[j]] + tPt)

              chk(5)
              S.dma("pool", wba[:], w_in[l].rearrange("(k p) c -> p k c", p=128)[:, :, 7168:7184], ds_ba, W=[tok("wba")])
              for (b0, bn, lis) in blocks:
                  ps, tp = bank()
                  mm(ps[:16, 0:bn], [(wba[:, k, :], hT[:, k, b0:b0 + bn]) for k in range(8)],
                     [tok("wba")] + [tH[i] for i in lis], [tp])
                  act(betaT[:, b0:b0 + bn], ps[:16, 0:bn], AF.Sigmoid, [tp], [tok("betaT")])
              psba, tpba = bank()
              for (li, t, c0, pt, smp) in tinfo:
                  mm(psba[:pt, li * 16:(li + 1) * 16], [(hT[:, k, c0:c0 + pt], wba[:, k, :]) for k in range(8)],
                     [tok("wba"), tH[li]], [tpba])
              for (li, t, c0, pt, smp) in tinfo:
                  act(beta_tok[:pt, li, :], psba[:pt, li * 16:li * 16 + 8], AF.Sigmoid, [tpba], [tok("beta_tok")])
                  tt(apre[:pt, li, :], psba[:pt, li * 16 + 8:li * 16 + 16], dtb_sb[:pt, l * 8:(l + 1) * 8], ALU.add,
                     [tpba, tC], [tok("apre")])
              for (li, t, c0, pt, smp) in tinfo:
                  act(apre[:pt, li, :], apre[:pt, li, :], AF.Exp, [tok("apre")], [tok("apre")])
                  act(apre[:pt, li, :], apre[:pt, li, :], AF.Ln, [tok("apre")], [tok("apre")], bias=one_c[:pt, :], scale=1.0)
                  tt(g_tok[:pt, li, :], apre[:pt, li, :], nA_sb[:pt, l * 8:(l + 1) * 8], ALU.mult, [tok("apre"), tok("nA")],
                     [tok("g_tok")])
                  mk = MASKS[smp]
                  ps, tp = bank()
                  mm(ps[:pt, 0:8], [(mk["Lincl"][:pt, :pt], g_tok[:pt, li, :])], [tok("g_tok"), tC], [tp])
                  mm(ps[:pt, 8:16], [(mk["Ustr"][:pt, :pt], g_tok[:pt, li, :])], [tok("g_tok"), tC], [tp])
                  if not smp:
                      mm(ps[:, 16:24], [(ones_f[:pt, :], g_tok[:pt, li, :])], [tok("g_tok"), tMisc], [tp])
                      act(GC[:, li, :], ps[:, 16:24], AF.Exp, [tp], [tok("GC")])
                  act(gam_tok[:pt, li, :], ps[:pt, 0:8], AF.Exp, [tp], [tok("gam")])
                  act(egl[:pt, li, :], ps[:pt, 8:16], AF.Exp, [tp], [tok("egl")])
                  stt(nbG[:pt, li, :], gam_tok[:pt, li, :], -1.0, beta_tok[:pt, li, :], ALU.mult, ALU.mult,
                      [tok("gam"), tok("beta_tok")], [tok("nbG")])
                  if smp:
                      tt(G2[:], g_tok[:64, li, :].unsqueeze(1).to_broadcast([64, 16, 8]),
                         seqsel_sb[:].unsqueeze(2).to_broadcast([64, 16, 8]), ALU.mult, [tok("g_tok"), tC], [tok("G2")])
                      ps2, tp2 = bank()
                      mm(ps2[:, 0:128], [(ones_f[:64, :], G2[:].rearrange("p s h -> p (s h)"))], [tok("G2"), tMisc], [tp2])
                      act(GCs[:].rearrange("p s h -> p (s h)"), ps2[:, 0:128], AF.Exp, [tp2], [tok("GCs")])

              chk(5.1)
              csq = scr[0][:, 0:MAXTOK]
              csk = scr[1][:, 0:MAXTOK]
              for h in range(8):
                  ws, tw, dw = slot()
                  wsb = ws[:].bitcast(BF16)
                  for ci, cbase in enumerate((3072, 4096, 5120, 6144)):
                      load_w(wsb[:, :, ci * 128:(ci + 1) * 128], w_in[l], cbase + h * 128, 128, tw, dw)
                  if has_s:
                      S.dma("pool", S0b[:], sssm[l, :, h].rearrange("s d v -> d s v"), ds_for("S0b"), W=[tok("S0b")])
                  for ci in range(3):
                      for jt in range(4):
                          ts(diag[:, ci * 4 + jt, :], ident_bf[:], cw_sb[:, l, ci * 8 + h, jt:jt + 1], None, ALU.mult, None,
                             [tC], [tok("diag")])
                  ssps = {}
                  for ci in range(3):
                      ctg = ci * 8 + h
                      dst = (csq, csk, vT)[ci]
                      tdst = (tok("scr0"), tok("scr1"), tok("vT"))[ci]
                      for (b0, bn, lis) in blocks:
                          smpb = (bn == 64)
                          pp, tpp = bank()
                          mm(pp[:, 0:bn], [(wsb[:, k, ci * 128:(ci + 1) * 128], hT[:, k, b0:b0 + bn]) for k in range(8)],
                             [tw] + [tH[i] for i in lis], [tpp])
                          pc, tpc = bank()
                          if not smpb:
                              cp(xpre[:, 0:3], halo[:, ctg, :], [tok("halo")], [tok("xpre")])
                              cp(xpre[:, 3:3 + bn], pp[:, 0:bn], [tpp], [tok("xpre")], eng="act")
                              cp(halo[:, ctg, :], xpre[:, bn:bn + 3], [tok("xpre")], [tok("halo")])
                              mm(pc[:, 0:bn], [(diag[:, ci * 4 + jt, :], xpre[:, jt:jt + bn]) for jt in range(4)],
                                 [tok("diag"), tok("xpre")], [tpc])
                          else:
                              cp(xps[:, :, 0:3], scv_bf[:, ctg, :, :], [tok("scv")], [tok("xps")])
                              cp(xps[:, :, 3:7], pp[:, 0:64].rearrange("p (s t) -> p s t", t=4), [tpp], [tok("xps")],
                                 eng="act")
                              mm(pc[:, 0:64].rearrange("p (s t) -> p s t", t=4),
                                 [(diag[:, ci * 4 + jt, :], xps[:, :, jt:jt + 4]) for jt in range(4)],
                                 [tok("diag"), tok("xps")], [tpc])
                          act(dst[:, b0:b0 + bn], pc[:, 0:bn], AF.Silu, [tpc], [tdst])
                          if ci < 2:
                              act(sqb[:, b0:b0 + bn], dst[:, b0:b0 + bn], AF.Square, [tdst], [tok("sqb")])
                              pss, tpss = bank()
                              mm(pss[:, 0:bn], [(ones_bf[:], sqb[:, b0:b0 + bn])], [tok("sqb"), tMisc], [tpss])
                              rsq_t = tok(f"rn{ci}")
                              rnb = (rn, rn2)[ci]
                              ssps[(ci, b0)] = (pss, tpss)
                              cp(rnb[:, b0:b0 + bn], pss[:, 0:bn], [tpss], [rsq_t])
                  chk(5.3)
                  for (b0, bn, lis) in blocks:
                      pz, tpz = bank()
                      mm(pz[:, 0:bn], [(wsb[:, k, 384:512], hT[:, k, b0:b0 + bn]) for k in range(8)],
                         [tw] + [tH[i] for i in lis], [tpz])
                      act(zbs[:, b0:b0 + bn], pz[:, 0:bn], AF.Silu, [tpz], [tok("zbs")])
                  for ci in range(2):
                      rnb = (rn, rn2)[ci]
                      rsq_t = tok(f"rn{ci}")
                      ntk = 512 + (64 if has_s else 0)
                      rsqrt_small(rnb[:, 0:ntk], rnb[:, 0:ntk], eps_nm, 1.0, [rsq_t], [rsq_t])
                      for (b0, bn, lis) in blocks:
                          if ci == 0:
                              stt(qT[:, b0:b0 + bn], csq[:, b0:b0 + bn], 128.0 ** -0.5, rnb[:, b0:b0 + bn], ALU.mult, ALU.mult,
                                  [tok("scr0"), rsq_t], [tok("qT")])
                          else:
                              tt(kT[:, b0:b0 + bn], csk[:, b0:b0 + bn], rnb[:, b0:b0 + bn], ALU.mult, [tok("scr1"), rsq_t],
                                 [tok("kT")])
                              pb_, tpb_ = bank()
                              mm(pb_[:, 0:bn], [(sel_bf[:, h, :], betaT[:, b0:b0 + bn])], [tok("betaT"), tC], [tpb_])
                              tt(kbT[:, b0:b0 + bn], kT[:, b0:b0 + bn], pb_[:, 0:bn], ALU.mult, [tok("kT"), tpb_],
                                 [tok("kbT")])
                  for (li, t, c0, pt, smp) in tinfo:
                      ps, tp = bank()
                      psb = ps[:].bitcast(BF16)
                      transpose(psb[:pt, 0:128], vT[:, c0:c0 + pt], ident_bf[:], [tok("vT"), tC], [tp])
                      transpose(psb[:pt, 128:256], kT[:, c0:c0 + pt], ident_bf[:], [tok("kT"), tC], [tp])
                      ts(Vb[:pt, li, :], psb[:pt, 0:128], beta_tok[:pt, li, h:h + 1], None, ALU.mult, None,
                         [tp, tok("beta_tok")], [tok("Vb")])
                      ts(Kd[:pt, li, :], psb[:pt, 128:256], egl[:pt, li, h:h + 1], None, ALU.mult, None,
                         [tp, tok("egl")], [tok("Kd")])

                  chk(5.4)
                  def tile_gen(li, t, c0, pt, smp, si, done):
                      (Gm, eET, eE, Grow, DTsn, DTi, Dsn, qgT, Mb, Nb, Xb, PTb, Rb, Vnb, KSTb, sqo, rro, t1o) = tsets[si]
                      mk = MASKS[smp]
                      yield
                      ts(Gm[:pt, :pt], mk["Lincl"][:pt, :pt], g_tok[:pt, li, h:h + 1], None, ALU.mult, None,
                         [tC, tok("g_tok")], [tok("Gm_%d" % si)])
                      pe_, tpe = bank()
                      yield
                      mm(pe_[:pt, 0:pt], [(mk["Ustr"][:pt, :pt], Gm[:pt, :pt])], [tC, tok("Gm_%d" % si)], [tpe])
                      yield
                      mm(pe_[:pt, 128:128 + pt], [(Gm[:pt, :pt], mk["Ustr"][:pt, :pt])], [tC, tok("Gm_%d" % si)], [tpe])
                      yield
                      mm(pe_[:, 256:256 + pt], [(ones_f[:pt, :], Gm[:pt, :pt])], [tMisc, tok("Gm_%d" % si)], [tpe])
                      yield
                      act(eET[:pt, :pt], pe_[:pt, 0:pt], AF.Exp, [tpe], [tok("eET_%d" % si)])
                      yield
                      act(eE[:pt, :pt], pe_[:pt, 128:128 + pt], AF.Exp, [tpe], [tok("eE_%d" % si)])
                      yield
                      act(Grow[:, :pt], pe_[:, 256:256 + pt], AF.Exp, [tpe], [tok("Grow_%d" % si)])
                      yield
                      tt(DTsn[:pt, :pt], eET[:pt, :pt], mk["mSTn"][:pt, :pt], ALU.mult, [tok("eET_%d" % si), tC], [tok("DTsn_%d" % si)])
                      yield
                      tt(DTi[:pt, :pt], eET[:pt, :pt], mk["mIT"][:pt, :pt], ALU.mult, [tok("eET_%d" % si), tC], [tok("DTi_%d" % si)])
                      yield
                      tt(Dsn[:pt, :pt], eE[:pt, :pt], mk["mSNn"][:pt, :pt], ALU.mult, [tok("eE_%d" % si), tC], [tok("Dsn_%d" % si)])
                      yield
                      tt(qgT[:, :pt], qT[:, c0:c0 + pt], Grow[:, :pt], ALU.mult, [tok("qT"), tok("Grow_%d" % si)], [tok("qgT_%d" % si)])
                      pa_, tpa_ = bank()
                      yield
                      mm(pa_[:pt, 0:pt], [(kT[:, c0:c0 + pt], kbT[:, c0:c0 + pt])], [tok("kT"), tok("kbT")], [tpa_])
                      yield
                      mm(pa_[:pt, 128:128 + pt], [(kbT[:, c0:c0 + pt], kT[:, c0:c0 + pt])], [tok("kT"), tok("kbT")], [tpa_])
                      yield
                      mm(pa_[:pt, 256:256 + pt], [(kT[:, c0:c0 + pt], qT[:, c0:c0 + pt])], [tok("kT"), tok("qT")], [tpa_])
                      tM = [tok("M0_%d" % si), tok("M1_%d" % si)]
                      tN = [tok("N0_%d" % si), tok("N1_%d" % si)]
                      tXx = [tok("X0_%d" % si), tok("X1_%d" % si)]
                      yield
                      tt(Mb[0][:pt, :pt], pa_[:pt, 0:pt], DTsn[:pt, :pt], ALU.mult, [tpa_, tok("DTsn_%d" % si)], [tM[0]])
                      yield
                      tt(Nb[0][:pt, :pt], pa_[:pt, 128:128 + pt], Dsn[:pt, :pt], ALU.mult, [tpa_, tok("Dsn_%d" % si)], [tN[0]])
                      yield
                      tt(PTb[:pt, :pt], pa_[:pt, 256:256 + pt], DTi[:pt, :pt], ALU.mult, [tpa_, tok("DTi_%d" % si)], [tok("PT_%d" % si)])
                      yield
                      tt(Xb[0][:pt, :pt], Mb[0][:pt, :pt], ident_bf[:pt, :pt], ALU.add, [tM[0], tC], [tXx[0]])
                      cur = 0
                      nlev = 1 if smp else 6
                      for lv in range(nlev):
                          lastlv = (lv == nlev - 1)
                          nx = 1 - cur
                          pn, tpn = bank()
                          yield
                          mm(pn[:pt, 0:pt], [(Mb[cur][:pt, :pt], Nb[cur][:pt, :pt])], [tM[cur], tN[cur]], [tpn])
                          if not lastlv:
                              yield
                              mm(pn[:pt, 128:128 + pt], [(Nb[cur][:pt, :pt], Mb[cur][:pt, :pt])], [tM[cur], tN[cur]], [tpn])
                          yield
                          cp(Nb[nx][:pt, :pt], pn[:pt, 0:pt], [tpn], [tN[nx]], eng="act")
                          if not lastlv:
                              yield
                              cp(Mb[nx][:pt, :pt], pn[:pt, 128:128 + pt], [tpn], [tM[nx]], eng="act")
                          px, tpx = bank()
                          yield
                          mm(px[:pt, 0:pt], [(ident_bf[:pt, :pt], Xb[cur][:pt, :pt]), (Nb[nx][:pt, :pt], Xb[cur][:pt, :pt])],
                             [tC, tXx[cur], tN[nx]], [tpx])
                          yield
                          cp(Xb[nx][:pt, :pt], px[:pt, 0:pt], [tpx], [tXx[nx]], eng="act")
                          cur = nx
                      Xf, tXf = Xb[cur], tXx[cur]
                      while li > 0 and not done.get(li - 1):
                          yield
                      pk, tpk = bank()
                      if not smp:
                          yield
                          mm(pk[:pt, 0:128], [(kT[:, c0:c0 + pt], S_bf[:, h, :])], [tok("kT"), tok("Sbf")], [tpk])
                          yield
                          stt(Rb[:pt, :], pk[:pt, 0:128], nbG[:pt, li, h:h + 1], Vb[:pt, li, :], ALU.mult, ALU.add,
                              [tpk, tok("nbG"), tok("Vb")], [tok("R_%d" % si)])
                      else:
                          for s in range(NSEQ):
                              yield
                              mm(pk[:, 4 * s:4 * s + 4], [(S0b[:, s, :], kT[:, c0 + 4 * s:c0 + 4 * s + 4])],
                                 [tok("kT"), tok("S0b")], [tpk])
                          yield
                          cp(KSTb[:, :], pk[:, 0:64], [tpk], [tok("KST_%d" % si)], eng="act")
                          pk2, tpk2 = bank()
                          pk2b = pk2[:].bitcast(BF16)
                          yield
                          transpose(pk2b[:64, 0:128], KSTb[:, :], ident_bf[:], [tok("KST_%d" % si), tC], [tpk2])
                          yield
                          stt(Rb[:pt, :], pk2b[:64, 0:128], nbG[:pt, li, h:h + 1], Vb[:pt, li, :], ALU.mult, ALU.add,
                              [tpk2, tok("nbG"), tok("Vb")], [tok("R_%d" % si)])
                      pv, tpv = bank()
                      yield
                      mm(pv[:pt, 0:128], [(Xf[:pt, :pt], Rb[:pt, :])], [tXf, tok("R_%d" % si)], [tpv])
                      yield
                      cp(Vnb[:pt, :], pv[:pt, 0:128], [tpv], [tok("Vn_%d" % si)], eng="act")
                      po, tpo = bank()
                      if not smp:
                          yield
                          mm(po[:, 0:pt], [(S_bf[:, h, :], qgT[:, :pt]), (Vnb[:pt, :], PTb[:pt, :pt])],
                             [tok("Sbf"), tok("qgT_%d" % si), tok("Vn_%d" % si), tok("PT_%d" % si)], [tpo])
                          pd, tpd = bank()
                          yield
                          mm(pd[:, 0:128], [(Kd[:pt, li, :], Vnb[:pt, :])], [tok("Kd"), tok("Vn_%d" % si)], [tpd])
                          yield
                          stt(S32[:, h, :], S32[:, h, :], GC[:, li, h:h + 1], pd[:, 0:128], ALU.mult, ALU.add,
                              [tok("S32"), tok("GC"), tpd], [tok("S32")])
                          yield
                          cp(S_bf[:, h, :], S32[:, h, :], [tok("S32")], [tok("Sbf")], eng="act")
                          if t == 15:
                              yield
                              out_dma(o_ssmp[l, h], S32[:, h, :], [tok("S32")])
                      else:
                          yield
                          mm(po[:, 0:64], [(Vnb[:64, :], PTb[:64, :64])], [tok("Vn_%d" % si), tok("PT_%d" % si)], [tpo], first=True, last=False)
                          for s in range(NSEQ):
                              yield
                              mm(po[:, 4 * s:4 * s + 4], [(S0b[:, s, :], qgT[:, 4 * s:4 * s + 4])], [tok("S0b"), tok("qgT_%d" % si)],
                                 [tpo], first=False, last=(s == NSEQ - 1))
                          for sh in range(2):
                              yield
                              S.dma("sp", S0f[:], sssm[l, 8 * sh:8 * sh + 8, h].rearrange("s d v -> d s v"), ds_for("S0f"),
                                    W=[tok("S0f")])
                              for s4 in range(2):
                                  pd, tpd = bank()
                                  for sq_ in range(4):
                                      s = 8 * sh + 4 * s4 + sq_
                                      yield
                                      ts(Kdm[:, :], Kd[:64, li, :], seqsel_sb[:, s:s + 1], None, ALU.mult, None,
                                         [tok("Kd"), tC], [tok("Kdm")])
                                      yield
                                      mm(pd[:, sq_ * 128:(sq_ + 1) * 128], [(Kdm[:, :], Vnb[:64, :])], [tok("Kdm"), tok("Vn_%d" % si)],
                                         [tpd])
                                  for sq_ in range(4):
                                      s = 8 * sh + 4 * s4 + sq_
                                      sl = 4 * s4 + sq_
                                      yield
                                      stt(S0f[:, sl, :], S0f[:, sl, :], GCs[:, s, h:h + 1], pd[:, sq_ * 128:(sq_ + 1) * 128],
                                          ALU.mult, ALU.add, [tok("S0f"), tok("GCs"), tpd], [tok("S0f")])
                              yield
                              out_dma(o_ssms[l, 8 * sh:8 * sh + 8, h].rearrange("s d v -> d s v"), S0f[:], [tok("S0f")])
                      yield
                      act(sqo[:, :pt], po[:, 0:pt], AF.Square, [tpo], [tok("sqo_%d" % si)])
                      pr, tpr = bank()
                      yield
                      mm(pr[:, 0:pt], [(ones_bf[:], sqo[:, :pt])], [tok("sqo_%d" % si), tMisc], [tpr])
                      yield
                      rsqrt_small(rro[:, :pt], pr[:, 0:pt], eps_nm, 1.0 / 128.0, [tpr], [tok("rro_%d" % si)])
                      yield
                      tt(t1o[:, :pt], po[:, 0:pt], rro[:, :pt], ALU.mult, [tpo, tok("rro_%d" % si)], [tok("t1o_%d" % si)])
                      yield
                      stt(bufQ[:, h, c0:c0 + pt], t1o[:, :pt], ogc_sb[:, l:l + 1], zbs[:, c0:c0 + pt], ALU.mult, ALU.mult,
                          [tok("t1o_%d" % si), tC, tok("zbs")], [tQ[h]])

                      done[li] = True
                  done = {}
                  pending = list(tinfo)
                  active = []
                  free_sets = list(range(NSETS))
                  while pending or active:
                      while pending and free_sets:
                          ti_ = pending.pop(0)
                          si_ = free_sets.pop(0)
                          active.append((tile_gen(*ti_, si_, done), si_))
                      for (g_, si_) in list(active):
                          try:
                              next(g_)
                          except StopIteration:
                              active.remove((g_, si_))
                              free_sets.append(si_)
              chk(6)
              if has_s or (15 in tiles):
                  for grp6 in range(6):
                      ws, tw, dw = slot()
                      wsb = ws[:].bitcast(BF16)
                      load_w(wsb, w_in[l], 3072 + grp6 * 512, 512, tw, dw)
                      if has_s:
                          ps, tp = bank()
                          mm(ps[:64, :], [(hT[:, k, 512:576], wsb[:, k, :]) for k in range(8)], [tw, tH[4]], [tp])
                          cp(ev32[0][:64, :], ps[:64, :], [tp], [tok("ev0")], eng="act")
                          for tq in range(1, 4):
                              out_dma(o_convs[l, :, tq - 1, grp6 * 512:(grp6 + 1) * 512], ev32[0][tq:64:4, :], [tok("ev0")])
                      if 15 in tiles:
                          l15 = tiles.index(15)
                          ps2, tp2 = bank()
                          mm(ps2[:, :], [(hT[:, k, l15 * 128:(l15 + 1) * 128], wsb[:, k, :]) for k in range(8)], [tw, tH[l15]], [tp2])
                          cp(ev32[1][64:128, :], ps2[64:128, :], [tp2], [tok("ev1")], eng="act")
                          out_dma(o_convp[l, :, grp6 * 512:(grp6 + 1) * 512], ev32[1][125:128, :], [tok("ev1")])

              chk(7)
              for j in range(8):
                  ws, tw, dw = slot()
                  wsb = ws[:].bitcast(BF16)
                  load_w(wsb[:, :, 0:128], w_pb[l], j * 128, 128, tw, dw)
                  load_w(wsb[:, :, 128:256], w_in[l], 8208 + j * 128, 128, tw, dw)
                  for (b0, bn, lis) in blocks:
                      pa, tpa = bank()
                      mm(pa[:, 0:bn], [(wsb[:, k, 0:128], bufQ[:, k, b0:b0 + bn]) for k in range(8)], [tw] + tQ, [tpa])
                      pg, tpg = bank()
                      mm(pg[:, 0:bn], [(wsb[:, k, 128:256], hT[:, k, b0:b0 + bn]) for k in range(8)],
                         [tw] + [tH[i] for i in lis], [tpg])
                      act(ev32[0][:, 0:bn], pg[:, 0:bn], AF.Sigmoid, [tpg], [tok("ev0")])
                      tt(ev32[0][:, 0:bn], ev32[0][:, 0:bn], pa[:, 0:bn], ALU.mult, [tok("ev0"), tpa], [tok("ev0")])
                      tt(mTv[:, j, b0:b0 + bn], ev32[0][:, 0:bn], mTv[:, j, b0:b0 + bn], ALU.add, [tok("ev0"), tPj[j]],
                         [tPj[j]])

              chk(8)
              S.dma("sp", rowA[:], ln_g[l:l + 1, :].partition_broadcast(128), ds_row[0], W=[tok("rowA")])
              S.dma("sp", rowB[:], ln_b[l:l + 1, :].partition_broadcast(128), ds_row[1], W=[tok("rowB")])
              wo = []
              for hf in range(2):
                  ws, tw, dw = slot()
                  wsb = ws[:].bitcast(BF16)
                  load_w(wsb, w_o[l], hf * 512, 512, tw, dw)
                  wo.append((wsb, tw))
              for (li, t, c0, pt, smp) in tinfo:
                  gt = gate_s if smp else gate_p
                  tg = tok("gate_s") if smp else tok("gate_p")
                  for hf in range(2):
                      ps, tp = bank()
                      mm(ps[:pt, :], [(mTv[:, k, c0:c0 + pt], wo[hf][0][:, k, :]) for k in range(8)], tPj + [wo[hf][1]], [tp])
                      tt(scr[0][:pt, hf * 512:(hf + 1) * 512], ps[:pt, :], gt[:pt, hf * 512:(hf + 1) * 512], ALU.mult,
                         [tp, tg], [tok("scr0")])
                  stt(scr[0][:pt, :], x_sb[:pt, li, :], float(ALPHA), scr[0][:pt, :], ALU.mult, ALU.add, [tX[li], tok("scr0")],
                      [tok("scr0")])
                  layer_norm_stats(scr[0][:pt, :], pt, [tok("scr0")])
                  act(scr[1][:pt, :], scr[0][:pt, :], AF.Identity, [tok("scr0"), tok("rstd"), tok("nbias")], [tok("scr1")],
                      bias=nbias[:pt, :], scale=rstd[:pt, :])
                  tt(scr[1][:pt, :], scr[1][:pt, :], rowA[:pt, :], ALU.mult, [tok("scr1"), tok("rowA")], [tok("scr1")])
                  tt(x_sb[:pt, li, :], scr[1][:pt, :], rowB[:pt, :], ALU.add, [tok("scr1"), tok("rowB")], [tX[li]])
                  if last_layer:
                      if smp:
                          out_dma(y_s, x_sb[:64, li, :], [tX[li]])
                      else:
                          out_dma(y_p[t * 128:(t + 1) * 128, :], x_sb[:, li, :], [tX[li]])

    try:
        main_loop()
    except _Stop:
        pass
    for i_ in range(int(os.environ.get("KDMA", "0"))):
        S.dma("pool", wba[:], w_in[0].rearrange("(k p) c -> p k c", p=128)[:, :, 16 * (i_ % 500):16 * (i_ % 500) + 16], ds_ba,
              W=[tok("wba")])
    for i_ in range(int(os.environ.get("KDMAH", "0"))):
        S.dma("sp", rowA[:], xin[i_:i_ + 1, :].partition_broadcast(128), ds_row[0], W=[tok("rowA")])
    for i_ in range(int(os.environ.get("KACT", "0"))):
        nc.scalar.activation(out=ev32[0][0:1, 0:1], in_=one_c[0:1, 0:1], func=AF.Gelu_apprx_tanh)
        nc.scalar.activation(out=ev32[0][0:1, 0:1], in_=one_c[0:1, 0:1], func=AF.Silu)
    if os.environ.get("KINC"):
        semx = nc.alloc_semaphore("dummy_inc")
        for i_ in range(int(os.environ["KINC"])):
            nc.vector.memset(ev32[1][0:1, 0:1], 0.0).then_inc(semx, 1)
    if os.environ.get("KDUMMY"):
        for e_ in ("pe", "act", "dve"):
            for b in tPS:
                S._wait(e_, b.w)
                for ev in b.r.values():
                    S._wait(e_, ev)
        for _ in range(20000):
            nc.tensor.matmul(psum[0][0:1, 0:1], ones_row[0:1, 0:1], ones_row[0:1, 0:1], start=True, stop=True)
        for _ in range(12000):
            nc.scalar.copy(out=ev32[0][0:1, 0:1], in_=one_c[0:1, 0:1])
        for _ in range(12000):
            nc.vector.memset(ev32[1][0:1, 0:1], 0.0)
    for (sm, c) in S.final_waits:
        nc.sync.wait_ge(sm, c)
    for d in S.dsems:
        if d.count:
            nc.sync.wait_ge(d.sem, d.count)
    print("sbuf bytes remaining:", nc.sbuf_bytes_remaining, "instr counts:", S.total, "nsem:", S.nsem)
    return nc


def _consts():
    c = np.zeros((128, 12, 128), np.float32)
    i = np.arange(128)
    a, b = i[:, None], i[None, :]
    c[:, 0] = (a == b)
    c[:, 1] = (a <= b)
    c[:, 2] = (a > b)
    c[:, 3] = -1.0 * (b > a)
    c[:, 4] = (b >= a)
    c[:, 5] = -1.0 * (b < a)
    same = ((a // 4) == (b // 4)) & (a < 64) & (b < 64)
    c[:, 6] = same & (a <= b)
    c[:, 7] = same & (a > b)
    c[:, 8] = -1.0 * (same & (b > a))
    c[:, 9] = same & (b >= a)
    c[:, 10] = -1.0 * (same & (b < a))
    sel = np.zeros((16, 8, 128), np.float32)
    for h in range(8):
        sel[h, h, :] = 1.0
    seqsel = (np.arange(64)[:, None] // 4 == np.arange(16)[None, :]).astype(np.float32)
    return c, sel, seqsel


def kernel(x_prompt, x_sample, state_conv, state_ssm, c_prompt, c_sample, w_ada, b_ada, w_in,
           w_s, b_s, lnv_g, lnv_b, conv_w, a_log, dt_bias, onorm_g, w_pa, w_pb, w_o, ln_g, ln_b):
    f = lambda a: np.ascontiguousarray(np.asarray(a, dtype=np.float32))
    (x_prompt, x_sample, state_conv, state_ssm, c_prompt, c_sample, w_ada, b_ada, w_in, w_s, b_s, lnv_g, lnv_b,
     conv_w, a_log, dt_bias, onorm_g, w_pa, w_pb, w_o, ln_g, ln_b) = [f(a) for a in (
        x_prompt, x_sample, state_conv, state_ssm, c_prompt, c_sample, w_ada, b_ada, w_in, w_s, b_s, lnv_g, lnv_b,
        conv_w, a_log, dt_bias, onorm_g, w_pa, w_pb, w_o, ln_g, ln_b)]
    depth = w_in.shape[0]
    nc = build_nc(depth)
    consts, sel, seqsel = _consts()
    b_adaT = f(b_ada[:, :2048].reshape(depth, 16, 128).transpose(2, 0, 1))
    b_gate = f(b_ada[:, 2048:])
    w_sT = f(w_s.transpose(0, 3, 1, 2))
    w_sTs = np.zeros((depth, 64, 8, 64), np.float32)
    for q in range(16):
        w_sTs[:, 4 * q:4 * q + 4, :, 4 * q:4 * q + 4] = w_s[:, :, :4, :4].transpose(0, 3, 1, 2)
    bsr = f(b_s.reshape(1, -1))
    bsrs = f(np.tile(b_s[:, :, :4], (1, 1, 16)).reshape(1, -1))
    cw = f(conv_w.reshape(depth, 4, 24, 128).transpose(3, 0, 2, 1))
    alog = f(a_log.reshape(1, -1))
    dtb = f(dt_bias.reshape(1, -1))
    ogc = f(onorm_g.T)
    shared = dict(w_ada=w_ada, b_adaT=b_adaT, b_gate=b_gate, w_in=w_in, w_sT=w_sT, w_sTs=w_sTs, bsr=bsr, bsrs=bsrs,
                  lnv_g=lnv_g, lnv_b=lnv_b, cw=cw, alog=alog, dtb=dtb, ogc=ogc, w_pa=w_pa, w_pb=w_pb, w_o=w_o,
                  ln_g=ln_g, ln_b=ln_b, consts=consts, selc=sel, seqsel=seqsel)
    in_maps = []
    for i in range(NCORE):
        ss = slice(NSEQ * i, NSEQ * (i + 1))
        xin = f(np.concatenate([x_prompt[i], x_sample[ss].reshape(64, D)], axis=0))
        cc = np.concatenate([c_prompt[i:i + 1], c_sample[ss]], axis=0)
        cT = f(cc.reshape(17, 8, 128).transpose(2, 1, 0))
        sconv = f(state_conv[:, ss].reshape(depth, NSEQ, 3, 24, 128).transpose(0, 4, 3, 1, 2))
        sssm = f(state_ssm[:, ss])
        m = dict(shared)
        m.update(xin=xin, cT=cT, sconv=sconv, sssm=sssm)
        in_maps.append(m)
    import os
    if os.environ.get('KTRACE'):
        res = run_bass_kernel_spmd(nc, in_maps, core_ids=list(range(NCORE)), trace=True)
        print('EXEC_TIME_NS', res.exec_time_ns)
    else:
        res = run_bass_kernel_spmd(nc, in_maps, core_ids=list(range(NCORE)))
    R = res.results
    y_prompt = np.stack([R[i]["y_p"] for i in range(NCORE)], 0)
    y_sample = np.concatenate([R[i]["y_s"].reshape(NSEQ, 4, D) for i in range(NCORE)], 0)
    conv_p = np.stack([R[i]["o_convp"] for i in range(NCORE)], 1)
    ssm_p = np.stack([R[i]["o_ssmp"] for i in range(NCORE)], 1)
    cv_p = np.stack([R[i]["o_cvp"] for i in range(NCORE)], 1)
    conv_s = np.concatenate([R[i]["o_convs"] for i in range(NCORE)], 1)
    ssm_s = np.concatenate([R[i]["o_ssms"] for i in range(NCORE)], 1)
    cv_s = np.concatenate([R[i]["o_cvs"].reshape(depth, NSEQ, 4, D) for i in range(NCORE)], 1)
    return tuple(np.ascontiguousarray(a, dtype=np.float32) for a in
                 (y_prompt, y_sample, conv_p, ssm_p, cv_p, conv_s, ssm_s, cv_s))
```

```python
import numpy as np
import concourse.bass as bass
import concourse.mybir as mybir
from concourse.bass_utils import run_bass_kernel_spmd

F32, BF16 = mybir.dt.float32, mybir.dt.bfloat16
AF = mybir.ActivationFunctionType
ALU = mybir.AluOpType

D = 1024
DEPTH = 4
NCORE = 8
SEQ = 2048
NPT = 16
NSEQ = 16
P_IN = 9232
ALPHA = (2 * DEPTH) ** 0.25
LN_EPS = 1e-5
NORM_EPS = 1e-6
GROUPS = [[0, 1, 2, 3, 16], [4, 5, 6, 7], [8, 9, 10, 11], [12, 13, 14, 15]]
MAXTOK = 576


class Buf:
    __slots__ = ("name", "w", "r")

    def __init__(self, name):
        self.name = name
        self.w = None
        self.r = {}


class DmaSem:
    def __init__(self, sem, name):
        self.sem = sem
        self.count = 0
        self.name = name


class Sched:
    import os as _os
    EPOCH = int(_os.environ.get('KEPOCH', '8000'))

    def __init__(self, nc):
        self.nc = nc
        self.eng = {"pe": nc.tensor, "act": nc.scalar, "dve": nc.vector, "pool": nc.gpsimd, "sp": nc.sync}
        self.sem = {k: nc.alloc_semaphore("sem_" + k) for k in self.eng}
        self.epoch = {k: 0 for k in self.eng}
        self.cnt = {k: 0 for k in self.eng}
        self.total = {k: 0 for k in self.eng}
        self.known = {k: {} for k in self.eng}
        self.dsems = []
        self.final_waits = []
        self.nsem = len(self.eng)

    def dsem(self, name):
        d = DmaSem(self.nc.alloc_semaphore("ds_" + name), name)
        d.gen = 0
        self.dsems.append(d)
        self.nsem += 1
        return d

    def _wait(self, eng, ev):
        if ev is None:
            return
        if ev[0] == "e":
            _, src, ep, idx, sem = ev
            if src == eng and eng in ("pe", "sp"):
                return
            key = ("e", src)
        else:
            _, dname, ep, idx, sem = ev
            key = ("d", dname)
        kep, kval = self.known[eng].get(key, (-1, 0))
        if kep > ep or (kep == ep and kval >= idx):
            return
        self.known[eng][key] = (ep, idx)
        self.eng[eng].wait_ge(sem, idx)

    def _deps(self, eng, R, W):
        for b in R:
            self._wait(eng, b.w)
        for b in W:
            self._wait(eng, b.w)
            for ev in b.r.values():
                if ev[0] == "e" and ev[1] == eng and eng == "pe":
                    continue
                self._wait(eng, ev)

    def ops(self, eng, fns, R=(), W=()):
        self._deps(eng, R, W)
        if self.cnt[eng] >= self.EPOCH:
            self.sem[eng] = self.nc.alloc_semaphore(f"sem_{eng}_{self.epoch[eng] + 1}")
            self.epoch[eng] += 1
            self.cnt[eng] = 0
            self.nsem += 1
        ins = None
        for fn in fns:
            ins = fn()
        self.cnt[eng] += 1
        self.total[eng] += 1
        ins.then_inc(self.sem[eng], 1)
        ev = ("e", eng, self.epoch[eng], self.cnt[eng], self.sem[eng])
        for b in W:
            b.w = ev
            b.r = {}
        for b in R:
            b.r[eng] = ev
        return ins

    def op(self, eng, fn, R=(), W=()):
        return self.ops(eng, [fn], R, W)

    def dma(self, q, out, in_, ds, R=(), W=()):
        self._deps(q, R, W)
        if ds.count + 16 > self.EPOCH:
            self.final_waits.append((ds.sem, ds.count))
            ds.gen += 1
            ds.sem = self.nc.alloc_semaphore(f"ds_{ds.name}_{ds.gen}")
            ds.count = 0
            self.nsem += 1
        ins = self.eng[q].dma_start(out=out, in_=in_)
        ds.count += 16
        ins.then_inc(ds.sem, 16)
        ev = ("d", ds.name, ds.gen, ds.count, ds.sem)
        for b in W:
            b.w = ev
            b.r = {}
        for b in R:
            b.r[("d", ds.name)] = ev


class _Stop(Exception):
    pass


def build_nc(depth=DEPTH):
    import os
    KSTOP = float(os.environ.get("KSTOP", "99"))
    KGRP = os.environ.get("KGRP")
    groups = GROUPS if KGRP is None else [GROUPS[int(g_)] for g_ in KGRP.split(',')]

    def chk(n):
        if KSTOP <= n:
            raise _Stop()
    nc = bass.Bass("TRN2", target_bir_lowering=False)
    S = Sched(nc)
    dt_in = lambda name, shape: nc.dram_tensor(name, list(shape), F32, kind="ExternalInput").ap()
    dt_out = lambda name, shape: nc.dram_tensor(name, list(shape), F32, kind="ExternalOutput").ap()

    xin = dt_in("xin", [NPT * 128 + 64, D])
    cT = dt_in("cT", [128, 8, 17])
    sconv = dt_in("sconv", [depth, 128, 24, NSEQ, 3])
    sssm = dt_in("sssm", [depth, NSEQ, 8, 128, 128])
    w_ada = dt_in("w_ada", [depth, D, 3 * D])
    b_adaT = dt_in("b_adaT", [128, depth, 16])
    b_gate = dt_in("b_gate", [depth, D])
    w_in = dt_in("w_in", [depth, D, P_IN])
    w_sT = dt_in("w_sT", [depth, 128, 8, 128])
    w_sTs = dt_in("w_sTs", [depth, 64, 8, 64])
    bsr = dt_in("bsr", [1, depth * 8 * 128])
    bsrs = dt_in("bsrs", [1, depth * 8 * 64])
    lnv_g = dt_in("lnv_g", [depth, D])
    lnv_b = dt_in("lnv_b", [depth, D])
    cw = dt_in("cw", [128, depth, 24, 4])
    alog = dt_in("alog", [1, depth * 8])
    dtb = dt_in("dtb", [1, depth * 8])
    ogc = dt_in("ogc", [128, depth])
    w_pa = dt_in("w_pa", [depth, D, D])
    w_pb = dt_in("w_pb", [depth, D, D])
    w_o = dt_in("w_o", [depth, D, D])
    ln_g = dt_in("ln_g", [depth, D])
    ln_b = dt_in("ln_b", [depth, D])
    consts = dt_in("consts", [128, 12, 128])
    selc = dt_in("selc", [16, 8, 128])
    seqsel = dt_in("seqsel", [64, 16])

    y_p = dt_out("y_p", [NPT * 128, D])
    y_s = dt_out("y_s", [64, D])
    o_convp = dt_out("o_convp", [depth, 3, 3 * D])
    o_ssmp = dt_out("o_ssmp", [depth, 8, 128, 128])
    o_cvp = dt_out("o_cvp", [depth, 128, D])
    o_convs = dt_out("o_convs", [depth, NSEQ, 3, 3 * D])
    o_ssms = dt_out("o_ssms", [depth, NSEQ, 8, 128, 128])
    o_cvs = dt_out("o_cvs", [depth, 64, D])

    sb = lambda name, shape, dt=F32: nc.alloc_sbuf_tensor(name, list(shape), dt)
    x_sb = sb("x_sb", [128, 5, D])
    hT = sb("hT", [128, 8, MAXTOK], BF16)
    bufP = sb("bufP", [128, 5 * D], BF16)
    bufQ = sb("bufQ", [128, 8, MAXTOK], BF16)
    NSLOT = 2
    wslot = [sb(f"wslot{i}", [128, 8, 256]) for i in range(NSLOT)]
    rowA = sb("rowA", [128, D])
    rowB = sb("rowB", [128, D])
    gate_p = sb("gate_p", [128, D])
    gate_s = sb("gate_s", [128, D])
    scr = [sb(f"scr{i}", [128, D]) for i in range(2)]
    xn_bf = sb("xn_bf", [128, D], BF16)
    cst = sb("cst", [128, 12, 128])
    ident_bf = sb("ident_bf", [128, 128], BF16)
    ones_bf = sb("ones_bf", [128, 128], BF16)
    ones_f = sb("ones_f", [128, 128])
    sel_bf = sb("sel_bf", [16, 8, 128], BF16)
    seqsel_sb = sb("seqsel_sb", [64, 16])
    eps_ln = sb("eps_ln", [128, 1])
    eps_nm = sb("eps_nm", [128, 1])
    one_c = sb("one_c", [128, 1])
    scT = sb("scT", [128, 8, 17])
    scT_rep = sb("scT_rep", [128, 8, 128])
    scT_s4 = sb("scT_s4", [128, 8, 64])
    modT_all = sb("modT_all", [128, depth, 16, 17])
    opsT_all = sb("opsT_all", [128, depth, 8, 17])
    badaT = sb("badaT", [128, depth, 16])
    cw_sb = sb("cw_sb", [128, depth, 24, 4])
    ogc_sb = sb("ogc_sb", [128, depth])
    alog_sb = sb("alog_sb", [128, depth * 8])
    dtb_sb = sb("dtb_sb", [128, depth * 8])
    nA_sb = sb("nA_sb", [128, depth * 8])
    wsT_bf = sb("wsT_bf", [128, 8, 128], BF16)
    wsTs_bf = sb("wsTs_bf", [64, 8, 64], BF16)
    bsr_bf = sb("bsr_bf", [1, 8 * 128], BF16)
    bsrs_bf = sb("bsrs_bf", [1, 8 * 64], BF16)
    ones_row = sb("ones_row", [1, 128], BF16)
    scv_bf = sb("scv_bf", [128, 24, NSEQ, 3], BF16)
    halo_all = sb("halo_all", [128, depth, 24, 3], BF16)
    mv = sb("mv", [128, 2])
    st6 = sb("st6", [128, 2, 6])
    rstd = sb("rstd", [128, 1])
    nbias = sb("nbias", [128, 1])
    wba = sb("wba", [128, 8, 16], BF16)
    betaT = sb("betaT", [16, MAXTOK], BF16)
    beta_tok = sb("beta_tok", [128, 5, 8])
    apre = sb("apre", [128, 5, 8])
    g_tok = sb("g_tok", [128, 5, 8])
    gam_tok = sb("gam_tok", [128, 5, 8])
    nbG = sb("nbG", [128, 5, 8])
    egl = sb("egl", [128, 5, 8])
    GC = sb("GC", [128, 5, 8])
    G2 = sb("G2", [64, 16, 8])
    GCs = sb("GCs", [128, 16, 8])
    xpre = sb("xpre", [128, 3 + 512], BF16)
    xps = sb("xps", [128, NSEQ, 7], BF16)
    diag = sb("diag", [128, 12, 128], BF16)
    sqb = sb("sqb", [128, MAXTOK], BF16)
    rn = sb("rn", [128, MAXTOK])
    rn2 = sb("rn2", [128, MAXTOK])
    kT = sb("kT", [128, MAXTOK], BF16)
    kbT = sb("kbT", [128, MAXTOK], BF16)
    qT = sb("qT", [128, MAXTOK], BF16)
    vT = sb("vT", [128, MAXTOK], BF16)
    zbs = sb("zbs", [128, MAXTOK], BF16)
    Vb = sb("Vb", [128, 5, 128], BF16)
    Kd = sb("Kd", [128, 5, 128], BF16)
    S32_all = sb("S32_all", [128, depth, 8, 128])
    S_bf = sb("S_bf", [128, 8, 128], BF16)
    S0f = sb("S0f", [128, 8, 128])
    S0b = sb("S0b", [128, NSEQ, 128], BF16)
    Kdm = sb("Kdm", [64, 128], BF16)
    NSETS = int(os.environ.get("KSETS", "4"))
    tsets = []
    for si in range(NSETS):
        tsets.append((
            sb(f"Gm{si}", [128, 128]), sb(f"eET{si}", [128, 128]), sb(f"eE{si}", [128, 128]), sb(f"Grow{si}", [128, 128]),
            sb(f"DTsn{si}", [128, 128]), sb(f"DTi{si}", [128, 128]), sb(f"Dsn{si}", [128, 128]),
            sb(f"qgT{si}", [128, 128], BF16),
            [sb(f"Mb{si}_{i}", [128, 128], BF16) for i in range(2)],
            [sb(f"Nb{si}_{i}", [128, 128], BF16) for i in range(2)],
            [sb(f"Xb{si}_{i}", [128, 128], BF16) for i in range(2)],
            sb(f"PTb{si}", [128, 128], BF16), sb(f"Rb{si}", [128, 128], BF16), sb(f"Vnb{si}", [128, 128], BF16),
            sb(f"KSTb{si}", [128, 64], BF16), sb(f"sqo{si}", [128, 128], BF16), sb(f"rro{si}", [128, 128]),
            sb(f"t1o{si}", [128, 128])))
    ev32 = [sb(f"ev32_{i}", [128, 512]) for i in range(2)]

    psum = [nc.alloc_psum_tensor(f"ps{i}", [128, 512], F32) for i in range(8)]

    T = {}

    def tok(name):
        if name not in T:
            T[name] = Buf(name)
        return T[name]

    tX = [tok(f"x{i}") for i in range(5)]
    tH = [tok(f"h{i}") for i in range(5)]
    tPt = [tok(f"Pt{i}") for i in range(5)]
    tPj = [tok(f"Pj{i}") for i in range(8)]
    tQ = [tok(f"Q{i}") for i in range(8)]
    tPS = [tok(f"ps{i}") for i in range(8)]
    tW = [tok(f"w{i}") for i in range(NSLOT)]
    dsW = [S.dsem(f"w{i}") for i in range(NSLOT)]
    dsWh = [S.dsem(f"wh{i}") for i in range(NSLOT)]
    state = {"bank": 0, "slot": 0}

    def bank():
        i = state["bank"]
        state["bank"] = (i + 1) % 8
        return psum[i], tPS[i]

    def slot(hw=False):
        i = state["slot"]
        state["slot"] = (i + 1) % NSLOT
        return wslot[i], tW[i], (dsWh[i] if hw else dsW[i])

    ds_c = S.dsem("const")
    ds_c2 = S.dsem("const_sw")
    ds_x = [S.dsem(f"x{i}") for i in range(5)]
    ds_row = [S.dsem("rowA"), S.dsem("rowB")]
    ds_ba = S.dsem("wba")
    _ds_named = {}

    def ds_for(name):
        if name not in _ds_named:
            _ds_named[name] = S.dsem(name)
        return _ds_named[name]

    V, A, PE, PO = nc.vector, nc.scalar, nc.tensor, nc.gpsimd

    def act(out, in_, func, R, W, **kw):
        S.op("act", lambda: A.activation(out=out, in_=in_, func=func, **kw), R, W)

    def tt(out, in0, in1, op, R, W, eng="dve"):
        e = V if eng == "dve" else PO
        S.op(eng, lambda: e.tensor_tensor(out=out, in0=in0, in1=in1, op=op), R, W)

    def ts(out, in0, s1, s2, op0, op1, R, W):
        if op1 is None:
            S.op("dve", lambda: V.tensor_scalar(out=out, in0=in0, scalar1=s1, scalar2=None, op0=op0), R, W)
        else:
            S.op("dve", lambda: V.tensor_scalar(out=out, in0=in0, scalar1=s1, scalar2=s2, op0=op0, op1=op1), R, W)

    def stt(out, in0, scalar, in1, op0, op1, R, W):
        S.op("dve", lambda: V.scalar_tensor_tensor(out=out, in0=in0, scalar=scalar, in1=in1, op0=op0, op1=op1), R, W)

    def cp(out, in_, R, W, eng="dve"):
        if eng == "act":
            S.op("act", lambda: A.copy(out=out, in_=in_), R, W)
        else:
            S.op("dve", lambda: V.tensor_copy(out=out, in_=in_), R, W)

    def mm(out, pairs, R, W, first=True, last=True):
        n = len(pairs)
        fns = []
        for i, (l, r) in enumerate(pairs):
            fns.append(lambda l=l, r=r, i=i: PE.matmul(out, l, r, start=(first and i == 0), stop=(last and i == n - 1),
                                                      skip_group_check=True))
        S.ops("pe", fns, R, W)

    def transpose(out, in_, idn, R, W):
        S.op("pe", lambda: PE.transpose(out, in_, idn), R, W)

    tC = tok("const")
    S.dma("sp", cst[:], consts, ds_c, W=[tC])
    S.dma("pool", ident_bf[:], consts[:, 0, :], ds_c2, W=[tok("c2")])
    S.dma("pool", sel_bf[:], selc, ds_c2, W=[tok("c2")])
    S.dma("sp", seqsel_sb[:], seqsel, ds_c, W=[tC])
    S.dma("sp", scT[:], cT, ds_c, W=[tC])
    S.dma("sp", badaT[:], b_adaT, ds_c, W=[tC])
    S.dma("sp", cw_sb[:], cw, ds_c, W=[tC])
    S.dma("sp", ogc_sb[:], ogc, ds_c, W=[tC])
    S.dma("sp", alog_sb[:], alog.partition_broadcast(128), ds_c, W=[tC])
    S.dma("sp", dtb_sb[:], dtb.partition_broadcast(128), ds_c, W=[tC])
    for e_ in ("pe", "act", "dve", "pool", "sp"):
        S.eng[e_].wait_ge(ds_c.sem, ds_c.count)
        S.eng[e_].wait_ge(ds_c2.sem, ds_c2.count)
    tC.w = None
    MASKS = {
        False: dict(Lincl=cst[:, 1, :], Ustr=cst[:, 2, :], mSTn=cst[:, 3, :], mIT=cst[:, 4, :], mSNn=cst[:, 5, :]),
        True: dict(Lincl=cst[:, 6, :], Ustr=cst[:, 7, :], mSTn=cst[:, 8, :], mIT=cst[:, 9, :], mSNn=cst[:, 10, :]),
    }
    tMisc = tok("misc")
    S.op("dve", lambda: V.memset(ones_bf[:], 1.0), W=[tMisc])
    S.op("dve", lambda: V.memset(ones_f[:], 1.0), W=[tMisc])
    S.op("dve", lambda: V.memset(ones_row[:], 1.0), W=[tMisc])
    S.op("dve", lambda: V.memset(eps_ln[:], LN_EPS), W=[tMisc])
    S.op("dve", lambda: V.memset(eps_nm[:], NORM_EPS), W=[tMisc])
    S.op("dve", lambda: V.memset(one_c[:], 1.0), W=[tMisc])
    S.op("dve", lambda: V.memset(halo_all[:], 0.0), W=[tok("halo")])
    S.op("dve", lambda: V.memset(S32_all[:], 0.0), W=[tok("S32")])
    tSc = tok("scT")
    act(scT[:], scT[:], AF.Silu, [tC], [tSc])
    cp(scT_rep[:], scT[:, :, 0:1].to_broadcast([128, 8, 128]), [tSc], [tok("scTrep")])
    cp(scT_s4[:].rearrange("p k (s t) -> p k s t", t=4), scT[:, :, 1:17].unsqueeze(3).to_broadcast([128, 8, 16, 4]),
       [tSc], [tok("scTs4")])
    act(nA_sb[:], alog_sb[:], AF.Exp, [tC], [tok("nA")])
    ts(nA_sb[:], nA_sb[:], -1.0, None, ALU.mult, None, [tok("nA")], [tok("nA")])

    def rsqrt_small(out, in_, eps_t, scale, R, W, n=128):
        act(out, in_, AF.Ln, R, W, bias=eps_t[:n, :], scale=scale)
        act(out, out, AF.Exp, W, W, scale=-0.5)

    def layer_norm_stats(src, pt, Rt):
        tS = tok("st6")
        S.op("dve", lambda: V.bn_stats(st6[:pt, 0, :], src[:, 0:512]), Rt, [tS])
        S.op("dve", lambda: V.bn_stats(st6[:pt, 1, :], src[:, 512:1024]), Rt + [tS], [tS])
        S.op("dve", lambda: V.bn_aggr(mv[:pt, :], st6[:pt, :, :]), [tS], [tok("mv")])
        rsqrt_small(rstd[:pt, :], mv[:pt, 1:2], eps_ln, 1.0, [tok("mv")], [tok("rstd")], n=pt)
        stt(nbias[:pt, :], mv[:pt, 0:1], -1.0, rstd[:pt, :], ALU.mult, ALU.mult, [tok("mv"), tok("rstd")], [tok("nbias")])

    wscr = nc.dram_tensor("wscr", [depth * 42, 128, 4096], BF16).ap()
    wcache = {}
    cur = {"skip": False}
    hwds = {tW[i_]: dsWh[i_] for i_ in range(NSLOT)}
    ds_wb = S.dsem("wb")

    def fill_begin(key, ws_, tw_):
        cur["key"], cur["ws"], cur["tw"] = key, ws_, tw_
        if key in wcache:
            S.dma("sp", ws_[:].bitcast(BF16).rearrange("p k c -> p (k c)"), wscr[wcache[key]], hwds[tw_],
                  R=[tok("wc%d" % wcache[key])], W=[tw_])
            cur["skip"] = True
        else:
            cur["skip"] = False

    def fill_end():
        if not cur["skip"]:
            idx = len(wcache)
            wcache[cur["key"]] = idx
            S.dma("sp", wscr[idx], cur["ws"][:].bitcast(BF16).rearrange("p k c -> p (k c)"), ds_wb, R=[cur["tw"]],
                  W=[tok("wc%d" % idx)])
        cur["skip"] = False

    def load_w(dst3, src_l, c0, ncols, tW_, dsW_, q="pool"):
        if cur["skip"]:
            return
        S.dma(q, dst3, src_l.rearrange("(k p) c -> p k c", p=128)[:, :, c0:c0 + ncols], dsW_, W=[tW_])

    out_ds_i = [0]

    def out_dma(dst, src, R):
        S.dma("sp", dst, src, ds_for("o_" + R[0].name), R=R)

    for l in range(depth):
        for g in range(8):
            ws, tw, dw = slot(hw=True)
            S.dma("sp", ws[:], w_ada[l].rearrange("(k p) c -> p k c", p=128)[:, :, g * 256:(g + 1) * 256], dw, W=[tw])
            for c2 in range(2):
                ct = g * 2 + c2
                ps, tp = bank()
                mm(ps[:, 0:17], [(ws[:, k, c2 * 128:(c2 + 1) * 128], scT[:, k, :]) for k in range(8)], [tw, tSc], [tp])
                ts(modT_all[:, l, ct, :], ps[:, 0:17], badaT[:, l, ct:ct + 1], None, ALU.add, None, [tp, tC], [tok("modT")])
    ts(opsT_all[:], modT_all[:, :, 8:16, :], 1.0, None, ALU.add, None, [tok("modT")], [tok("opsT")])

    def main_loop():
      chk(0)
      for gi, tiles in enumerate(groups):
          has_s = 16 in tiles
          nt = len(tiles)
          blocks = [(0, 512, list(range(4)))] + ([(512, 64, [4])] if has_s else [])
          tinfo = []
          for li, t in enumerate(tiles):
              tinfo.append((li, t, li * 128, 128 if t < 16 else 64, t == 16))
          for (li, t, c0, pt, smp) in tinfo:
              S.dma("sp", x_sb[:pt, li, :], xin[t * 128:t * 128 + pt, :], ds_x[li], W=[tX[li]])

          for l in range(depth):
              last_layer = (l == depth - 1)
              modT = modT_all[:, l]
              opsT = opsT_all[:, l]
              halo = halo_all[:, l]
              S32 = S32_all[:, l]
              wsT_f = scr[0][:, :].rearrange("p (h t) -> p h t", h=8)
              S.dma("sp", wsT_f, w_sT[l], ds_for("l_wsT"), W=[tok("scr0")])
              tt(wsT_bf[:], wsT_f, MASKS[False]["mIT"].unsqueeze(1).to_broadcast([128, 8, 128]), ALU.mult,
                 [tok("scr0"), tC], [tok("wsT")])
              S.dma("pool", bsr_bf[:], bsr[:, l * 1024:(l + 1) * 1024], ds_for("l_bsr"), W=[tok("bsr")])
              if has_s:
                  wsTs_f = scr[1][:64, 0:512].rearrange("p (h t) -> p h t", h=8)
                  S.dma("sp", wsTs_f, w_sTs[l], ds_for("l_wsTs"), W=[tok("scr1")])
                  tt(wsTs_bf[:], wsTs_f, MASKS[True]["mIT"][:64, :64].unsqueeze(1).to_broadcast([64, 8, 64]), ALU.mult,
                     [tok("scr1"), tC], [tok("wsTs")])
                  S.dma("pool", bsrs_bf[:], bsrs[:, l * 512:(l + 1) * 512], ds_for("l_bsrs"), W=[tok("bsrs")])
                  S.dma("pool", scv_bf[:], sconv[l], ds_for("l_scv"), W=[tok("scv")])
              cp(S_bf[:], S32, [tok("S32")], [tok("Sbf")], eng="act")
              S.dma("sp", rowA[:], b_gate[l:l + 1, :].partition_broadcast(128), ds_row[0], W=[tok("rowA")])
              for g in range(4):
                  ws, tw, dw = slot(hw=True)
                  S.dma("sp", ws[:], w_ada[l].rearrange("(k p) c -> p k c", p=128)[:, :, 2048 + g * 256:2048 + (g + 1) * 256],
                        dw, W=[tw])
                  c0g = g * 256
                  ps, tp = bank()
                  mm(ps[:, 0:256], [(scT_rep[:, k, :], ws[:, k, :]) for k in range(8)], [tw, tok("scTrep")], [tp])
                  tt(gate_p[:, c0g:c0g + 256], ps[:, 0:256], rowA[:, c0g:c0g + 256], ALU.add, [tp, tok("rowA")], [tok("gate_p")])
                  if has_s:
                      ps, tp = bank()
                      mm(ps[:64, 0:256], [(scT_s4[:, k, :], ws[:, k, :]) for k in range(8)], [tw, tok("scTs4")], [tp])
                      tt(gate_s[:64, c0g:c0g + 256], ps[:64, 0:256], rowA[:64, c0g:c0g + 256], ALU.add, [tp, tok("rowA")],
                         [tok("gate_s")])

              chk(1)
              for (li, t, c0, pt, smp) in tinfo:
                  layer_norm_stats(x_sb[:pt, li, :], pt, [tX[li]])
                  act(xn_bf[:pt, :], x_sb[:pt, li, :], AF.Identity, [tX[li], tok("rstd"), tok("nbias")], [tok("xn")],
                      bias=nbias[:pt, :], scale=rstd[:pt, :])
                  ps, tp = bank()
                  psb = ps[:].bitcast(BF16)
                  for k in range(8):
                      transpose(psb[:, k * 128:k * 128 + pt], xn_bf[:pt, k * 128:(k + 1) * 128], ident_bf[:pt, :pt],
                                [tok("xn"), tC], [tp])
                  for k in range(8):
                      if not smp:
                          act(hT[:, k, c0:c0 + pt], psb[:, k * 128:k * 128 + pt], AF.Identity,
                              [tp, tok("opsT"), tok("modT")], [tH[li]], bias=modT[:, k, 0:1], scale=opsT[:, k, 0:1])
                      else:
                          tt(ev32[0][:, 0:64].rearrange("p (s t) -> p s t", t=4),
                             psb[:, k * 128:k * 128 + 64].rearrange("p (s t) -> p s t", t=4),
                             opsT[:, k, 1:17].unsqueeze(2).to_broadcast([128, 16, 4]), ALU.mult,
                             [tp, tok("opsT")], [tok("ev0")])
                          tt(hT[:, k, c0:c0 + 64].rearrange("p (s t) -> p s t", t=4),
                             ev32[0][:, 0:64].rearrange("p (s t) -> p s t", t=4),
                             modT[:, k, 1:17].unsqueeze(2).to_broadcast([128, 16, 4]), ALU.add,
                             [tok("ev0"), tok("modT")], [tH[li]])

              chk(2)
              S.dma("sp", rowA[:], lnv_g[l:l + 1, :].partition_broadcast(128), ds_row[0], W=[tok("rowA")])
              S.dma("sp", rowB[:], lnv_b[l:l + 1, :].partition_broadcast(128), ds_row[1], W=[tok("rowB")])
              wv = []
              for hf in range(2):
                  ws, tw, dw = slot()
                  wsb = ws[:].bitcast(BF16)
                  fill_begin((l, "P1", hf), ws, tw)
                  load_w(wsb, w_in[l], 1024 + hf * 512, 512, tw, dw)
                  fill_end()
                  wv.append((wsb, tw))
              for (li, t, c0, pt, smp) in tinfo:
                  for hf in range(2):
                      ps, tp = bank()
                      mm(ps[:pt, :], [(hT[:, k, c0:c0 + pt], wv[hf][0][:, k, :]) for k in range(8)],
                         [tH[li], wv[hf][1]], [tp])
                      act(scr[0][:pt, hf * 512:(hf + 1) * 512], ps[:pt, :], AF.Gelu_apprx_tanh, [tp], [tok("scr0")])
                  layer_norm_stats(scr[0][:pt, :], pt, [tok("scr0")])
                  act(scr[1][:pt, :], scr[0][:pt, :], AF.Identity, [tok("scr0"), tok("rstd"), tok("nbias")], [tok("scr1")],
                      bias=nbias[:pt, :], scale=rstd[:pt, :])
                  tt(scr[1][:pt, :], scr[1][:pt, :], rowA[:pt, :], ALU.mult, [tok("scr1"), tok("rowA")], [tok("scr1")])
                  if t >= 15:
                      tt(scr[1][:pt, :], scr[1][:pt, :], rowB[:pt, :], ALU.add, [tok("scr1"), tok("rowB")], [tok("scr1")])
                      out_dma(o_cvp[l] if t == 15 else o_cvs[l], scr[1][:pt, :], [tok("scr1")])
                      cp(bufP[:pt, li * D:(li + 1) * D], scr[1][:pt, :], [tok("scr1")], [tPt[li]] + tPj)
                  else:
                      tt(bufP[:pt, li * D:(li + 1) * D], scr[1][:pt, :], rowB[:pt, :], ALU.add, [tok("scr1"), tok("rowB")],
                         [tPt[li]] + tPj)

              chk(3)
              for j in range(8):
                  ws, tw, dw = slot()
                  wsb = ws[:].bitcast(BF16)
                  fill_begin((l, "P2", j), ws, tw)
                  load_w(wsb[:, :, 0:128], w_in[l], j * 128, 128, tw, dw)
                  load_w(wsb[:, :, 128:256], w_in[l], 2048 + j * 128, 128, tw, dw)
                  fill_end()
                  for (b0, bn, lis) in blocks:
                      pu, tpu = bank()
                      mm(pu[:, 0:bn], [(wsb[:, k, 0:128], hT[:, k, b0:b0 + bn]) for k in range(8)],
                         [tw] + [tH[i] for i in lis], [tpu])
                      pz, tpz = bank()
                      mm(pz[:, 0:bn], [(wsb[:, k, 128:256], hT[:, k, b0:b0 + bn]) for k in range(8)],
                         [tw] + [tH[i] for i in lis], [tpz])
                      pS, tpS = bank()
                      for li in lis:
                          (_, t, c0, pt, smp) = tinfo[li]
                          if not smp:
                              wmat = wsT_bf[:, j, :]
                              brow = bsr_bf[0:1, j * 128:(j + 1) * 128]
                              tws = [tok("wsT"), tok("bsr")]
                          else:
                              wmat = wsTs_bf[:, j, :]
                              brow = bsrs_bf[0:1, j * 64:(j + 1) * 64]
                              tws = [tok("wsTs"), tok("bsrs")]
                          mm(pS[:, c0 - b0:c0 - b0 + pt],
                             [(bufP[:pt, li * D + j * 128:li * D + (j + 1) * 128], wmat),
                              (ones_row[0:1, :], brow)], [tPt[li], tC, tMisc] + tws, [tpS])
                      act(ev32[0][:, 0:bn], pu[:, 0:bn], AF.Gelu_apprx_tanh, [tpu], [tok("ev0")])
                      act(ev32[1][:, 0:bn], pz[:, 0:bn], AF.Silu, [tpz], [tok("ev1")])
                      tt(ev32[0][:, 0:bn], ev32[0][:, 0:bn], pS[:, 0:bn], ALU.mult, [tok("ev0"), tpS], [tok("ev0")])
                      tt(bufQ[:, j, b0:b0 + bn], ev32[0][:, 0:bn], ev32[1][:, 0:bn], ALU.mult, [tok("ev0"), tok("ev1")],
                         [tQ[j]])

              chk(4)
              mTv = bufP[:, 0:8 * MAXTOK].rearrange("p (j n) -> p j n", j=8)
              for j in range(8):
                  ws, tw, dw = slot()
                  wsb = ws[:].bitcast(BF16)
                  fill_begin((l, "P3", j), ws, tw)
                  load_w(wsb[:, :, 0:128], w_pa[l], j * 128, 128, tw, dw)
                  load_w(wsb[:, :, 128:256], w_in[l], 7184 + j * 128, 128, tw, dw)
                  fill_end()
                  for (b0, bn, lis) in blocks:
                      pa, tpa = bank()
                      mm(pa[:, 0:bn], [(wsb[:, k, 0:128], bufQ[:, k, b0:b0 + bn]) for k in range(8)], [tw] + tQ, [tpa])
                      pg, tpg = bank()
                      mm(pg[:, 0:bn], [(wsb[:, k, 128:256], hT[:, k, b0:b0 + bn]) for k in range(8)],
                         [tw] + [tH[i] for i in lis], [tpg])
                      act(ev32[0][:, 0:bn], pg[:, 0:bn], AF.Sigmoid, [tpg], [tok("ev0")])
                      tt(mTv[:, j, b0:b0 + bn], ev32[0][:, 0:bn], pa[:, 0:bn], ALU.mult, [tok("ev0"), tpa], [tPj[j]] + tPt)

              chk(5)
              S.dma("pool", wba[:], w_in[l].rearrange("(k p) c -> p k c", p=128)[:, :, 7168:7184], ds_ba, W=[tok("wba")])
              for (b0, bn, lis) in blocks:
                  ps, tp = bank()
                  mm(ps[:16, 0:bn], [(wba[:, k, :], hT[:, k, b0:b0 + bn]) for k in range(8)],
                     [tok("wba")] + [tH[i] for i in lis], [tp])
                  act(betaT[:, b0:b0 + bn], ps[:16, 0:bn], AF.Sigmoid, [tp], [tok("betaT")])
              psba, tpba = bank()
              for (li, t, c0, pt, smp) in tinfo:
                  mm(psba[:pt, li * 16:(li + 1) * 16], [(hT[:, k, c0:c0 + pt], wba[:, k, :]) for k in range(8)],
                     [tok("wba"), tH[li]], [tpba])
              for (li, t, c0, pt, smp) in tinfo:
                  act(beta_tok[:pt, li, :], psba[:pt, li * 16:li * 16 + 8], AF.Sigmoid, [tpba], [tok("beta_tok")])
                  tt(apre[:pt, li, :], psba[:pt, li * 16 + 8:li * 16 + 16], dtb_sb[:pt, l * 8:(l + 1) * 8], ALU.add,
                     [tpba, tC], [tok("apre")])
              for (li, t, c0, pt, smp) in tinfo:
                  act(apre[:pt, li, :], apre[:pt, li, :], AF.Exp, [tok("apre")], [tok("apre")])
                  act(apre[:pt, li, :], apre[:pt, li, :], AF.Ln, [tok("apre")], [tok("apre")], bias=one_c[:pt, :], scale=1.0)
                  tt(g_tok[:pt, li, :], apre[:pt, li, :], nA_sb[:pt, l * 8:(l + 1) * 8], ALU.mult, [tok("apre"), tok("nA")],
                     [tok("g_tok")])
                  mk = MASKS[smp]
                  ps, tp = bank()
                  mm(ps[:pt, 0:8], [(mk["Lincl"][:pt, :pt], g_tok[:pt, li, :])], [tok("g_tok"), tC], [tp])
                  mm(ps[:pt, 8:16], [(mk["Ustr"][:pt, :pt], g_tok[:pt, li, :])], [tok("g_tok"), tC], [tp])
                  if not smp:
                      mm(ps[:, 16:24], [(ones_f[:pt, :], g_tok[:pt, li, :])], [tok("g_tok"), tMisc], [tp])
                      act(GC[:, li, :], ps[:, 16:24], AF.Exp, [tp], [tok("GC")])
                  act(gam_tok[:pt, li, :], ps[:pt, 0:8], AF.Exp, [tp], [tok("gam")])
                  act(egl[:pt, li, :], ps[:pt, 8:16], AF.Exp, [tp], [tok("egl")])
                  stt(nbG[:pt, li, :], gam_tok[:pt, li, :], -1.0, beta_tok[:pt, li, :], ALU.mult, ALU.mult,
                      [tok("gam"), tok("beta_tok")], [tok("nbG")])
                  if smp:
                      tt(G2[:], g_tok[:64, li, :].unsqueeze(1).to_broadcast([64, 16, 8]),
                         seqsel_sb[:].unsqueeze(2).to_broadcast([64, 16, 8]), ALU.mult, [tok("g_tok"), tC], [tok("G2")])
                      ps2, tp2 = bank()
                      mm(ps2[:, 0:128], [(ones_f[:64, :], G2[:].rearrange("p s h -> p (s h)"))], [tok("G2"), tMisc], [tp2])
                      act(GCs[:].rearrange("p s h -> p (s h)"), ps2[:, 0:128], AF.Exp, [tp2], [tok("GCs")])

              chk(5.1)
              csq = scr[0][:, 0:MAXTOK]
              csk = scr[1][:, 0:MAXTOK]
              for h in range(8):
                  ws, tw, dw = slot()
                  wsb = ws[:].bitcast(BF16)
                  fill_begin((l, "P4", h), ws, tw)
                  for ci, cbase in enumerate((3072, 4096, 5120, 6144)):
                      load_w(wsb[:, :, ci * 128:(ci + 1) * 128], w_in[l], cbase + h * 128, 128, tw, dw)
                  fill_end()
                  if has_s:
                      S.dma("pool", S0b[:], sssm[l, :, h].rearrange("s d v -> d s v"), ds_for("S0b"), W=[tok("S0b")])
                  for ci in range(3):
                      for jt in range(4):
                          ts(diag[:, ci * 4 + jt, :], ident_bf[:], cw_sb[:, l, ci * 8 + h, jt:jt + 1], None, ALU.mult, None,
                             [tC], [tok("diag")])
                  ssps = {}
                  for ci in range(3):
                      ctg = ci * 8 + h
                      dst = (csq, csk, vT)[ci]
                      tdst = (tok("scr0"), tok("scr1"), tok("vT"))[ci]
                      for (b0, bn, lis) in blocks:
                          smpb = (bn == 64)
                          pp, tpp = bank()
                          mm(pp[:, 0:bn], [(wsb[:, k, ci * 128:(ci + 1) * 128], hT[:, k, b0:b0 + bn]) for k in range(8)],
                             [tw] + [tH[i] for i in lis], [tpp])
                          pc, tpc = bank()
                          if not smpb:
                              cp(xpre[:, 0:3], halo[:, ctg, :], [tok("halo")], [tok("xpre")])
                              cp(xpre[:, 3:3 + bn], pp[:, 0:bn], [tpp], [tok("xpre")], eng="act")
                              cp(halo[:, ctg, :], xpre[:, bn:bn + 3], [tok("xpre")], [tok("halo")])
                              mm(pc[:, 0:bn], [(diag[:, ci * 4 + jt, :], xpre[:, jt:jt + bn]) for jt in range(4)],
                                 [tok("diag"), tok("xpre")], [tpc])
                          else:
                              cp(xps[:, :, 0:3], scv_bf[:, ctg, :, :], [tok("scv")], [tok("xps")])
                              cp(xps[:, :, 3:7], pp[:, 0:64].rearrange("p (s t) -> p s t", t=4), [tpp], [tok("xps")],
                                 eng="act")
                              mm(pc[:, 0:64].rearrange("p (s t) -> p s t", t=4),
                                 [(diag[:, ci * 4 + jt, :], xps[:, :, jt:jt + 4]) for jt in range(4)],
                                 [tok("diag"), tok("xps")], [tpc])
                          act(dst[:, b0:b0 + bn], pc[:, 0:bn], AF.Silu, [tpc], [tdst])
                          if ci < 2:
                              act(sqb[:, b0:b0 + bn], dst[:, b0:b0 + bn], AF.Square, [tdst], [tok("sqb")])
                              pss, tpss = bank()
                              mm(pss[:, 0:bn], [(ones_bf[:], sqb[:, b0:b0 + bn])], [tok("sqb"), tMisc], [tpss])
                              rsq_t = tok(f"rn{ci}")
                              rnb = (rn, rn2)[ci]
                              ssps[(ci, b0)] = (pss, tpss)
                              cp(rnb[:, b0:b0 + bn], pss[:, 0:bn], [tpss], [rsq_t])
                  chk(5.3)
                  for (b0, bn, lis) in blocks:
                      pz, tpz = bank()
                      mm(pz[:, 0:bn], [(wsb[:, k, 384:512], hT[:, k, b0:b0 + bn]) for k in range(8)],
                         [tw] + [tH[i] for i in lis], [tpz])
                      act(zbs[:, b0:b0 + bn], pz[:, 0:bn], AF.Silu, [tpz], [tok("zbs")])
                  for ci in range(2):
                      rnb = (rn, rn2)[ci]
                      rsq_t = tok(f"rn{ci}")
                      ntk = 512 + (64 if has_s else 0)
                      rsqrt_small(rnb[:, 0:ntk], rnb[:, 0:ntk], eps_nm, 1.0, [rsq_t], [rsq_t])
                      for (b0, bn, lis) in blocks:
                          if ci == 0:
                              stt(qT[:, b0:b0 + bn], csq[:, b0:b0 + bn], 128.0 ** -0.5, rnb[:, b0:b0 + bn], ALU.mult, ALU.mult,
                                  [tok("scr0"), rsq_t], [tok("qT")])
                          else:
                              tt(kT[:, b0:b0 + bn], csk[:, b0:b0 + bn], rnb[:, b0:b0 + bn], ALU.mult, [tok("scr1"), rsq_t],
                                 [tok("kT")])
                              pb_, tpb_ = bank()
                              mm(pb_[:, 0:bn], [(sel_bf[:, h, :], betaT[:, b0:b0 + bn])], [tok("betaT"), tC], [tpb_])
                              tt(kbT[:, b0:b0 + bn], kT[:, b0:b0 + bn], pb_[:, 0:bn], ALU.mult, [tok("kT"), tpb_],
                                 [tok("kbT")])
                  for (li, t, c0, pt, smp) in tinfo:
                      ps, tp = bank()
                      psb = ps[:].bitcast(BF16)
                      transpose(psb[:pt, 0:128], vT[:, c0:c0 + pt], ident_bf[:], [tok("vT"), tC], [tp])
                      transpose(psb[:pt, 128:256], kT[:, c0:c0 + pt], ident_bf[:], [tok("kT"), tC], [tp])
                      ts(Vb[:pt, li, :], psb[:pt, 0:128], beta_tok[:pt, li, h:h + 1], None, ALU.mult, None,
                         [tp, tok("beta_tok")], [tok("Vb")])
                      ts(Kd[:pt, li, :], psb[:pt, 128:256], egl[:pt, li, h:h + 1], None, ALU.mult, None,
                         [tp, tok("egl")], [tok("Kd")])

                  chk(5.4)
                  def tile_gen(li, t, c0, pt, smp, si, done):
                      (Gm, eET, eE, Grow, DTsn, DTi, Dsn, qgT, Mb, Nb, Xb, PTb, Rb, Vnb, KSTb, sqo, rro, t1o) = tsets[si]
                      mk = MASKS[smp]
                      yield
                      ts(Gm[:pt, :pt], mk["Lincl"][:pt, :pt], g_tok[:pt, li, h:h + 1], None, ALU.mult, None,
                         [tC, tok("g_tok")], [tok("Gm_%d" % si)])
                      pe_, tpe = bank()
                      yield
                      mm(pe_[:pt, 0:pt], [(mk["Ustr"][:pt, :pt], Gm[:pt, :pt])], [tC, tok("Gm_%d" % si)], [tpe])
                      yield
                      mm(pe_[:pt, 128:128 + pt], [(Gm[:pt, :pt], mk["Ustr"][:pt, :pt])], [tC, tok("Gm_%d" % si)], [tpe])
                      yield
                      mm(pe_[:, 256:256 + pt], [(ones_f[:pt, :], Gm[:pt, :pt])], [tMisc, tok("Gm_%d" % si)], [tpe])
                      yield
                      act(eET[:pt, :pt], pe_[:pt, 0:pt], AF.Exp, [tpe], [tok("eET_%d" % si)])
                      yield
                      act(eE[:pt, :pt], pe_[:pt, 128:128 + pt], AF.Exp, [tpe], [tok("eE_%d" % si)])
                      yield
                      act(Grow[:, :pt], pe_[:, 256:256 + pt], AF.Exp, [tpe], [tok("Grow_%d" % si)])
                      yield
                      tt(DTsn[:pt, :pt], eET[:pt, :pt], mk["mSTn"][:pt, :pt], ALU.mult, [tok("eET_%d" % si), tC], [tok("DTsn_%d" % si)])
                      yield
                      tt(DTi[:pt, :pt], eET[:pt, :pt], mk["mIT"][:pt, :pt], ALU.mult, [tok("eET_%d" % si), tC], [tok("DTi_%d" % si)])
                      yield
                      tt(Dsn[:pt, :pt], eE[:pt, :pt], mk["mSNn"][:pt, :pt], ALU.mult, [tok("eE_%d" % si), tC], [tok("Dsn_%d" % si)])
                      yield
                      tt(qgT[:, :pt], qT[:, c0:c0 + pt], Grow[:, :pt], ALU.mult, [tok("qT"), tok("Grow_%d" % si)], [tok("qgT_%d" % si)])
                      pa_, tpa_ = bank()
                      yield
                      mm(pa_[:pt, 0:pt], [(kT[:, c0:c0 + pt], kbT[:, c0:c0 + pt])], [tok("kT"), tok("kbT")], [tpa_])
                      yield
                      mm(pa_[:pt, 128:128 + pt], [(kbT[:, c0:c0 + pt], kT[:, c0:c0 + pt])], [tok("kT"), tok("kbT")], [tpa_])
                      yield
                      mm(pa_[:pt, 256:256 + pt], [(kT[:, c0:c0 + pt], qT[:, c0:c0 + pt])], [tok("kT"), tok("qT")], [tpa_])
                      tM = [tok("M0_%d" % si), tok("M1_%d" % si)]
                      tN = [tok("N0_%d" % si), tok("N1_%d" % si)]
                      tXx = [tok("X0_%d" % si), tok("X1_%d" % si)]
                      yield
                      tt(Mb[0][:pt, :pt], pa_[:pt, 0:pt], DTsn[:pt, :pt], ALU.mult, [tpa_, tok("DTsn_%d" % si)], [tM[0]])
                      yield
                      tt(Nb[0][:pt, :pt], pa_[:pt, 128:128 + pt], Dsn[:pt, :pt], ALU.mult, [tpa_, tok("Dsn_%d" % si)], [tN[0]])
                      yield
                      tt(PTb[:pt, :pt], pa_[:pt, 256:256 + pt], DTi[:pt, :pt], ALU.mult, [tpa_, tok("DTi_%d" % si)], [tok("PT_%d" % si)])
                      yield
                      tt(Xb[0][:pt, :pt], Mb[0][:pt, :pt], ident_bf[:pt, :pt], ALU.add, [tM[0], tC], [tXx[0]])
                      cur = 0
                      nlev = 1 if smp else 6
                      for lv in range(nlev):
                          lastlv = (lv == nlev - 1)
                          nx = 1 - cur
                          pn, tpn = bank()
                          yield
                          mm(pn[:pt, 0:pt], [(Mb[cur][:pt, :pt], Nb[cur][:pt, :pt])], [tM[cur], tN[cur]], [tpn])
                          if not lastlv:
                              yield
                              mm(pn[:pt, 128:128 + pt], [(Nb[cur][:pt, :pt], Mb[cur][:pt, :pt])], [tM[cur], tN[cur]], [tpn])
                          yield
                          cp(Nb[nx][:pt, :pt], pn[:pt, 0:pt], [tpn], [tN[nx]], eng="act")
                          if not lastlv:
                              yield
                              cp(Mb[nx][:pt, :pt], pn[:pt, 128:128 + pt], [tpn], [tM[nx]], eng="act")
                          px, tpx = bank()
                          yield
                          mm(px[:pt, 0:pt], [(ident_bf[:pt, :pt], Xb[cur][:pt, :pt]), (Nb[nx][:pt, :pt], Xb[cur][:pt, :pt])],
                             [tC, tXx[cur], tN[nx]], [tpx])
                          yield
                          cp(Xb[nx][:pt, :pt], px[:pt, 0:pt], [tpx], [tXx[nx]], eng="act")
                          cur = nx
                      Xf, tXf = Xb[cur], tXx[cur]
                      while li > 0 and not done.get(li - 1):
                          yield
                      pk, tpk = bank()
                      if not smp:
                          yield
                          mm(pk[:pt, 0:128], [(kT[:, c0:c0 + pt], S_bf[:, h, :])], [tok("kT"), tok("Sbf")], [tpk])
                          yield
                          stt(Rb[:pt, :], pk[:pt, 0:128], nbG[:pt, li, h:h + 1], Vb[:pt, li, :], ALU.mult, ALU.add,
                              [tpk, tok("nbG"), tok("Vb")], [tok("R_%d" % si)])
                      else:
                          for s in range(NSEQ):
                              yield
                              mm(pk[:, 4 * s:4 * s + 4], [(S0b[:, s, :], kT[:, c0 + 4 * s:c0 + 4 * s + 4])],
                                 [tok("kT"), tok("S0b")], [tpk])
                          yield
                          cp(KSTb[:, :], pk[:, 0:64], [tpk], [tok("KST_%d" % si)], eng="act")
                          pk2, tpk2 = bank()
                          pk2b = pk2[:].bitcast(BF16)
                          yield
                          transpose(pk2b[:64, 0:128], KSTb[:, :], ident_bf[:], [tok("KST_%d" % si), tC], [tpk2])
                          yield
                          stt(Rb[:pt, :], pk2b[:64, 0:128], nbG[:pt, li, h:h + 1], Vb[:pt, li, :], ALU.mult, ALU.add,
                              [tpk2, tok("nbG"), tok("Vb")], [tok("R_%d" % si)])
                      pv, tpv = bank()
                      yield
                      mm(pv[:pt, 0:128], [(Xf[:pt, :pt], Rb[:pt, :])], [tXf, tok("R_%d" % si)], [tpv])
                      yield
                      cp(Vnb[:pt, :], pv[:pt, 0:128], [tpv], [tok("Vn_%d" % si)], eng="act")
                      po, tpo = bank()
                      if not smp:
                          yield
                          mm(po[:, 0:pt], [(S_bf[:, h, :], qgT[:, :pt]), (Vnb[:pt, :], PTb[:pt, :pt])],
                             [tok("Sbf"), tok("qgT_%d" % si), tok("Vn_%d" % si), tok("PT_%d" % si)], [tpo])
                          pd, tpd = bank()
                          yield
                          mm(pd[:, 0:128], [(Kd[:pt, li, :], Vnb[:pt, :])], [tok("Kd"), tok("Vn_%d" % si)], [tpd])
                          yield
                          stt(S32[:, h, :], S32[:, h, :], GC[:, li, h:h + 1], pd[:, 0:128], ALU.mult, ALU.add,
                              [tok("S32"), tok("GC"), tpd], [tok("S32")])
                          yield
                          cp(S_bf[:, h, :], S32[:, h, :], [tok("S32")], [tok("Sbf")], eng="act")
                          if t == 15:
                              yield
                              out_dma(o_ssmp[l, h], S32[:, h, :], [tok("S32")])
                      else:
                          yield
                          mm(po[:, 0:64], [(Vnb[:64, :], PTb[:64, :64])], [tok("Vn_%d" % si), tok("PT_%d" % si)], [tpo], first=True, last=False)
                          for s in range(NSEQ):
                              yield
                              mm(po[:, 4 * s:4 * s + 4], [(S0b[:, s, :], qgT[:, 4 * s:4 * s + 4])], [tok("S0b"), tok("qgT_%d" % si)],
                                 [tpo], first=False, last=(s == NSEQ - 1))
                          for sh in range(2):
                              yield
                              S.dma("sp", S0f[:], sssm[l, 8 * sh:8 * sh + 8, h].rearrange("s d v -> d s v"), ds_for("S0f"),
                                    W=[tok("S0f")])
                              for s4 in range(2):
                                  pd, tpd = bank()
                                  for sq_ in range(4):
                                      s = 8 * sh + 4 * s4 + sq_
                                      yield
                                      ts(Kdm[:, :], Kd[:64, li, :], seqsel_sb[:, s:s + 1], None, ALU.mult, None,
                                         [tok("Kd"), tC], [tok("Kdm")])
                                      yield
                                      mm(pd[:, sq_ * 128:(sq_ + 1) * 128], [(Kdm[:, :], Vnb[:64, :])], [tok("Kdm"), tok("Vn_%d" % si)],
                                         [tpd])
                                  for sq_ in range(4):
                                      s = 8 * sh + 4 * s4 + sq_
                                      sl = 4 * s4 + sq_
                                      yield
                                      stt(S0f[:, sl, :], S0f[:, sl, :], GCs[:, s, h:h + 1], pd[:, sq_ * 128:(sq_ + 1) * 128],
                                          ALU.mult, ALU.add, [tok("S0f"), tok("GCs"), tpd], [tok("S0f")])
                              yield
                              out_dma(o_ssms[l, 8 * sh:8 * sh + 8, h].rearrange("s d v -> d s v"), S0f[:], [tok("S0f")])
                      yield
                      act(sqo[:, :pt], po[:, 0:pt], AF.Square, [tpo], [tok("sqo_%d" % si)])
                      pr, tpr = bank()
                      yield
                      mm(pr[:, 0:pt], [(ones_bf[:], sqo[:, :pt])], [tok("sqo_%d" % si), tMisc], [tpr])
                      yield
                      rsqrt_small(rro[:, :pt], pr[:, 0:pt], eps_nm, 1.0 / 128.0, [tpr], [tok("rro_%d" % si)])
                      yield
                      tt(t1o[:, :pt], po[:, 0:pt], rro[:, :pt], ALU.mult, [tpo, tok("rro_%d" % si)], [tok("t1o_%d" % si)])
                      yield
                      stt(bufQ[:, h, c0:c0 + pt], t1o[:, :pt], ogc_sb[:, l:l + 1], zbs[:, c0:c0 + pt], ALU.mult, ALU.mult,
                          [tok("t1o_%d" % si), tC, tok("zbs")], [tQ[h]])

                      done[li] = True
                  done = {}
                  pending = list(tinfo)
                  active = []
                  free_sets = list(range(NSETS))
                  while pending or active:
                      while pending and free_sets:
                          ti_ = pending.pop(0)
                          si_ = free_sets.pop(0)
                          active.append((tile_gen(*ti_, si_, done), si_))
                      for (g_, si_) in list(active):
                          try:
                              next(g_)
                          except StopIteration:
                              active.remove((g_, si_))
                              free_sets.append(si_)
              chk(6)
              if has_s or (15 in tiles):
                  for grp6 in range(6):
                      ws, tw, dw = slot()
                      wsb = ws[:].bitcast(BF16)
                      fill_begin((l, "CO", grp6), ws, tw)
                      load_w(wsb, w_in[l], 3072 + grp6 * 512, 512, tw, dw)
                      fill_end()
                      if has_s:
                          ps, tp = bank()
                          mm(ps[:64, :], [(hT[:, k, 512:576], wsb[:, k, :]) for k in range(8)], [tw, tH[4]], [tp])
                          cp(ev32[0][:64, :], ps[:64, :], [tp], [tok("ev0")], eng="act")
                          for tq in range(1, 4):
                              out_dma(o_convs[l, :, tq - 1, grp6 * 512:(grp6 + 1) * 512], ev32[0][tq:64:4, :], [tok("ev0")])
                      if 15 in tiles:
                          l15 = tiles.index(15)
                          ps2, tp2 = bank()
                          mm(ps2[:, :], [(hT[:, k, l15 * 128:(l15 + 1) * 128], wsb[:, k, :]) for k in range(8)], [tw, tH[l15]], [tp2])
                          cp(ev32[1][64:128, :], ps2[64:128, :], [tp2], [tok("ev1")], eng="act")
                          out_dma(o_convp[l, :, grp6 * 512:(grp6 + 1) * 512], ev32[1][125:128, :], [tok("ev1")])

              chk(7)
              for j in range(8):
                  ws, tw, dw = slot()
                  wsb = ws[:].bitcast(BF16)
                  fill_begin((l, "P5", j), ws, tw)
                  load_w(wsb[:, :, 0:128], w_pb[l], j * 128, 128, tw, dw)
                  load_w(wsb[:, :, 128:256], w_in[l], 8208 + j * 128, 128, tw, dw)
                  fill_end()
                  for (b0, bn, lis) in blocks:
                      pa, tpa = bank()
                      mm(pa[:, 0:bn], [(wsb[:, k, 0:128], bufQ[:, k, b0:b0 + bn]) for k in range(8)], [tw] + tQ, [tpa])
                      pg, tpg = bank()
                      mm(pg[:, 0:bn], [(wsb[:, k, 128:256], hT[:, k, b0:b0 + bn]) for k in range(8)],
                         [tw] + [tH[i] for i in lis], [tpg])
                      act(ev32[0][:, 0:bn], pg[:, 0:bn], AF.Sigmoid, [tpg], [tok("ev0")])
                      tt(ev32[0][:, 0:bn], ev32[0][:, 0:bn], pa[:, 0:bn], ALU.mult, [tok("ev0"), tpa], [tok("ev0")])
                      tt(mTv[:, j, b0:b0 + bn], ev32[0][:, 0:bn], mTv[:, j, b0:b0 + bn], ALU.add, [tok("ev0"), tPj[j]],
                         [tPj[j]])

              chk(8)
              S.dma("sp", rowA[:], ln_g[l:l + 1, :].partition_broadcast(128), ds_row[0], W=[tok("rowA")])
              S.dma("sp", rowB[:], ln_b[l:l + 1, :].partition_broadcast(128), ds_row[1], W=[tok("rowB")])
              wo = []
              for hf in range(2):
                  ws, tw, dw = slot()
                  wsb = ws[:].bitcast(BF16)
                  fill_begin((l, "P6", hf), ws, tw)
                  load_w(wsb, w_o[l], hf * 512, 512, tw, dw)
                  fill_end()
                  wo.append((wsb, tw))
              for (li, t, c0, pt, smp) in tinfo:
                  gt = gate_s if smp else gate_p
                  tg = tok("gate_s") if smp else tok("gate_p")
                  for hf in range(2):
                      ps, tp = bank()
                      mm(ps[:pt, :], [(mTv[:, k, c0:c0 + pt], wo[hf][0][:, k, :]) for k in range(8)], tPj + [wo[hf][1]], [tp])
                      tt(scr[0][:pt, hf * 512:(hf + 1) * 512], ps[:pt, :], gt[:pt, hf * 512:(hf + 1) * 512], ALU.mult,
                         [tp, tg], [tok("scr0")])
                  stt(scr[0][:pt, :], x_sb[:pt, li, :], float(ALPHA), scr[0][:pt, :], ALU.mult, ALU.add, [tX[li], tok("scr0")],
                      [tok("scr0")])
                  layer_norm_stats(scr[0][:pt, :], pt, [tok("scr0")])
                  act(scr[1][:pt, :], scr[0][:pt, :], AF.Identity, [tok("scr0"), tok("rstd"), tok("nbias")], [tok("scr1")],
                      bias=nbias[:pt, :], scale=rstd[:pt, :])
                  tt(scr[1][:pt, :], scr[1][:pt, :], rowA[:pt, :], ALU.mult, [tok("scr1"), tok("rowA")], [tok("scr1")])
                  tt(x_sb[:pt, li, :], scr[1][:pt, :], rowB[:pt, :], ALU.add, [tok("scr1"), tok("rowB")], [tX[li]])
                  if last_layer:
                      if smp:
                          out_dma(y_s, x_sb[:64, li, :], [tX[li]])
                      else:
                          out_dma(y_p[t * 128:(t + 1) * 128, :], x_sb[:, li, :], [tX[li]])

    try:
        main_loop()
    except _Stop:
        pass
    for i_ in range(int(os.environ.get("KDMA", "0"))):
        S.dma("pool", wba[:], w_in[0].rearrange("(k p) c -> p k c", p=128)[:, :, 16 * (i_ % 500):16 * (i_ % 500) + 16], ds_ba,
              W=[tok("wba")])
    for i_ in range(int(os.environ.get("KDMAH", "0"))):
        S.dma("sp", rowA[:], xin[i_:i_ + 1, :].partition_broadcast(128), ds_row[0], W=[tok("rowA")])
    for i_ in range(int(os.environ.get("KACT", "0"))):
        nc.scalar.activation(out=ev32[0][0:1, 0:1], in_=one_c[0:1, 0:1], func=AF.Gelu_apprx_tanh)
        nc.scalar.activation(out=ev32[0][0:1, 0:1], in_=one_c[0:1, 0:1], func=AF.Silu)
    if os.environ.get("KINC"):
        semx = nc.alloc_semaphore("dummy_inc")
        for i_ in range(int(os.environ["KINC"])):
            nc.vector.memset(ev32[1][0:1, 0:1], 0.0).then_inc(semx, 1)
    if os.environ.get("KDUMMY"):
        for e_ in ("pe", "act", "dve"):
            for b in tPS:
                S._wait(e_, b.w)
                for ev in b.r.values():
                    S._wait(e_, ev)
        for _ in range(20000):
            nc.tensor.matmul(psum[0][0:1, 0:1], ones_row[0:1, 0:1], ones_row[0:1, 0:1], start=True, stop=True)
        for _ in range(12000):
            nc.scalar.copy(out=ev32[0][0:1, 0:1], in_=one_c[0:1, 0:1])
        for _ in range(12000):
            nc.vector.memset(ev32[1][0:1, 0:1], 0.0)
    for (sm, c) in S.final_waits:
        nc.sync.wait_ge(sm, c)
    for d in S.dsems:
        if d.count:
            nc.sync.wait_ge(d.sem, d.count)
    print("sbuf bytes remaining:", nc.sbuf_bytes_remaining, "instr counts:", S.total, "nsem:", S.nsem)
    return nc


def _consts():
    c = np.zeros((128, 12, 128), np.float32)
    i = np.arange(128)
    a, b = i[:, None], i[None, :]
    c[:, 0] = (a == b)
    c[:, 1] = (a <= b)
    c[:, 2] = (a > b)
    c[:, 3] = -1.0 * (b > a)
    c[:, 4] = (b >= a)
    c[:, 5] = -1.0 * (b < a)
    same = ((a // 4) == (b // 4)) & (a < 64) & (b < 64)
    c[:, 6] = same & (a <= b)
    c[:, 7] = same & (a > b)
    c[:, 8] = -1.0 * (same & (b > a))
    c[:, 9] = same & (b >= a)
    c[:, 10] = -1.0 * (same & (b < a))
    sel = np.zeros((16, 8, 128), np.float32)
    for h in range(8):
        sel[h, h, :] = 1.0
    seqsel = (np.arange(64)[:, None] // 4 == np.arange(16)[None, :]).astype(np.float32)
    return c, sel, seqsel


def kernel(x_prompt, x_sample, state_conv, state_ssm, c_prompt, c_sample, w_ada, b_ada, w_in,
           w_s, b_s, lnv_g, lnv_b, conv_w, a_log, dt_bias, onorm_g, w_pa, w_pb, w_o, ln_g, ln_b):
    f = lambda a: np.ascontiguousarray(np.asarray(a, dtype=np.float32))
    (x_prompt, x_sample, state_conv, state_ssm, c_prompt, c_sample, w_ada, b_ada, w_in, w_s, b_s, lnv_g, lnv_b,
     conv_w, a_log, dt_bias, onorm_g, w_pa, w_pb, w_o, ln_g, ln_b) = [f(a) for a in (
        x_prompt, x_sample, state_conv, state_ssm, c_prompt, c_sample, w_ada, b_ada, w_in, w_s, b_s, lnv_g, lnv_b,
        conv_w, a_log, dt_bias, onorm_g, w_pa, w_pb, w_o, ln_g, ln_b)]
    depth = w_in.shape[0]
    nc = build_nc(depth)
    consts, sel, seqsel = _consts()
    b_adaT = f(b_ada[:, :2048].reshape(depth, 16, 128).transpose(2, 0, 1))
    b_gate = f(b_ada[:, 2048:])
    w_sT = f(w_s.transpose(0, 3, 1, 2))
    w_sTs = np.zeros((depth, 64, 8, 64), np.float32)
    for q in range(16):
        w_sTs[:, 4 * q:4 * q + 4, :, 4 * q:4 * q + 4] = w_s[:, :, :4, :4].transpose(0, 3, 1, 2)
    bsr = f(b_s.reshape(1, -1))
    bsrs = f(np.tile(b_s[:, :, :4], (1, 1, 16)).reshape(1, -1))
    cw = f(conv_w.reshape(depth, 4, 24, 128).transpose(3, 0, 2, 1))
    alog = f(a_log.reshape(1, -1))
    dtb = f(dt_bias.reshape(1, -1))
    ogc = f(onorm_g.T)
    shared = dict(w_ada=w_ada, b_adaT=b_adaT, b_gate=b_gate, w_in=w_in, w_sT=w_sT, w_sTs=w_sTs, bsr=bsr, bsrs=bsrs,
                  lnv_g=lnv_g, lnv_b=lnv_b, cw=cw, alog=alog, dtb=dtb, ogc=ogc, w_pa=w_pa, w_pb=w_pb, w_o=w_o,
                  ln_g=ln_g, ln_b=ln_b, consts=consts, selc=sel, seqsel=seqsel)
    in_maps = []
    for i in range(NCORE):
        ss = slice(NSEQ * i, NSEQ * (i + 1))
        xin = f(np.concatenate([x_prompt[i], x_sample[ss].reshape(64, D)], axis=0))
        cc = np.concatenate([c_prompt[i:i + 1], c_sample[ss]], axis=0)
        cT = f(cc.reshape(17, 8, 128).transpose(2, 1, 0))
        sconv = f(state_conv[:, ss].reshape(depth, NSEQ, 3, 24, 128).transpose(0, 4, 3, 1, 2))
        sssm = f(state_ssm[:, ss])
        m = dict(shared)
        m.update(xin=xin, cT=cT, sconv=sconv, sssm=sssm)
        in_maps.append(m)
    import os
    if os.environ.get('KTRACE'):
        res = run_bass_kernel_spmd(nc, in_maps, core_ids=list(range(NCORE)), trace=True)
        print('EXEC_TIME_NS', res.exec_time_ns)
    else:
        res = run_bass_kernel_spmd(nc, in_maps, core_ids=list(range(NCORE)))
    R = res.results
    y_prompt = np.stack([R[i]["y_p"] for i in range(NCORE)], 0)
    y_sample = np.concatenate([R[i]["y_s"].reshape(NSEQ, 4, D) for i in range(NCORE)], 0)
    conv_p = np.stack([R[i]["o_convp"] for i in range(NCORE)], 1)
    ssm_p = np.stack([R[i]["o_ssmp"] for i in range(NCORE)], 1)
    cv_p = np.stack([R[i]["o_cvp"] for i in range(NCORE)], 1)
    conv_s = np.concatenate([R[i]["o_convs"] for i in range(NCORE)], 1)
    ssm_s = np.concatenate([R[i]["o_ssms"] for i in range(NCORE)], 1)
    cv_s = np.concatenate([R[i]["o_cvs"].reshape(depth, NSEQ, 4, D) for i in range(NCORE)], 1)
    return tuple(np.ascontiguousarray(a, dtype=np.float32) for a in
                 (y_prompt, y_sample, conv_p, ssm_p, cv_p, conv_s, ssm_s, cv_s))
```

```python
import numpy as np
import concourse.bass as bass
import concourse.mybir as mybir
from concourse.bass_utils import run_bass_kernel_spmd

F32, BF16 = mybir.dt.float32, mybir.dt.bfloat16
AF = mybir.ActivationFunctionType
ALU = mybir.AluOpType

D = 1024
DEPTH = 4
NCORE = 8
SEQ = 2048
NPT = 16
NSEQ = 16
P_IN = 9232
ALPHA = (2 * DEPTH) ** 0.25
LN_EPS = 1e-5
NORM_EPS = 1e-6
GROUPS = [[0, 1, 2, 3, 16], [4, 5, 6, 7], [8, 9, 10, 11], [12, 13, 14, 15]]
MAXTOK = 576


class Buf:
    __slots__ = ("name", "w", "r")

    def __init__(self, name):
        self.name = name
        self.w = None
        self.r = {}


class DmaSem:
    def __init__(self, sem, name):
        self.sem = sem
        self.count = 0
        self.name = name


class Sched:
    import os as _os
    EPOCH = int(_os.environ.get('KEPOCH', '8000'))

    def __init__(self, nc):
        self.nc = nc
        self.eng = {"pe": nc.tensor, "act": nc.scalar, "dve": nc.vector, "pool": nc.gpsimd, "sp": nc.sync}
        self.sem = {k: nc.alloc_semaphore("sem_" + k) for k in self.eng}
        self.epoch = {k: 0 for k in self.eng}
        self.cnt = {k: 0 for k in self.eng}
        self.total = {k: 0 for k in self.eng}
        self.known = {k: {} for k in self.eng}
        self.dsems = []
        self.final_waits = []
        self.nsem = len(self.eng)

    def dsem(self, name):
        d = DmaSem(self.nc.alloc_semaphore("ds_" + name), name)
        d.gen = 0
        self.dsems.append(d)
        self.nsem += 1
        return d

    def _wait(self, eng, ev):
        if ev is None:
            return
        if ev[0] == "e":
            _, src, ep, idx, sem = ev
            if src == eng and eng in ("pe", "sp"):
                return
            key = ("e", src)
        else:
            _, dname, ep, idx, sem = ev
            key = ("d", dname)
        kep, kval = self.known[eng].get(key, (-1, 0))
        if kep > ep or (kep == ep and kval >= idx):
            return
        self.known[eng][key] = (ep, idx)
        self.eng[eng].wait_ge(sem, idx)

    def _deps(self, eng, R, W):
        for b in R:
            self._wait(eng, b.w)
        for b in W:
            self._wait(eng, b.w)
            for ev in b.r.values():
                if ev[0] == "e" and ev[1] == eng and eng == "pe":
                    continue
                self._wait(eng, ev)

    def ops(self, eng, fns, R=(), W=()):
        self._deps(eng, R, W)
        if self.cnt[eng] >= self.EPOCH:
            self.sem[eng] = self.nc.alloc_semaphore(f"sem_{eng}_{self.epoch[eng] + 1}")
            self.epoch[eng] += 1
            self.cnt[eng] = 0
            self.nsem += 1
        ins = None
        for fn in fns:
            ins = fn()
        self.cnt[eng] += 1
        self.total[eng] += 1
        ins.then_inc(self.sem[eng], 1)
        ev = ("e", eng, self.epoch[eng], self.cnt[eng], self.sem[eng])
        for b in W:
            b.w = ev
            b.r = {}
        for b in R:
            b.r[eng] = ev
        return ins

    def op(self, eng, fn, R=(), W=()):
        return self.ops(eng, [fn], R, W)

    def dma(self, q, out, in_, ds, R=(), W=()):
        self._deps(q, R, W)
        if ds.count + 16 > self.EPOCH:
            self.final_waits.append((ds.sem, ds.count))
            ds.gen += 1
            ds.sem = self.nc.alloc_semaphore(f"ds_{ds.name}_{ds.gen}")
            ds.count = 0
            self.nsem += 1
        ins = self.eng[q].dma_start(out=out, in_=in_)
        ds.count += 16
        ins.then_inc(ds.sem, 16)
        ev = ("d", ds.name, ds.gen, ds.count, ds.sem)
        for b in W:
            b.w = ev
            b.r = {}
        for b in R:
            b.r[("d", ds.name)] = ev


class _Stop(Exception):
    pass


def build_nc(depth=DEPTH):
    import os
    KSTOP = float(os.environ.get("KSTOP", "99"))
    KGRP = os.environ.get("KGRP")
    groups = GROUPS if KGRP is None else [GROUPS[int(g_)] for g_ in KGRP.split(',')]

    def chk(n):
        if KSTOP <= n:
            raise _Stop()
    nc = bass.Bass("TRN2", target_bir_lowering=False)
    S = Sched(nc)
    dt_in = lambda name, shape: nc.dram_tensor(name, list(shape), F32, kind="ExternalInput").ap()
    dt_out = lambda name, shape: nc.dram_tensor(name, list(shape), F32, kind="ExternalOutput").ap()

    xin = dt_in("xin", [NPT * 128 + 64, D])
    cT = dt_in("cT", [128, 8, 17])
    sconv = dt_in("sconv", [depth, 128, 24, NSEQ, 3])
    sssm = dt_in("sssm", [depth, NSEQ, 8, 128, 128])
    w_ada = dt_in("w_ada", [depth, D, 3 * D])
    b_adaT = dt_in("b_adaT", [128, depth, 16])
    b_gate = dt_in("b_gate", [depth, D])
    w_in = dt_in("w_in", [depth, D, P_IN])
    w_sT = dt_in("w_sT", [depth, 128, 8, 128])
    w_sTs = dt_in("w_sTs", [depth, 64, 8, 64])
    bsr = dt_in("bsr", [1, depth * 8 * 128])
    bsrs = dt_in("bsrs", [1, depth * 8 * 64])
    lnv_g = dt_in("lnv_g", [depth, D])
    lnv_b = dt_in("lnv_b", [depth, D])
    cw = dt_in("cw", [128, depth, 24, 4])
    alog = dt_in("alog", [1, depth * 8])
    dtb = dt_in("dtb", [1, depth * 8])
    ogc = dt_in("ogc", [128, depth])
    w_pa = dt_in("w_pa", [depth, D, D])
    w_pb = dt_in("w_pb", [depth, D, D])
    w_o = dt_in("w_o", [depth, D, D])
    ln_g = dt_in("ln_g", [depth, D])
    ln_b = dt_in("ln_b", [depth, D])
    consts = dt_in("consts", [128, 12, 128])
    selc = dt_in("selc", [16, 8, 128])
    seqsel = dt_in("seqsel", [64, 16])

    y_p = dt_out("y_p", [NPT * 128, D])
    y_s = dt_out("y_s", [64, D])
    o_convp = dt_out("o_convp", [depth, 3, 3 * D])
    o_ssmp = dt_out("o_ssmp", [depth, 8, 128, 128])
    o_cvp = dt_out("o_cvp", [depth, 128, D])
    o_convs = dt_out("o_convs", [depth, NSEQ, 3, 3 * D])
    o_ssms = dt_out("o_ssms", [depth, NSEQ, 8, 128, 128])
    o_cvs = dt_out("o_cvs", [depth, 64, D])

    sb = lambda name, shape, dt=F32: nc.alloc_sbuf_tensor(name, list(shape), dt)
    x_sb = sb("x_sb", [128, 5, D])
    hT = sb("hT", [128, 8, MAXTOK], BF16)
    bufP = sb("bufP", [128, 5 * D], BF16)
    bufQ = sb("bufQ", [128, 8, MAXTOK], BF16)
    NSLOT = 2
    wslot = [sb(f"wslot{i}", [128, 8, 256]) for i in range(NSLOT)]
    rowA = sb("rowA", [128, D])
    rowB = sb("rowB", [128, D])
    gate_p = sb("gate_p", [128, D])
    gate_s = sb("gate_s", [128, D])
    scr = [sb(f"scr{i}", [128, D]) for i in range(2)]
    xn_bf = sb("xn_bf", [128, D], BF16)
    cst = sb("cst", [128, 12, 128])
    ident_bf = sb("ident_bf", [128, 128], BF16)
    ones_bf = sb("ones_bf", [128, 128], BF16)
    ones_f = sb("ones_f", [128, 128])
    sel_bf = sb("sel_bf", [16, 8, 128], BF16)
    seqsel_sb = sb("seqsel_sb", [64, 16])
    eps_ln = sb("eps_ln", [128, 1])
    eps_nm = sb("eps_nm", [128, 1])
    one_c = sb("one_c", [128, 1])
    scT = sb("scT", [128, 8, 17])
    scT_rep = sb("scT_rep", [128, 8, 128])
    scT_s4 = sb("scT_s4", [128, 8, 64])
    modT_all = sb("modT_all", [128, depth, 16, 17])
    opsT_all = sb("opsT_all", [128, depth, 8, 17])
    badaT = sb("badaT", [128, depth, 16])
    cw_sb = sb("cw_sb", [128, depth, 24, 4])
    ogc_sb = sb("ogc_sb", [128, depth])
    alog_sb = sb("alog_sb", [128, depth * 8])
    dtb_sb = sb("dtb_sb", [128, depth * 8])
    nA_sb = sb("nA_sb", [128, depth * 8])
    wsT_bf = sb("wsT_bf", [128, 8, 128], BF16)
    wsTs_bf = sb("wsTs_bf", [64, 8, 64], BF16)
    bsr_bf = sb("bsr_bf", [1, 8 * 128], BF16)
    bsrs_bf = sb("bsrs_bf", [1, 8 * 64], BF16)
    ones_row = sb("ones_row", [1, 128], BF16)
    scv_bf = sb("scv_bf", [128, 24, NSEQ, 3], BF16)
    halo_all = sb("halo_all", [128, depth, 24, 3], BF16)
    mv = sb("mv", [128, 2])
    st6 = sb("st6", [128, 2, 6])
    rstd = sb("rstd", [128, 1])
    nbias = sb("nbias", [128, 1])
    wba = sb("wba", [128, 8, 16], BF16)
    betaT = sb("betaT", [16, MAXTOK], BF16)
    beta_tok = sb("beta_tok", [128, 5, 8])
    apre = sb("apre", [128, 5, 8])
    g_tok = sb("g_tok", [128, 5, 8])
    gam_tok = sb("gam_tok", [128, 5, 8])
    nbG = sb("nbG", [128, 5, 8])
    egl = sb("egl", [128, 5, 8])
    GC = sb("GC", [128, 5, 8])
    G2 = sb("G2", [64, 16, 8])
    GCs = sb("GCs", [128, 16, 8])
    xpre = sb("xpre", [128, 3 + 512], BF16)
    xps = sb("xps", [128, NSEQ, 7], BF16)
    diag = sb("diag", [128, 12, 128], BF16)
    sqb = sb("sqb", [128, MAXTOK], BF16)
    rn = sb("rn", [128, MAXTOK])
    rn2 = sb("rn2", [128, MAXTOK])
    kT = sb("kT", [128, MAXTOK], BF16)
    kbT = sb("kbT", [128, MAXTOK], BF16)
    qT = sb("qT", [128, MAXTOK], BF16)
    vT = sb("vT", [128, MAXTOK], BF16)
    zbs = sb("zbs", [128, MAXTOK], BF16)
    Vb = sb("Vb", [128, 5, 128], BF16)
    Kd = sb("Kd", [128, 5, 128], BF16)
    S32_all = sb("S32_all", [128, depth, 8, 128])
    S_bf = sb("S_bf", [128, 8, 128], BF16)
    S0f = sb("S0f", [128, 8, 128])
    S0b = sb("S0b", [128, NSEQ, 128], BF16)
    Kdm = sb("Kdm", [64, 128], BF16)
    NSETS = int(os.environ.get("KSETS", "4"))
    tsets = []
    for si in range(NSETS):
        tsets.append((
            sb(f"Gm{si}", [128, 128]), sb(f"eET{si}", [128, 128]), sb(f"eE{si}", [128, 128]), sb(f"Grow{si}", [128, 128]),
            sb(f"DTsn{si}", [128, 128]), sb(f"DTi{si}", [128, 128]), sb(f"Dsn{si}", [128, 128]),
            sb(f"qgT{si}", [128, 128], BF16),
            [sb(f"Mb{si}_{i}", [128, 128], BF16) for i in range(2)],
            [sb(f"Nb{si}_{i}", [128, 128], BF16) for i in range(2)],
            [sb(f"Xb{si}_{i}", [128, 128], BF16) for i in range(2)],
            sb(f"PTb{si}", [128, 128], BF16), sb(f"Rb{si}", [128, 128], BF16), sb(f"Vnb{si}", [128, 128], BF16),
            sb(f"KSTb{si}", [128, 64], BF16), sb(f"sqo{si}", [128, 128], BF16), sb(f"rro{si}", [128, 128]),
            sb(f"t1o{si}", [128, 128])))
    ev32 = [sb(f"ev32_{i}", [128, 512]) for i in range(2)]

    psum = [nc.alloc_psum_tensor(f"ps{i}", [128, 512], F32) for i in range(8)]

    T = {}

    def tok(name):
        if name not in T:
            T[name] = Buf(name)
        return T[name]

    tX = [tok(f"x{i}") for i in range(5)]
    tH = [tok(f"h{i}") for i in range(5)]
    tPt = [tok(f"Pt{i}") for i in range(5)]
    tPj = [tok(f"Pj{i}") for i in range(8)]
    tQ = [tok(f"Q{i}") for i in range(8)]
    tPS = [tok(f"ps{i}") for i in range(8)]
    tW = [tok(f"w{i}") for i in range(NSLOT)]
    dsW = [S.dsem(f"w{i}") for i in range(NSLOT)]
    dsWh = [S.dsem(f"wh{i}") for i in range(NSLOT)]
    state = {"bank": 0, "slot": 0}

    def bank():
        i = state["bank"]
        state["bank"] = (i + 1) % 8
        return psum[i], tPS[i]

    def slot(hw=False):
        i = state["slot"]
        state["slot"] = (i + 1) % NSLOT
        return wslot[i], tW[i], (dsWh[i] if hw else dsW[i])

    ds_c = S.dsem("const")
    ds_c2 = S.dsem("const_sw")
    ds_x = [S.dsem(f"x{i}") for i in range(5)]
    ds_row = [S.dsem("rowA"), S.dsem("rowB")]
    ds_ba = S.dsem("wba")
    _ds_named = {}

    def ds_for(name):
        if name not in _ds_named:
            _ds_named[name] = S.dsem(name)
        return _ds_named[name]

    V, A, PE, PO = nc.vector, nc.scalar, nc.tensor, nc.gpsimd

    def act(out, in_, func, R, W, **kw):
        S.op("act", lambda: A.activation(out=out, in_=in_, func=func, **kw), R, W)

    def tt(out, in0, in1, op, R, W, eng="dve"):
        e = V if eng == "dve" else PO
        S.op(eng, lambda: e.tensor_tensor(out=out, in0=in0, in1=in1, op=op), R, W)

    def ts(out, in0, s1, s2, op0, op1, R, W):
        if op1 is None:
            S.op("dve", lambda: V.tensor_scalar(out=out, in0=in0, scalar1=s1, scalar2=None, op0=op0), R, W)
        else:
            S.op("dve", lambda: V.tensor_scalar(out=out, in0=in0, scalar1=s1, scalar2=s2, op0=op0, op1=op1), R, W)

    def stt(out, in0, scalar, in1, op0, op1, R, W):
        S.op("dve", lambda: V.scalar_tensor_tensor(out=out, in0=in0, scalar=scalar, in1=in1, op0=op0, op1=op1), R, W)

    def cp(out, in_, R, W, eng="dve"):
        if eng == "act":
            S.op("act", lambda: A.copy(out=out, in_=in_), R, W)
        else:
            S.op("dve", lambda: V.tensor_copy(out=out, in_=in_), R, W)

    def mm(out, pairs, R, W, first=True, last=True):
        n = len(pairs)
        fns = []
        for i, (l, r) in enumerate(pairs):
            fns.append(lambda l=l, r=r, i=i: PE.matmul(out, l, r, start=(first and i == 0), stop=(last and i == n - 1),
                                                      skip_group_check=True))
        S.ops("pe", fns, R, W)

    def transpose(out, in_, idn, R, W):
        S.op("pe", lambda: PE.transpose(out, in_, idn), R, W)

    tC = tok("const")
    S.dma("sp", cst[:], consts, ds_c, W=[tC])
    S.dma("pool", ident_bf[:], consts[:, 0, :], ds_c2, W=[tok("c2")])
    S.dma("pool", sel_bf[:], selc, ds_c2, W=[tok("c2")])
    S.dma("sp", seqsel_sb[:], seqsel, ds_c, W=[tC])
    S.dma("sp", scT[:], cT, ds_c, W=[tC])
    S.dma("sp", badaT[:], b_adaT, ds_c, W=[tC])
    S.dma("sp", cw_sb[:], cw, ds_c, W=[tC])
    S.dma("sp", ogc_sb[:], ogc, ds_c, W=[tC])
    S.dma("sp", alog_sb[:], alog.partition_broadcast(128), ds_c, W=[tC])
    S.dma("sp", dtb_sb[:], dtb.partition_broadcast(128), ds_c, W=[tC])
    for e_ in ("pe", "act", "dve", "pool", "sp"):
        S.eng[e_].wait_ge(ds_c.sem, ds_c.count)
        S.eng[e_].wait_ge(ds_c2.sem, ds_c2.count)
    tC.w = None
    MASKS = {
        False: dict(Lincl=cst[:, 1, :], Ustr=cst[:, 2, :], mSTn=cst[:, 3, :], mIT=cst[:, 4, :], mSNn=cst[:, 5, :]),
        True: dict(Lincl=cst[:, 6, :], Ustr=cst[:, 7, :], mSTn=cst[:, 8, :], mIT=cst[:, 9, :], mSNn=cst[:, 10, :]),
    }
    tMisc = tok("misc")
    S.op("dve", lambda: V.memset(ones_bf[:], 1.0), W=[tMisc])
    S.op("dve", lambda: V.memset(ones_f[:], 1.0), W=[tMisc])
    S.op("dve", lambda: V.memset(ones_row[:], 1.0), W=[tMisc])
    S.op("dve", lambda: V.memset(eps_ln[:], LN_EPS), W=[tMisc])
    S.op("dve", lambda: V.memset(eps_nm[:], NORM_EPS), W=[tMisc])
    S.op("dve", lambda: V.memset(one_c[:], 1.0), W=[tMisc])
    S.op("dve", lambda: V.memset(halo_all[:], 0.0), W=[tok("halo")])
    S.op("dve", lambda: V.memset(S32_all[:], 0.0), W=[tok("S32")])
    tSc = tok("scT")
    act(scT[:], scT[:], AF.Silu, [tC], [tSc])
    cp(scT_rep[:], scT[:, :, 0:1].to_broadcast([128, 8, 128]), [tSc], [tok("scTrep")])
    cp(scT_s4[:].rearrange("p k (s t) -> p k s t", t=4), scT[:, :, 1:17].unsqueeze(3).to_broadcast([128, 8, 16, 4]),
       [tSc], [tok("scTs4")])
    act(nA_sb[:], alog_sb[:], AF.Exp, [tC], [tok("nA")])
    ts(nA_sb[:], nA_sb[:], -1.0, None, ALU.mult, None, [tok("nA")], [tok("nA")])

    def rsqrt_small(out, in_, eps_t, scale, R, W, n=128):
        act(out, in_, AF.Ln, R, W, bias=eps_t[:n, :], scale=scale)
        act(out, out, AF.Exp, W, W, scale=-0.5)

    def layer_norm_stats(src, pt, Rt):
        tS = tok("st6")
        S.op("dve", lambda: V.bn_stats(st6[:pt, 0, :], src[:, 0:512]), Rt, [tS])
        S.op("dve", lambda: V.bn_stats(st6[:pt, 1, :], src[:, 512:1024]), Rt + [tS], [tS])
        S.op("dve", lambda: V.bn_aggr(mv[:pt, :], st6[:pt, :, :]), [tS], [tok("mv")])
        rsqrt_small(rstd[:pt, :], mv[:pt, 1:2], eps_ln, 1.0, [tok("mv")], [tok("rstd")], n=pt)
        stt(nbias[:pt, :], mv[:pt, 0:1], -1.0, rstd[:pt, :], ALU.mult, ALU.mult, [tok("mv"), tok("rstd")], [tok("nbias")])

    wscr = nc.dram_tensor("wscr", [depth * 42, 128, 4096], BF16).ap()
    wcache = {}
    cur = {"skip": False}
    hwds = {tW[i_]: dsWh[i_] for i_ in range(NSLOT)}
    wbds = {tW[i_]: S.dsem(f"wb{i_}") for i_ in range(NSLOT)}

    def fill_begin(key, ws_, tw_):
        cur["key"], cur["ws"], cur["tw"] = key, ws_, tw_
        if key in wcache:
            S.dma("sp", ws_[:].bitcast(BF16).rearrange("p k c -> p (k c)"), wscr[wcache[key]], hwds[tw_],
                  R=[tok("wc%d" % wcache[key])], W=[tw_])
            cur["skip"] = True
        else:
            cur["skip"] = False

    def fill_end():
        if not cur["skip"]:
            idx = len(wcache)
            wcache[cur["key"]] = idx
            S.dma("sp", wscr[idx], cur["ws"][:].bitcast(BF16).rearrange("p k c -> p (k c)"), wbds[cur["tw"]], R=[cur["tw"]],
                  W=[tok("wc%d" % idx)])
        cur["skip"] = False

    def load_w(dst3, src_l, c0, ncols, tW_, dsW_, q="pool"):
        if cur["skip"]:
            return
        S.dma(q, dst3, src_l.rearrange("(k p) c -> p k c", p=128)[:, :, c0:c0 + ncols], dsW_, W=[tW_])

    out_ds_i = [0]

    def out_dma(dst, src, R):
        S.dma("sp", dst, src, ds_for("o_" + R[0].name), R=R)

    for l in range(depth):
        for g in range(8):
            ws, tw, dw = slot(hw=True)
            S.dma("sp", ws[:], w_ada[l].rearrange("(k p) c -> p k c", p=128)[:, :, g * 256:(g + 1) * 256], dw, W=[tw])
            for c2 in range(2):
                ct = g * 2 + c2
                ps, tp = bank()
                mm(ps[:, 0:17], [(ws[:, k, c2 * 128:(c2 + 1) * 128], scT[:, k, :]) for k in range(8)], [tw, tSc], [tp])
                ts(modT_all[:, l, ct, :], ps[:, 0:17], badaT[:, l, ct:ct + 1], None, ALU.add, None, [tp, tC], [tok("modT")])
    ts(opsT_all[:], modT_all[:, :, 8:16, :], 1.0, None, ALU.add, None, [tok("modT")], [tok("opsT")])

    def main_loop():
      chk(0)
      for gi, tiles in enumerate(groups):
          has_s = 16 in tiles
          nt = len(tiles)
          blocks = [(0, 512, list(range(4)))] + ([(512, 64, [4])] if has_s else [])
          tinfo = []
          for li, t in enumerate(tiles):
              tinfo.append((li, t, li * 128, 128 if t < 16 else 64, t == 16))
          for (li, t, c0, pt, smp) in tinfo:
              S.dma("sp", x_sb[:pt, li, :], xin[t * 128:t * 128 + pt, :], ds_x[li], W=[tX[li]])

          for l in range(depth):
              last_layer = (l == depth - 1)
              modT = modT_all[:, l]
              opsT = opsT_all[:, l]
              halo = halo_all[:, l]
              S32 = S32_all[:, l]
              wsT_f = scr[0][:, :].rearrange("p (h t) -> p h t", h=8)
              S.dma("sp", wsT_f, w_sT[l], ds_for("l_wsT"), W=[tok("scr0")])
              tt(wsT_bf[:], wsT_f, MASKS[False]["mIT"].unsqueeze(1).to_broadcast([128, 8, 128]), ALU.mult,
                 [tok("scr0"), tC], [tok("wsT")])
              S.dma("pool", bsr_bf[:], bsr[:, l * 1024:(l + 1) * 1024], ds_for("l_bsr"), W=[tok("bsr")])
              if has_s:
                  wsTs_f = scr[1][:64, 0:512].rearrange("p (h t) -> p h t", h=8)
                  S.dma("sp", wsTs_f, w_sTs[l], ds_for("l_wsTs"), W=[tok("scr1")])
                  tt(wsTs_bf[:], wsTs_f, MASKS[True]["mIT"][:64, :64].unsqueeze(1).to_broadcast([64, 8, 64]), ALU.mult,
                     [tok("scr1"), tC], [tok("wsTs")])
                  S.dma("pool", bsrs_bf[:], bsrs[:, l * 512:(l + 1) * 512], ds_for("l_bsrs"), W=[tok("bsrs")])
                  S.dma("pool", scv_bf[:], sconv[l], ds_for("l_scv"), W=[tok("scv")])
              cp(S_bf[:], S32, [tok("S32")], [tok("Sbf")], eng="act")
              S.dma("sp", rowA[:], b_gate[l:l + 1, :].partition_broadcast(128), ds_row[0], W=[tok("rowA")])
              for g in range(4):
                  ws, tw, dw = slot(hw=True)
                  S.dma("sp", ws[:], w_ada[l].rearrange("(k p) c -> p k c", p=128)[:, :, 2048 + g * 256:2048 + (g + 1) * 256],
                        dw, W=[tw])
                  c0g = g * 256
                  ps, tp = bank()
                  mm(ps[:, 0:256], [(scT_rep[:, k, :], ws[:, k, :]) for k in range(8)], [tw, tok("scTrep")], [tp])
                  tt(gate_p[:, c0g:c0g + 256], ps[:, 0:256], rowA[:, c0g:c0g + 256], ALU.add, [tp, tok("rowA")], [tok("gate_p")])
                  if has_s:
                      ps, tp = bank()
                      mm(ps[:64, 0:256], [(scT_s4[:, k, :], ws[:, k, :]) for k in range(8)], [tw, tok("scTs4")], [tp])
                      tt(gate_s[:64, c0g:c0g + 256], ps[:64, 0:256], rowA[:64, c0g:c0g + 256], ALU.add, [tp, tok("rowA")],
                         [tok("gate_s")])

              chk(1)
              for (li, t, c0, pt, smp) in tinfo:
                  layer_norm_stats(x_sb[:pt, li, :], pt, [tX[li]])
                  act(xn_bf[:pt, :], x_sb[:pt, li, :], AF.Identity, [tX[li], tok("rstd"), tok("nbias")], [tok("xn")],
                      bias=nbias[:pt, :], scale=rstd[:pt, :])
                  ps, tp = bank()
                  psb = ps[:].bitcast(BF16)
                  for k in range(8):
                      transpose(psb[:, k * 128:k * 128 + pt], xn_bf[:pt, k * 128:(k + 1) * 128], ident_bf[:pt, :pt],
                                [tok("xn"), tC], [tp])
                  for k in range(8):
                      if not smp:
                          act(hT[:, k, c0:c0 + pt], psb[:, k * 128:k * 128 + pt], AF.Identity,
                              [tp, tok("opsT"), tok("modT")], [tH[li]], bias=modT[:, k, 0:1], scale=opsT[:, k, 0:1])
                      else:
                          tt(ev32[0][:, 0:64].rearrange("p (s t) -> p s t", t=4),
                             psb[:, k * 128:k * 128 + 64].rearrange("p (s t) -> p s t", t=4),
                             opsT[:, k, 1:17].unsqueeze(2).to_broadcast([128, 16, 4]), ALU.mult,
                             [tp, tok("opsT")], [tok("ev0")])
                          tt(hT[:, k, c0:c0 + 64].rearrange("p (s t) -> p s t", t=4),
                             ev32[0][:, 0:64].rearrange("p (s t) -> p s t", t=4),
                             modT[:, k, 1:17].unsqueeze(2).to_broadcast([128, 16, 4]), ALU.add,
                             [tok("ev0"), tok("modT")], [tH[li]])

              chk(2)
              S.dma("sp", rowA[:], lnv_g[l:l + 1, :].partition_broadcast(128), ds_row[0], W=[tok("rowA")])
              S.dma("sp", rowB[:], lnv_b[l:l + 1, :].partition_broadcast(128), ds_row[1], W=[tok("rowB")])
              wv = []
              for hf in range(2):
                  ws, tw, dw = slot()
                  wsb = ws[:].bitcast(BF16)
                  fill_begin((l, "P1", hf), ws, tw)
                  load_w(wsb, w_in[l], 1024 + hf * 512, 512, tw, dw)
                  fill_end()
                  wv.append((wsb, tw))
              for (li, t, c0, pt, smp) in tinfo:
                  for hf in range(2):
                      ps, tp = bank()
                      mm(ps[:pt, :], [(hT[:, k, c0:c0 + pt], wv[hf][0][:, k, :]) for k in range(8)],
                         [tH[li], wv[hf][1]], [tp])
                      act(scr[0][:pt, hf * 512:(hf + 1) * 512], ps[:pt, :], AF.Gelu_apprx_tanh, [tp], [tok("scr0")])
                  layer_norm_stats(scr[0][:pt, :], pt, [tok("scr0")])
                  act(scr[1][:pt, :], scr[0][:pt, :], AF.Identity, [tok("scr0"), tok("rstd"), tok("nbias")], [tok("scr1")],
                      bias=nbias[:pt, :], scale=rstd[:pt, :])
                  tt(scr[1][:pt, :], scr[1][:pt, :], rowA[:pt, :], ALU.mult, [tok("scr1"), tok("rowA")], [tok("scr1")])
                  if t >= 15:
                      tt(scr[1][:pt, :], scr[1][:pt, :], rowB[:pt, :], ALU.add, [tok("scr1"), tok("rowB")], [tok("scr1")])
                      out_dma(o_cvp[l] if t == 15 else o_cvs[l], scr[1][:pt, :], [tok("scr1")])
                      cp(bufP[:pt, li * D:(li + 1) * D], scr[1][:pt, :], [tok("scr1")], [tPt[li]] + tPj)
                  else:
                      tt(bufP[:pt, li * D:(li + 1) * D], scr[1][:pt, :], rowB[:pt, :], ALU.add, [tok("scr1"), tok("rowB")],
                         [tPt[li]] + tPj)

              chk(3)
              for j in range(8):
                  ws, tw, dw = slot()
                  wsb = ws[:].bitcast(BF16)
                  fill_begin((l, "P2", j), ws, tw)
                  load_w(wsb[:, :, 0:128], w_in[l], j * 128, 128, tw, dw)
                  load_w(wsb[:, :, 128:256], w_in[l], 2048 + j * 128, 128, tw, dw)
                  fill_end()
                  for (b0, bn, lis) in blocks:
                      pu, tpu = bank()
                      mm(pu[:, 0:bn], [(wsb[:, k, 0:128], hT[:, k, b0:b0 + bn]) for k in range(8)],
                         [tw] + [tH[i] for i in lis], [tpu])
                      pz, tpz = bank()
                      mm(pz[:, 0:bn], [(wsb[:, k, 128:256], hT[:, k, b0:b0 + bn]) for k in range(8)],
                         [tw] + [tH[i] for i in lis], [tpz])
                      pS, tpS = bank()
                      for li in lis:
                          (_, t, c0, pt, smp) = tinfo[li]
                          if not smp:
                              wmat = wsT_bf[:, j, :]
                              brow = bsr_bf[0:1, j * 128:(j + 1) * 128]
                              tws = [tok("wsT"), tok("bsr")]
                          else:
                              wmat = wsTs_bf[:, j, :]
                              brow = bsrs_bf[0:1, j * 64:(j + 1) * 64]
                              tws = [tok("wsTs"), tok("bsrs")]
                          mm(pS[:, c0 - b0:c0 - b0 + pt],
                             [(bufP[:pt, li * D + j * 128:li * D + (j + 1) * 128], wmat),
                              (ones_row[0:1, :], brow)], [tPt[li], tC, tMisc] + tws, [tpS])
                      act(ev32[0][:, 0:bn], pu[:, 0:bn], AF.Gelu_apprx_tanh, [tpu], [tok("ev0")])
                      act(ev32[1][:, 0:bn], pz[:, 0:bn], AF.Silu, [tpz], [tok("ev1")])
                      tt(ev32[0][:, 0:bn], ev32[0][:, 0:bn], pS[:, 0:bn], ALU.mult, [tok("ev0"), tpS], [tok("ev0")])
                      tt(bufQ[:, j, b0:b0 + bn], ev32[0][:, 0:bn], ev32[1][:, 0:bn], ALU.mult, [tok("ev0"), tok("ev1")],
                         [tQ[j]])

              chk(4)
              mTv = bufP[:, 0:8 * MAXTOK].rearrange("p (j n) -> p j n", j=8)
              for j in range(8):
                  ws, tw, dw = slot()
                  wsb = ws[:].bitcast(BF16)
                  fill_begin((l, "P3", j), ws, tw)
                  load_w(wsb[:, :, 0:128], w_pa[l], j * 128, 128, tw, dw)
                  load_w(wsb[:, :, 128:256], w_in[l], 7184 + j * 128, 128, tw, dw)
                  fill_end()
                  for (b0, bn, lis) in blocks:
                      pa, tpa = bank()
                      mm(pa[:, 0:bn], [(wsb[:, k, 0:128], bufQ[:, k, b0:b0 + bn]) for k in range(8)], [tw] + tQ, [tpa])
                      pg, tpg = bank()
                      mm(pg[:, 0:bn], [(wsb[:, k, 128:256], hT[:, k, b0:b0 + bn]) for k in range(8)],
                         [tw] + [tH[i] for i in lis], [tpg])
                      act(ev32[0][:, 0:bn], pg[:, 0:bn], AF.Sigmoid, [tpg], [tok("ev0")])
                      tt(mTv[:, j, b0:b0 + bn], ev32[0][:, 0:bn], pa[:, 0:bn], ALU.mult, [tok("ev0"), tpa], [tPj[j]] + tPt)

              chk(5)
              S.dma("pool", wba[:], w_in[l].rearrange("(k p) c -> p k c", p=128)[:, :, 7168:7184], ds_ba, W=[tok("wba")])
              for (b0, bn, lis) in blocks:
                  ps, tp = bank()
                  mm(ps[:16, 0:bn], [(wba[:, k, :], hT[:, k, b0:b0 + bn]) for k in range(8)],
                     [tok("wba")] + [tH[i] for i in lis], [tp])
                  act(betaT[:, b0:b0 + bn], ps[:16, 0:bn], AF.Sigmoid, [tp], [tok("betaT")])
              psba, tpba = bank()
              for (li, t, c0, pt, smp) in tinfo:
                  mm(psba[:pt, li * 16:(li + 1) * 16], [(hT[:, k, c0:c0 + pt], wba[:, k, :]) for k in range(8)],
                     [tok("wba"), tH[li]], [tpba])
              for (li, t, c0, pt, smp) in tinfo:
                  act(beta_tok[:pt, li, :], psba[:pt, li * 16:li * 16 + 8], AF.Sigmoid, [tpba], [tok("beta_tok")])
                  tt(apre[:pt, li, :], psba[:pt, li * 16 + 8:li * 16 + 16], dtb_sb[:pt, l * 8:(l + 1) * 8], ALU.add,
                     [tpba, tC], [tok("apre")])
              for (li, t, c0, pt, smp) in tinfo:
                  act(apre[:pt, li, :], apre[:pt, li, :], AF.Exp, [tok("apre")], [tok("apre")])
                  act(apre[:pt, li, :], apre[:pt, li, :], AF.Ln, [tok("apre")], [tok("apre")], bias=one_c[:pt, :], scale=1.0)
                  tt(g_tok[:pt, li, :], apre[:pt, li, :], nA_sb[:pt, l * 8:(l + 1) * 8], ALU.mult, [tok("apre"), tok("nA")],
                     [tok("g_tok")])
                  mk = MASKS[smp]
                  ps, tp = bank()
                  mm(ps[:pt, 0:8], [(mk["Lincl"][:pt, :pt], g_tok[:pt, li, :])], [tok("g_tok"), tC], [tp])
                  mm(ps[:pt, 8:16], [(mk["Ustr"][:pt, :pt], g_tok[:pt, li, :])], [tok("g_tok"), tC], [tp])
                  if not smp:
                      mm(ps[:, 16:24], [(ones_f[:pt, :], g_tok[:pt, li, :])], [tok("g_tok"), tMisc], [tp])
                      act(GC[:, li, :], ps[:, 16:24], AF.Exp, [tp], [tok("GC")])
                  act(gam_tok[:pt, li, :], ps[:pt, 0:8], AF.Exp, [tp], [tok("gam")])
                  act(egl[:pt, li, :], ps[:pt, 8:16], AF.Exp, [tp], [tok("egl")])
                  stt(nbG[:pt, li, :], gam_tok[:pt, li, :], -1.0, beta_tok[:pt, li, :], ALU.mult, ALU.mult,
                      [tok("gam"), tok("beta_tok")], [tok("nbG")])
                  if smp:
                      tt(G2[:], g_tok[:64, li, :].unsqueeze(1).to_broadcast([64, 16, 8]),
                         seqsel_sb[:].unsqueeze(2).to_broadcast([64, 16, 8]), ALU.mult, [tok("g_tok"), tC], [tok("G2")])
                      ps2, tp2 = bank()
                      mm(ps2[:, 0:128], [(ones_f[:64, :], G2[:].rearrange("p s h -> p (s h)"))], [tok("G2"), tMisc], [tp2])
                      act(GCs[:].rearrange("p s h -> p (s h)"), ps2[:, 0:128], AF.Exp, [tp2], [tok("GCs")])

              chk(5.1)
              csq = scr[0][:, 0:MAXTOK]
              csk = scr[1][:, 0:MAXTOK]
              for h in range(8):
                  ws, tw, dw = slot()
                  wsb = ws[:].bitcast(BF16)
                  fill_begin((l, "P4", h), ws, tw)
                  for ci, cbase in enumerate((3072, 4096, 5120, 6144)):
                      load_w(wsb[:, :, ci * 128:(ci + 1) * 128], w_in[l], cbase + h * 128, 128, tw, dw)
                  fill_end()
                  if has_s:
                      S.dma("pool", S0b[:], sssm[l, :, h].rearrange("s d v -> d s v"), ds_for("S0b"), W=[tok("S0b")])
                  for ci in range(3):
                      for jt in range(4):
                          ts(diag[:, ci * 4 + jt, :], ident_bf[:], cw_sb[:, l, ci * 8 + h, jt:jt + 1], None, ALU.mult, None,
                             [tC], [tok("diag")])
                  ssps = {}
                  for ci in range(3):
                      ctg = ci * 8 + h
                      dst = (csq, csk, vT)[ci]
                      tdst = (tok("scr0"), tok("scr1"), tok("vT"))[ci]
                      for (b0, bn, lis) in blocks:
                          smpb = (bn == 64)
                          pp, tpp = bank()
                          mm(pp[:, 0:bn], [(wsb[:, k, ci * 128:(ci + 1) * 128], hT[:, k, b0:b0 + bn]) for k in range(8)],
                             [tw] + [tH[i] for i in lis], [tpp])
                          pc, tpc = bank()
                          if not smpb:
                              cp(xpre[:, 0:3], halo[:, ctg, :], [tok("halo")], [tok("xpre")])
                              cp(xpre[:, 3:3 + bn], pp[:, 0:bn], [tpp], [tok("xpre")], eng="act")
                              cp(halo[:, ctg, :], xpre[:, bn:bn + 3], [tok("xpre")], [tok("halo")])
                              mm(pc[:, 0:bn], [(diag[:, ci * 4 + jt, :], xpre[:, jt:jt + bn]) for jt in range(4)],
                                 [tok("diag"), tok("xpre")], [tpc])
                          else:
                              cp(xps[:, :, 0:3], scv_bf[:, ctg, :, :], [tok("scv")], [tok("xps")])
                              cp(xps[:, :, 3:7], pp[:, 0:64].rearrange("p (s t) -> p s t", t=4), [tpp], [tok("xps")],
                                 eng="act")
                              mm(pc[:, 0:64].rearrange("p (s t) -> p s t", t=4),
                                 [(diag[:, ci * 4 + jt, :], xps[:, :, jt:jt + 4]) for jt in range(4)],
                                 [tok("diag"), tok("xps")], [tpc])
                          act(dst[:, b0:b0 + bn], pc[:, 0:bn], AF.Silu, [tpc], [tdst])
                          if ci < 2:
                              act(sqb[:, b0:b0 + bn], dst[:, b0:b0 + bn], AF.Square, [tdst], [tok("sqb")])
                              pss, tpss = bank()
                              mm(pss[:, 0:bn], [(ones_bf[:], sqb[:, b0:b0 + bn])], [tok("sqb"), tMisc], [tpss])
                              rsq_t = tok(f"rn{ci}")
                              rnb = (rn, rn2)[ci]
                              ssps[(ci, b0)] = (pss, tpss)
                              cp(rnb[:, b0:b0 + bn], pss[:, 0:bn], [tpss], [rsq_t])
                  chk(5.3)
                  for (b0, bn, lis) in blocks:
                      pz, tpz = bank()
                      mm(pz[:, 0:bn], [(wsb[:, k, 384:512], hT[:, k, b0:b0 + bn]) for k in range(8)],
                         [tw] + [tH[i] for i in lis], [tpz])
                      act(zbs[:, b0:b0 + bn], pz[:, 0:bn], AF.Silu, [tpz], [tok("zbs")])
                  for ci in range(2):
                      rnb = (rn, rn2)[ci]
                      rsq_t = tok(f"rn{ci}")
                      ntk = 512 + (64 if has_s else 0)
                      rsqrt_small(rnb[:, 0:ntk], rnb[:, 0:ntk], eps_nm, 1.0, [rsq_t], [rsq_t])
                      for (b0, bn, lis) in blocks:
                          if ci == 0:
                              stt(qT[:, b0:b0 + bn], csq[:, b0:b0 + bn], 128.0 ** -0.5, rnb[:, b0:b0 + bn], ALU.mult, ALU.mult,
                                  [tok("scr0"), rsq_t], [tok("qT")])
                          else:
                              tt(kT[:, b0:b0 + bn], csk[:, b0:b0 + bn], rnb[:, b0:b0 + bn], ALU.mult, [tok("scr1"), rsq_t],
                                 [tok("kT")])
                              pb_, tpb_ = bank()
                              mm(pb_[:, 0:bn], [(sel_bf[:, h, :], betaT[:, b0:b0 + bn])], [tok("betaT"), tC], [tpb_])
                              tt(kbT[:, b0:b0 + bn], kT[:, b0:b0 + bn], pb_[:, 0:bn], ALU.mult, [tok("kT"), tpb_],
                                 [tok("kbT")])
                  for (li, t, c0, pt, smp) in tinfo:
                      ps, tp = bank()
                      psb = ps[:].bitcast(BF16)
                      transpose(psb[:pt, 0:128], vT[:, c0:c0 + pt], ident_bf[:], [tok("vT"), tC], [tp])
                      transpose(psb[:pt, 128:256], kT[:, c0:c0 + pt], ident_bf[:], [tok("kT"), tC], [tp])
                      ts(Vb[:pt, li, :], psb[:pt, 0:128], beta_tok[:pt, li, h:h + 1], None, ALU.mult, None,
                         [tp, tok("beta_tok")], [tok("Vb")])
                      ts(Kd[:pt, li, :], psb[:pt, 128:256], egl[:pt, li, h:h + 1], None, ALU.mult, None,
                         [tp, tok("egl")], [tok("Kd")])

                  chk(5.4)
                  def tile_gen(li, t, c0, pt, smp, si, done):
                      (Gm, eET, eE, Grow, DTsn, DTi, Dsn, qgT, Mb, Nb, Xb, PTb, Rb, Vnb, KSTb, sqo, rro, t1o) = tsets[si]
                      mk = MASKS[smp]
                      yield
                      ts(Gm[:pt, :pt], mk["Lincl"][:pt, :pt], g_tok[:pt, li, h:h + 1], None, ALU.mult, None,
                         [tC, tok("g_tok")], [tok("Gm_%d" % si)])
                      pe_, tpe = bank()
                      yield
                      mm(pe_[:pt, 0:pt], [(mk["Ustr"][:pt, :pt], Gm[:pt, :pt])], [tC, tok("Gm_%d" % si)], [tpe])
                      yield
                      mm(pe_[:pt, 128:128 + pt], [(Gm[:pt, :pt], mk["Ustr"][:pt, :pt])], [tC, tok("Gm_%d" % si)], [tpe])
                      yield
                      mm(pe_[:, 256:256 + pt], [(ones_f[:pt, :], Gm[:pt, :pt])], [tMisc, tok("Gm_%d" % si)], [tpe])
                      yield
                      act(eET[:pt, :pt], pe_[:pt, 0:pt], AF.Exp, [tpe], [tok("eET_%d" % si)])
                      yield
                      act(eE[:pt, :pt], pe_[:pt, 128:128 + pt], AF.Exp, [tpe], [tok("eE_%d" % si)])
                      yield
                      act(Grow[:, :pt], pe_[:, 256:256 + pt], AF.Exp, [tpe], [tok("Grow_%d" % si)])
                      yield
                      tt(DTsn[:pt, :pt], eET[:pt, :pt], mk["mSTn"][:pt, :pt], ALU.mult, [tok("eET_%d" % si), tC], [tok("DTsn_%d" % si)])
                      yield
                      tt(DTi[:pt, :pt], eET[:pt, :pt], mk["mIT"][:pt, :pt], ALU.mult, [tok("eET_%d" % si), tC], [tok("DTi_%d" % si)])
                      yield
                      tt(Dsn[:pt, :pt], eE[:pt, :pt], mk["mSNn"][:pt, :pt], ALU.mult, [tok("eE_%d" % si), tC], [tok("Dsn_%d" % si)])
                      yield
                      tt(qgT[:, :pt], qT[:, c0:c0 + pt], Grow[:, :pt], ALU.mult, [tok("qT"), tok("Grow_%d" % si)], [tok("qgT_%d" % si)])
                      pa_, tpa_ = bank()
                      yield
                      mm(pa_[:pt, 0:pt], [(kT[:, c0:c0 + pt], kbT[:, c0:c0 + pt])], [tok("kT"), tok("kbT")], [tpa_])
                      yield
                      mm(pa_[:pt, 128:128 + pt], [(kbT[:, c0:c0 + pt], kT[:, c0:c0 + pt])], [tok("kT"), tok("kbT")], [tpa_])
                      yield
                      mm(pa_[:pt, 256:256 + pt], [(kT[:, c0:c0 + pt], qT[:, c0:c0 + pt])], [tok("kT"), tok("qT")], [tpa_])
                      tM = [tok("M0_%d" % si), tok("M1_%d" % si)]
                      tN = [tok("N0_%d" % si), tok("N1_%d" % si)]
                      tXx = [tok("X0_%d" % si), tok("X1_%d" % si)]
                      yield
                      tt(Mb[0][:pt, :pt], pa_[:pt, 0:pt], DTsn[:pt, :pt], ALU.mult, [tpa_, tok("DTsn_%d" % si)], [tM[0]])
                      yield
                      tt(Nb[0][:pt, :pt], pa_[:pt, 128:128 + pt], Dsn[:pt, :pt], ALU.mult, [tpa_, tok("Dsn_%d" % si)], [tN[0]])
                      yield
                      tt(PTb[:pt, :pt], pa_[:pt, 256:256 + pt], DTi[:pt, :pt], ALU.mult, [tpa_, tok("DTi_%d" % si)], [tok("PT_%d" % si)])
                      yield
                      tt(Xb[0][:pt, :pt], Mb[0][:pt, :pt], ident_bf[:pt, :pt], ALU.add, [tM[0], tC], [tXx[0]])
                      cur = 0
                      nlev = 1 if smp else 6
                      for lv in range(nlev):
                          lastlv = (lv == nlev - 1)
                          nx = 1 - cur
                          pn, tpn = bank()
                          yield
                          mm(pn[:pt, 0:pt], [(Mb[cur][:pt, :pt], Nb[cur][:pt, :pt])], [tM[cur], tN[cur]], [tpn])
                          if not lastlv:
                              yield
                              mm(pn[:pt, 128:128 + pt], [(Nb[cur][:pt, :pt], Mb[cur][:pt, :pt])], [tM[cur], tN[cur]], [tpn])
                          yield
                          cp(Nb[nx][:pt, :pt], pn[:pt, 0:pt], [tpn], [tN[nx]], eng="act")
                          if not lastlv:
                              yield
                              cp(Mb[nx][:pt, :pt], pn[:pt, 128:128 + pt], [tpn], [tM[nx]], eng="act")
                          px, tpx = bank()
                          yield
                          mm(px[:pt, 0:pt], [(ident_bf[:pt, :pt], Xb[cur][:pt, :pt]), (Nb[nx][:pt, :pt], Xb[cur][:pt, :pt])],
                             [tC, tXx[cur], tN[nx]], [tpx])
                          yield
                          cp(Xb[nx][:pt, :pt], px[:pt, 0:pt], [tpx], [tXx[nx]], eng="act")
                          cur = nx
                      Xf, tXf = Xb[cur], tXx[cur]
                      while li > 0 and not done.get(li - 1):
                          yield
                      pk, tpk = bank()
                      if not smp:
                          yield
                          mm(pk[:pt, 0:128], [(kT[:, c0:c0 + pt], S_bf[:, h, :])], [tok("kT"), tok("Sbf")], [tpk])
                          yield
                          stt(Rb[:pt, :], pk[:pt, 0:128], nbG[:pt, li, h:h + 1], Vb[:pt, li, :], ALU.mult, ALU.add,
                              [tpk, tok("nbG"), tok("Vb")], [tok("R_%d" % si)])
                      else:
                          for s in range(NSEQ):
                              yield
                              mm(pk[:, 4 * s:4 * s + 4], [(S0b[:, s, :], kT[:, c0 + 4 * s:c0 + 4 * s + 4])],
                                 [tok("kT"), tok("S0b")], [tpk])
                          yield
                          cp(KSTb[:, :], pk[:, 0:64], [tpk], [tok("KST_%d" % si)], eng="act")
                          pk2, tpk2 = bank()
                          pk2b = pk2[:].bitcast(BF16)
                          yield
                          transpose(pk2b[:64, 0:128], KSTb[:, :], ident_bf[:], [tok("KST_%d" % si), tC], [tpk2])
                          yield
                          stt(Rb[:pt, :], pk2b[:64, 0:128], nbG[:pt, li, h:h + 1], Vb[:pt, li, :], ALU.mult, ALU.add,
                              [tpk2, tok("nbG"), tok("Vb")], [tok("R_%d" % si)])
                      pv, tpv = bank()
                      yield
                      mm(pv[:pt, 0:128], [(Xf[:pt, :pt], Rb[:pt, :])], [tXf, tok("R_%d" % si)], [tpv])
                      yield
                      cp(Vnb[:pt, :], pv[:pt, 0:128], [tpv], [tok("Vn_%d" % si)], eng="act")
                      po, tpo = bank()
                      if not smp:
                          yield
                          mm(po[:, 0:pt], [(S_bf[:, h, :], qgT[:, :pt]), (Vnb[:pt, :], PTb[:pt, :pt])],
                             [tok("Sbf"), tok("qgT_%d" % si), tok("Vn_%d" % si), tok("PT_%d" % si)], [tpo])
                          pd, tpd = bank()
                          yield
                          mm(pd[:, 0:128], [(Kd[:pt, li, :], Vnb[:pt, :])], [tok("Kd"), tok("Vn_%d" % si)], [tpd])
                          yield
                          stt(S32[:, h, :], S32[:, h, :], GC[:, li, h:h + 1], pd[:, 0:128], ALU.mult, ALU.add,
                              [tok("S32"), tok("GC"), tpd], [tok("S32")])
                          yield
                          cp(S_bf[:, h, :], S32[:, h, :], [tok("S32")], [tok("Sbf")], eng="act")
                          if t == 15:
                              yield
                              out_dma(o_ssmp[l, h], S32[:, h, :], [tok("S32")])
                      else:
                          yield
                          mm(po[:, 0:64], [(Vnb[:64, :], PTb[:64, :64])], [tok("Vn_%d" % si), tok("PT_%d" % si)], [tpo], first=True, last=False)
                          for s in range(NSEQ):
                              yield
                              mm(po[:, 4 * s:4 * s + 4], [(S0b[:, s, :], qgT[:, 4 * s:4 * s + 4])], [tok("S0b"), tok("qgT_%d" % si)],
                                 [tpo], first=False, last=(s == NSEQ - 1))
                          for sh in range(2):
                              yield
                              S.dma("sp", S0f[:], sssm[l, 8 * sh:8 * sh + 8, h].rearrange("s d v -> d s v"), ds_for("S0f"),
                                    W=[tok("S0f")])
                              for s4 in range(2):
                                  pd, tpd = bank()
                                  for sq_ in range(4):
                                      s = 8 * sh + 4 * s4 + sq_
                                      yield
                                      ts(Kdm[:, :], Kd[:64, li, :], seqsel_sb[:, s:s + 1], None, ALU.mult, None,
                                         [tok("Kd"), tC], [tok("Kdm")])
                                      yield
                                      mm(pd[:, sq_ * 128:(sq_ + 1) * 128], [(Kdm[:, :], Vnb[:64, :])], [tok("Kdm"), tok("Vn_%d" % si)],
                                         [tpd])
                                  for sq_ in range(4):
                                      s = 8 * sh + 4 * s4 + sq_
                                      sl = 4 * s4 + sq_
                                      yield
                                      stt(S0f[:, sl, :], S0f[:, sl, :], GCs[:, s, h:h + 1], pd[:, sq_ * 128:(sq_ + 1) * 128],
                                          ALU.mult, ALU.add, [tok("S0f"), tok("GCs"), tpd], [tok("S0f")])
                              yield
                              out_dma(o_ssms[l, 8 * sh:8 * sh + 8, h].rearrange("s d v -> d s v"), S0f[:], [tok("S0f")])
                      yield
                      act(sqo[:, :pt], po[:, 0:pt], AF.Square, [tpo], [tok("sqo_%d" % si)])
                      pr, tpr = bank()
                      yield
                      mm(pr[:, 0:pt], [(ones_bf[:], sqo[:, :pt])], [tok("sqo_%d" % si), tMisc], [tpr])
                      yield
                      rsqrt_small(rro[:, :pt], pr[:, 0:pt], eps_nm, 1.0 / 128.0, [tpr], [tok("rro_%d" % si)])
                      yield
                      tt(t1o[:, :pt], po[:, 0:pt], rro[:, :pt], ALU.mult, [tpo, tok("rro_%d" % si)], [tok("t1o_%d" % si)])
                      yield
                      stt(bufQ[:, h, c0:c0 + pt], t1o[:, :pt], ogc_sb[:, l:l + 1], zbs[:, c0:c0 + pt], ALU.mult, ALU.mult,
                          [tok("t1o_%d" % si), tC, tok("zbs")], [tQ[h]])

                      done[li] = True
                  done = {}
                  pending = list(tinfo)
                  active = []
                  free_sets = list(range(NSETS))
                  while pending or active:
                      while pending and free_sets:
                          ti_ = pending.pop(0)
                          si_ = free_sets.pop(0)
                          active.append((tile_gen(*ti_, si_, done), si_))
                      for (g_, si_) in list(active):
                          try:
                              next(g_)
                          except StopIteration:
                              active.remove((g_, si_))
                              free_sets.append(si_)
              chk(6)
              if has_s or (15 in tiles):
                  for grp6 in range(6):
                      ws, tw, dw = slot()
                      wsb = ws[:].bitcast(BF16)
                      fill_begin((l, "CO", grp6), ws, tw)
                      load_w(wsb, w_in[l], 3072 + grp6 * 512, 512, tw, dw)
                      fill_end()
                      if has_s:
                          ps, tp = bank()
                          mm(ps[:64, :], [(hT[:, k, 512:576], wsb[:, k, :]) for k in range(8)], [tw, tH[4]], [tp])
                          cp(ev32[0][:64, :], ps[:64, :], [tp], [tok("ev0")], eng="act")
                          for tq in range(1, 4):
                              out_dma(o_convs[l, :, tq - 1, grp6 * 512:(grp6 + 1) * 512], ev32[0][tq:64:4, :], [tok("ev0")])
                      if 15 in tiles:
                          l15 = tiles.index(15)
                          ps2, tp2 = bank()
                          mm(ps2[:, :], [(hT[:, k, l15 * 128:(l15 + 1) * 128], wsb[:, k, :]) for k in range(8)], [tw, tH[l15]], [tp2])
                          cp(ev32[1][64:128, :], ps2[64:128, :], [tp2], [tok("ev1")], eng="act")
                          out_dma(o_convp[l, :, grp6 * 512:(grp6 + 1) * 512], ev32[1][125:128, :], [tok("ev1")])

              chk(7)
              for j in range(8):
                  ws, tw, dw = slot()
                  wsb = ws[:].bitcast(BF16)
                  fill_begin((l, "P5", j), ws, tw)
                  load_w(wsb[:, :, 0:128], w_pb[l], j * 128, 128, tw, dw)
                  load_w(wsb[:, :, 128:256], w_in[l], 8208 + j * 128, 128, tw, dw)
                  fill_end()
                  for (b0, bn, lis) in blocks:
                      pa, tpa = bank()
                      mm(pa[:, 0:bn], [(wsb[:, k, 0:128], bufQ[:, k, b0:b0 + bn]) for k in range(8)], [tw] + tQ, [tpa])
                      pg, tpg = bank()
                      mm(pg[:, 0:bn], [(wsb[:, k, 128:256], hT[:, k, b0:b0 + bn]) for k in range(8)],
                         [tw] + [tH[i] for i in lis], [tpg])
                      act(ev32[0][:, 0:bn], pg[:, 0:bn], AF.Sigmoid, [tpg], [tok("ev0")])
                      tt(ev32[0][:, 0:bn], ev32[0][:, 0:bn], pa[:, 0:bn], ALU.mult, [tok("ev0"), tpa], [tok("ev0")])
                      tt(mTv[:, j, b0:b0 + bn], ev32[0][:, 0:bn], mTv[:, j, b0:b0 + bn], ALU.add, [tok("ev0"), tPj[j]],
                         [tPj[j]])

              chk(8)
              S.dma("sp", rowA[:], ln_g[l:l + 1, :].partition_broadcast(128), ds_row[0], W=[tok("rowA")])
              S.dma("sp", rowB[:], ln_b[l:l + 1, :].partition_broadcast(128), ds_row[1], W=[tok("rowB")])
              wo = []
              for hf in range(2):
                  ws, tw, dw = slot()
                  wsb = ws[:].bitcast(BF16)
                  fill_begin((l, "P6", hf), ws, tw)
                  load_w(wsb, w_o[l], hf * 512, 512, tw, dw)
                  fill_end()
                  wo.append((wsb, tw))
              for (li, t, c0, pt, smp) in tinfo:
                  gt = gate_s if smp else gate_p
                  tg = tok("gate_s") if smp else tok("gate_p")
                  for hf in range(2):
                      ps, tp = bank()
                      mm(ps[:pt, :], [(mTv[:, k, c0:c0 + pt], wo[hf][0][:, k, :]) for k in range(8)], tPj + [wo[hf][1]], [tp])
                      tt(scr[0][:pt, hf * 512:(hf + 1) * 512], ps[:pt, :], gt[:pt, hf * 512:(hf + 1) * 512], ALU.mult,
                         [tp, tg], [tok("scr0")])
                  stt(scr[0][:pt, :], x_sb[:pt, li, :], float(ALPHA), scr[0][:pt, :], ALU.mult, ALU.add, [tX[li], tok("scr0")],
                      [tok("scr0")])
                  layer_norm_stats(scr[0][:pt, :], pt, [tok("scr0")])
                  act(scr[1][:pt, :], scr[0][:pt, :], AF.Identity, [tok("scr0"), tok("rstd"), tok("nbias")], [tok("scr1")],
                      bias=nbias[:pt, :], scale=rstd[:pt, :])
                  tt(scr[1][:pt, :], scr[1][:pt, :], rowA[:pt, :], ALU.mult, [tok("scr1"), tok("rowA")], [tok("scr1")])
                  tt(x_sb[:pt, li, :], scr[1][:pt, :], rowB[:pt, :], ALU.add, [tok("scr1"), tok("rowB")], [tX[li]])
                  if last_layer:
                      if smp:
                          out_dma(y_s, x_sb[:64, li, :], [tX[li]])
                      else:
                          out_dma(y_p[t * 128:(t + 1) * 128, :], x_sb[:, li, :], [tX[li]])

    try:
        main_loop()
    except _Stop:
        pass
    for i_ in range(int(os.environ.get("KDMA", "0"))):
        S.dma("pool", wba[:], w_in[0].rearrange("(k p) c -> p k c", p=128)[:, :, 16 * (i_ % 500):16 * (i_ % 500) + 16], ds_ba,
              W=[tok("wba")])
    for i_ in range(int(os.environ.get("KDMAH", "0"))):
        S.dma("sp", rowA[:], xin[i_:i_ + 1, :].partition_broadcast(128), ds_row[0], W=[tok("rowA")])
    for i_ in range(int(os.environ.get("KACT", "0"))):
        nc.scalar.activation(out=ev32[0][0:1, 0:1], in_=one_c[0:1, 0:1], func=AF.Gelu_apprx_tanh)
        nc.scalar.activation(out=ev32[0][0:1, 0:1], in_=one_c[0:1, 0:1], func=AF.Silu)
    if os.environ.get("KINC"):
        semx = nc.alloc_semaphore("dummy_inc")
        for i_ in range(int(os.environ["KINC"])):
            nc.vector.memset(ev32[1][0:1, 0:1], 0.0).then_inc(semx, 1)
    if os.environ.get("KDUMMY"):
        for e_ in ("pe", "act", "dve"):
            for b in tPS:
                S._wait(e_, b.w)
                for ev in b.r.values():
                    S._wait(e_, ev)
        for _ in range(20000):
            nc.tensor.matmul(psum[0][0:1, 0:1], ones_row[0:1, 0:1], ones_row[0:1, 0:1], start=True, stop=True)
        for _ in range(12000):
            nc.scalar.copy(out=ev32[0][0:1, 0:1], in_=one_c[0:1, 0:1])
        for _ in range(12000):
            nc.vector.memset(ev32[1][0:1, 0:1], 0.0)
    for (sm, c) in S.final_waits:
        nc.sync.wait_ge(sm, c)
    for d in S.dsems:
        if d.count:
            nc.sync.wait_ge(d.sem, d.count)
    print("sbuf bytes remaining:", nc.sbuf_bytes_remaining, "instr counts:", S.total, "nsem:", S.nsem)
    return nc


def _consts():
    c = np.zeros((128, 12, 128), np.float32)
    i = np.arange(128)
    a, b = i[:, None], i[None, :]
    c[:, 0] = (a == b)
    c[:, 1] = (a <= b)
    c[:, 2] = (a > b)
    c[:, 3] = -1.0 * (b > a)
    c[:, 4] = (b >= a)
    c[:, 5] = -1.0 * (b < a)
    same = ((a // 4) == (b // 4)) & (a < 64) & (b < 64)
    c[:, 6] = same & (a <= b)
    c[:, 7] = same & (a > b)
    c[:, 8] = -1.0 * (same & (b > a))
    c[:, 9] = same & (b >= a)
    c[:, 10] = -1.0 * (same & (b < a))
    sel = np.zeros((16, 8, 128), np.float32)
    for h in range(8):
        sel[h, h, :] = 1.0
    seqsel = (np.arange(64)[:, None] // 4 == np.arange(16)[None, :]).astype(np.float32)
    return c, sel, seqsel


def kernel(x_prompt, x_sample, state_conv, state_ssm, c_prompt, c_sample, w_ada, b_ada, w_in,
           w_s, b_s, lnv_g, lnv_b, conv_w, a_log, dt_bias, onorm_g, w_pa, w_pb, w_o, ln_g, ln_b):
    f = lambda a: np.ascontiguousarray(np.asarray(a, dtype=np.float32))
    (x_prompt, x_sample, state_conv, state_ssm, c_prompt, c_sample, w_ada, b_ada, w_in, w_s, b_s, lnv_g, lnv_b,
     conv_w, a_log, dt_bias, onorm_g, w_pa, w_pb, w_o, ln_g, ln_b) = [f(a) for a in (
        x_prompt, x_sample, state_conv, state_ssm, c_prompt, c_sample, w_ada, b_ada, w_in, w_s, b_s, lnv_g, lnv_b,
        conv_w, a_log, dt_bias, onorm_g, w_pa, w_pb, w_o, ln_g, ln_b)]
    depth = w_in.shape[0]
    nc = build_nc(depth)
    consts, sel, seqsel = _consts()
    b_adaT = f(b_ada[:, :2048].reshape(depth, 16, 128).transpose(2, 0, 1))
    b_gate = f(b_ada[:, 2048:])
    w_sT = f(w_s.transpose(0, 3, 1, 2))
    w_sTs = np.zeros((depth, 64, 8, 64), np.float32)
    for q in range(16):
        w_sTs[:, 4 * q:4 * q + 4, :, 4 * q:4 * q + 4] = w_s[:, :, :4, :4].transpose(0, 3, 1, 2)
    bsr = f(b_s.reshape(1, -1))
    bsrs = f(np.tile(b_s[:, :, :4], (1, 1, 16)).reshape(1, -1))
    cw = f(conv_w.reshape(depth, 4, 24, 128).transpose(3, 0, 2, 1))
    alog = f(a_log.reshape(1, -1))
    dtb = f(dt_bias.reshape(1, -1))
    ogc = f(onorm_g.T)
    shared = dict(w_ada=w_ada, b_adaT=b_adaT, b_gate=b_gate, w_in=w_in, w_sT=w_sT, w_sTs=w_sTs, bsr=bsr, bsrs=bsrs,
                  lnv_g=lnv_g, lnv_b=lnv_b, cw=cw, alog=alog, dtb=dtb, ogc=ogc, w_pa=w_pa, w_pb=w_pb, w_o=w_o,
                  ln_g=ln_g, ln_b=ln_b, consts=consts, selc=sel, seqsel=seqsel)
    in_maps = []
    for i in range(NCORE):
        ss = slice(NSEQ * i, NSEQ * (i + 1))
        xin = f(np.concatenate([x_prompt[i], x_sample[ss].reshape(64, D)], axis=0))
        cc = np.concatenate([c_prompt[i:i + 1], c_sample[ss]], axis=0)
        cT = f(cc.reshape(17, 8, 128).transpose(2, 1, 0))
        sconv = f(state_conv[:, ss].reshape(depth, NSEQ, 3, 24, 128).transpose(0, 4, 3, 1, 2))
        sssm = f(state_ssm[:, ss])
        m = dict(shared)
        m.update(xin=xin, cT=cT, sconv=sconv, sssm=sssm)
        in_maps.append(m)
    import os
    if os.environ.get('KTRACE'):
        res = run_bass_kernel_spmd(nc, in_maps, core_ids=list(range(NCORE)), trace=True)
        print('EXEC_TIME_NS', res.exec_time_ns)
    else:
        res = run_bass_kernel_spmd(nc, in_maps, core_ids=list(range(NCORE)))
    R = res.results
    y_prompt = np.stack([R[i]["y_p"] for i in range(NCORE)], 0)
    y_sample = np.concatenate([R[i]["y_s"].reshape(NSEQ, 4, D) for i in range(NCORE)], 0)
    conv_p = np.stack([R[i]["o_convp"] for i in range(NCORE)], 1)
    ssm_p = np.stack([R[i]["o_ssmp"] for i in range(NCORE)], 1)
    cv_p = np.stack([R[i]["o_cvp"] for i in range(NCORE)], 1)
    conv_s = np.concatenate([R[i]["o_convs"] for i in range(NCORE)], 1)
    ssm_s = np.concatenate([R[i]["o_ssms"] for i in range(NCORE)], 1)
    cv_s = np.concatenate([R[i]["o_cvs"].reshape(depth, NSEQ, 4, D) for i in range(NCORE)], 1)
    return tuple(np.ascontiguousarray(a, dtype=np.float32) for a in
                 (y_prompt, y_sample, conv_p, ssm_p, cv_p, conv_s, ssm_s, cv_s))
```
